# Optimizing a Trainium2 kernel written in Bass

```python
import math
import jax, jax.numpy as jnp
from jax import lax
import numpy as np

D_MODEL = 2048
BATCH = 8
SEQ = 2048
DEPTH = 2

HEAD_DIM = 128
N_HEADS = D_MODEL // HEAD_DIM
N_HEADS_A = (3 * N_HEADS) // 8
N_HEADS_B = (N_HEADS - N_HEADS_A) // 2
N_HEADS_C = N_HEADS - N_HEADS_A - N_HEADS_B
DILATION_PATTERNS = ((128, 1), (512, 4), (2048, 16))
BLOCK = 128
D_FF = ((8 * D_MODEL // 3 + 255) // 256) * 256
CONV_WIDTH = 3
EPS = 1e-6
NEG = -1e30
IN_SIZES = ([N_HEADS_A * HEAD_DIM] * 3 + [N_HEADS_B * HEAD_DIM] * 3
            + [N_HEADS_C * HEAD_DIM] * 3 + [N_HEADS_C])
N_IN = sum(IN_SIZES)

kernel_name = "hybrid_dilated_stickbreak_forgetting_convffn"


def rmsnorm(x, g):
    xf = x.astype(jnp.float32)
    y = xf * lax.rsqrt(jnp.mean(xf * xf, axis=-1, keepdims=True) + EPS)
    return (y * g.astype(jnp.float32)).astype(x.dtype)


def split_heads(t, h):
    b, s, _ = t.shape
    return t.reshape(b, s, h, HEAD_DIM).transpose(0, 2, 1, 3).astype(jnp.float32)


def alibi_slopes(n):
    return 2.0 ** (-8.0 * (jnp.arange(n, dtype=jnp.float32) + 1.0) / n)


def dilated_pattern(q, k, v, slopes, window, dil):
    b, h, t, d = q.shape
    length = t // dil
    nb = -(-length // BLOCK)
    lp = nb * BLOCK
    wsub = window // dil

    def to_sub(a):
        a = a.reshape(b, h, length, dil, d).transpose(0, 1, 3, 2, 4)
        a = jnp.pad(a, ((0, 0), (0, 0), (0, 0), (0, lp - length), (0, 0)))
        return a.reshape(b, h, dil, nb, BLOCK, d)

    def band(a):
        prev = jnp.concatenate([jnp.zeros_like(a[:, :, :, :1]), a[:, :, :, :-1]], axis=3)
        return jnp.concatenate([prev, a], axis=4)

    qb = to_sub(q)
    kw = band(to_sub(k))
    vw = band(to_sub(v))
    s = jnp.einsum('bhrnqd,bhrnkd->bhrnqk', qb, kw) / math.sqrt(d)
    qi = jnp.arange(BLOCK)[:, None]
    kc = jnp.arange(2 * BLOCK)[None, :]
    delta = BLOCK + qi - kc
    keypos = (jnp.arange(nb)[:, None, None] - 1) * BLOCK + kc[None]
    valid = (delta >= 0) & (delta <= wsub) & (keypos >= 0)
    alibi = -slopes[:, None, None, None, None] * (dil * delta).astype(jnp.float32)
    s = jnp.where(valid, s + alibi, NEG)
    lse = jax.nn.logsumexp(s, axis=-1)
    p = jnp.exp(s - lse[..., None])
    o = jnp.einsum('bhrnqk,bhrnkd->bhrnqd', p, vw)
    o = o.reshape(b, h, dil, lp, d)[:, :, :, :length].transpose(0, 1, 3, 2, 4).reshape(b, h, t, d)
    lse = lse.reshape(b, h, dil, lp)[:, :, :, :length].transpose(0, 1, 3, 2).reshape(b, h, t)
    return o, lse


def dilated_mixture(q, k, v):
    slopes = alibi_slopes(q.shape[1])
    outs, lses = [], []
    for window, dil in DILATION_PATTERNS:
        o, lse = dilated_pattern(q, k, v, slopes, window, dil)
        outs.append(o)
        lses.append(lse)
    w = jax.nn.softmax(jnp.stack(lses, axis=0), axis=0)
    return jnp.sum(w[..., None] * jnp.stack(outs, axis=0), axis=0)


def stick_breaking(q, k, v):
    b, h, t, d = q.shape
    nb = t // BLOCK
    qb = q.reshape(b, h, nb, BLOCK, d).transpose(2, 0, 1, 3, 4)
    kpos = jnp.arange(t)
    scale = 1.0 / math.sqrt(d)

    def one(args):
        qblk, n = args
        z = jnp.einsum('bhqd,bhkd->bhqk', qblk, k) * scale
        qpos = n * BLOCK + jnp.arange(BLOCK)
        causal = kpos[None, :] < qpos[:, None]
        log_1m = jnp.where(causal, jax.nn.log_sigmoid(-z), 0.0)
        after = lax.cumsum(log_1m, axis=3, reverse=True) - log_1m
        a = jnp.where(causal, jnp.exp(jax.nn.log_sigmoid(z) + after), 0.0)
        return jnp.einsum('bhqk,bhkd->bhqd', a, v)

    o = lax.map(one, (qb, jnp.arange(nb)))
    return o.transpose(1, 2, 0, 3, 4).reshape(b, h, t, d)


def forgetting_attention(q, k, v, log_f):
    b, h, t, d = q.shape
    nb = t // BLOCK
    c = lax.cumsum(log_f, axis=2)
    qb = q.reshape(b, h, nb, BLOCK, d).transpose(2, 0, 1, 3, 4)
    cb = c.reshape(b, h, nb, BLOCK).transpose(2, 0, 1, 3)
    kpos = jnp.arange(t)
    scale = 1.0 / math.sqrt(d)

    def one(args):
        qblk, cq, n = args
        s = jnp.einsum('bhqd,bhkd->bhqk', qblk, k) * scale + cq[..., None] - c[:, :, None, :]
        qpos = n * BLOCK + jnp.arange(BLOCK)
        causal = kpos[None, :] <= qpos[:, None]
        p = jax.nn.softmax(jnp.where(causal, s, NEG), axis=-1)
        return jnp.einsum('bhqk,bhkd->bhqd', p, v)

    o = lax.map(one, (qb, cb, jnp.arange(nb)))
    return o.transpose(1, 2, 0, 3, 4).reshape(b, h, t, d)


def causal_dwconv(hdn, w, bias):
    t = hdn.shape[1]
    padded = jnp.pad(hdn, ((0, 0), (CONV_WIDTH - 1, 0), (0, 0)))
    out = bias
    for j in range(CONV_WIDTH):
        out = out + w[j] * padded[:, j:j + t]
    return out


def hybrid_layer(x, g_mix, w_in, b_f, g_head, w_o, g_ffn, w_gu, conv_w, conv_b, w_down):
    b, t, _ = x.shape
    hn = rmsnorm(x, g_mix)
    proj = hn @ w_in
    idx = [int(v) for v in np.cumsum(IN_SIZES)[:-1]]
    qa, ka, va, qb_, kb, vb, qc, kc, vc, fc = jnp.split(proj, idx, axis=-1)
    o_a = dilated_mixture(split_heads(qa, N_HEADS_A), split_heads(ka, N_HEADS_A),
                          split_heads(va, N_HEADS_A))
    o_b = stick_breaking(split_heads(qb_, N_HEADS_B), split_heads(kb, N_HEADS_B),
                         split_heads(vb, N_HEADS_B))
    log_f = jax.nn.log_sigmoid(fc.astype(jnp.float32) + b_f.astype(jnp.float32))
    o_c = forgetting_attention(split_heads(qc, N_HEADS_C), split_heads(kc, N_HEADS_C),
                               split_heads(vc, N_HEADS_C), log_f.transpose(0, 2, 1))
    o = jnp.concatenate([o_a, o_b, o_c], axis=1)
    o = o * lax.rsqrt(jnp.mean(o * o, axis=-1, keepdims=True) + EPS) \
        * g_head.astype(jnp.float32)[None, :, None, :]
    o = o.transpose(0, 2, 1, 3).reshape(b, t, D_MODEL).astype(x.dtype)
    x = x + o @ w_o
    hn = rmsnorm(x, g_ffn)
    gu = causal_dwconv(hn @ w_gu, conv_w, conv_b)
    gate, up = jnp.split(gu, 2, axis=-1)
    return x + (jax.nn.silu(gate) * up) @ w_down


def setup_inputs(seed: int = 0) -> dict:
    key = jax.random.key(seed)
    ks = jax.random.split(key, 12)
    f32 = jnp.float32
    nrm = jax.random.normal
    return {
        "x": nrm(ks[0], (BATCH, SEQ, D_MODEL), f32),
        "g_mix": 1.0 + 0.02 * nrm(ks[1], (DEPTH, D_MODEL), f32),
        "w_in": nrm(ks[2], (DEPTH, D_MODEL, N_IN), f32) * D_MODEL ** -0.5,
        "b_f": 1.0 + 3.0 * jax.random.uniform(ks[3], (DEPTH, N_HEADS_C), f32),
        "g_head": 1.0 + 0.02 * nrm(ks[4], (DEPTH, N_HEADS, HEAD_DIM), f32),
        "w_o": nrm(ks[5], (DEPTH, D_MODEL, D_MODEL), f32) * D_MODEL ** -0.5,
        "g_ffn": 1.0 + 0.02 * nrm(ks[6], (DEPTH, D_MODEL), f32),
        "w_gu": nrm(ks[7], (DEPTH, D_MODEL, 2 * D_FF), f32) * D_MODEL ** -0.5,
        "conv_w": nrm(ks[8], (DEPTH, CONV_WIDTH, 2 * D_FF), f32) * CONV_WIDTH ** -0.5,
        "conv_b": 0.02 * nrm(ks[9], (DEPTH, 2 * D_FF), f32),
        "w_down": nrm(ks[10], (DEPTH, D_FF, D_MODEL), f32) * D_FF ** -0.5,
        "g_final": 1.0 + 0.02 * nrm(ks[11], (D_MODEL,), f32),
    }


def reference(x, g_mix, w_in, b_f, g_head, w_o, g_ffn, w_gu, conv_w, conv_b, w_down, g_final):
    for layer in range(DEPTH):
        x = hybrid_layer(x, g_mix[layer], w_in[layer], b_f[layer], g_head[layer], w_o[layer],
                         g_ffn[layer], w_gu[layer], conv_w[layer], conv_b[layer], w_down[layer])
    return rmsnorm(x, g_final)
```

```python
import math
import contextlib
import numpy as np
import concourse.bass as bass
import concourse.mybir as mybir
from concourse.bass_utils import run_bass_kernel_spmd

F32 = mybir.dt.float32
BF16 = mybir.dt.bfloat16
I32 = mybir.dt.int32
AF = mybir.ActivationFunctionType
ALU = mybir.AluOpType

ENGS = ("pe", "act", "dve", "pool", "sp")

T = 2048
D = 2048
NT = 16
HD = 128
NH = 16
HA, HB, HC = 6, 5, 5
DFF = 5632
NF = 44
N_IN = 6149
EPS = 1e-6
SCALE = 1.0 / math.sqrt(HD)
DEPTH = 2


class Res:
    __slots__ = ("name", "w", "readers")

    def __init__(self, name=""):
        self.name = name
        self.w = None
        self.readers = []


class DSem:
    __slots__ = ("idx", "val", "handle", "res")

    def __init__(self, idx):
        self.idx = idx
        self.val = 0
        self.handle = None
        self.res = Res("dsem%d" % idx)


class Op:
    __slots__ = ("eng", "call", "deps", "need_inc", "seq", "is_dma", "dsem", "dval")


class _Rec:
    def __init__(self):
        self.call = None

    def __getattr__(self, name):
        def f(*a, **k):
            assert self.call is None
            self.call = (name, a, k)
            return None
        return f


class Sched:
    def __init__(self):
        self.ops = {e: [] for e in ENGS}
        self.dsems = []
        self.bar_res = {e: Res("bar_" + e) for e in ENGS}

    def dsem(self):
        d = DSem(len(self.dsems))
        self.dsems.append(d)
        return d

    def _add(self, eng, fn, reads, writes, is_dma, dsem):
        op = Op()
        op.eng = eng
        rec = _Rec()
        fn(rec)
        op.call = rec.call
        op.is_dma = is_dma
        op.need_inc = False
        op.seq = None
        op.dsem = dsem
        deps = {}
        for t in reads:
            w = t.w
            if w is not None:
                deps[id(w)] = w
        for t in writes:
            w = t.w
            if w is not None and (w.is_dma or w.eng != eng):
                deps[id(w)] = w
            for r in t.readers:
                if r.is_dma or r.eng != eng:
                    deps[id(r)] = r
        op.deps = list(deps.values())
        if is_dma:
            dsem.val += 16
            op.dval = dsem.val
        else:
            op.dval = None
        for t in writes:
            t.w = op
            t.readers = []
        for t in reads:
            rs = [r for r in t.readers if r.is_dma or r.eng != eng]
            rs.append(op)
            t.readers = rs
        self.ops[eng].append(op)
        return op

    def op(self, eng, fn, reads=(), writes=()):
        return self._add(eng, fn, reads, writes, False, None)

    def dma(self, eng, fn, dsem, reads=(), writes=()):
        return self._add(eng, fn, reads, list(writes) + [dsem.res], True, dsem)

    def barrier(self):
        lasts = []
        for e in ENGS:
            for op in reversed(self.ops[e]):
                if not op.is_dma:
                    lasts.append(op)
                    break
        for d in self.dsems:
            if d.res.w is not None:
                lasts.append(d.res.w)
        for e in ENGS:
            op = self.op(e, lambda g: g.nop())
            op.deps = [o for o in lasts if o.is_dma or o.eng != e]

    def emit(self, nc):
        for e in ENGS:
            for op in self.ops[e]:
                for d in op.deps:
                    if not d.is_dma:
                        d.need_inc = True
        for e in ENGS:
            c = 0
            for op in self.ops[e]:
                if (not op.is_dma) and op.need_inc:
                    c += 1
                    op.seq = c
        with contextlib.ExitStack() as st:
            esem = {e: st.enter_context(nc.semaphore("s_" + e)) for e in ENGS}
            for d in self.dsems:
                d.handle = st.enter_context(nc.semaphore("d%d" % d.idx))
            block = st.enter_context(nc.Block())
            hmap = {"pe": "tensor", "act": "scalar", "dve": "vector", "pool": "gpsimd", "sp": "sync"}

            def make(e):
                ops = self.ops[e]

                def body(eng):
                    waited = {}
                    for op in ops:
                        need = {}
                        for d in op.deps:
                            if d.is_dma:
                                k = ("d", d.dsem.idx)
                                h = d.dsem.handle
                                v = d.dval
                            else:
                                k = ("e", d.eng)
                                h = esem[d.eng]
                                v = d.seq
                            if waited.get(k, 0) >= v:
                                continue
                            if k not in need or need[k][1] < v:
                                need[k] = (h, v)
                        for k, (h, v) in need.items():
                            eng.wait_ge(h, v)
                            waited[k] = v
                        name, a, k = op.call
                        inst = getattr(eng, name)(*a, **k)
                        if op.is_dma:
                            inst.then_inc(op.dsem.handle, 16)
                        elif op.need_inc:
                            inst.then_inc(esem[e], 1)
                    if e == "sp":
                        for d in self.dsems:
                            if d.val > 0:
                                eng.wait_ge(d.handle, d.val)
                return body

            for e in ENGS:
                getattr(block, hmap[e])(make(e))


class Ring:
    def __init__(self, S, aps, with_dsem=False):
        self.slots = [(ap, Res(), S.dsem() if with_dsem else None) for ap in aps]
        self.i = 0

    def next(self):
        s = self.slots[self.i % len(self.slots)]
        self.i += 1
        return s


def head_info(h):
    if h < HA:
        base, n, i, typ = 0, HA, h, "A"
    elif h < HA + HB:
        base, n, i, typ = 3 * HA * HD, HB, h - HA, "B"
    else:
        base, n, i, typ = 3 * HA * HD + 3 * HB * HD, HC, h - HA - HB, "C"
    return typ, i, base + i * HD, base + n * HD + i * HD, base + 2 * n * HD + i * HD


def build_nc(dbg=None, n_layers=DEPTH):
    dbg = dbg or {}
    nc = bass.Bass("TRN2", target_bir_lowering=False)

    def din(name, shape, dt=F32):
        return nc.dram_tensor(name, list(shape), dt, kind="ExternalInput").ap()

    x_d = din("x", [T, D])
    w_in_d = din("w_in", [DEPTH, D, N_IN])
    w_o_d = din("w_o", [DEPTH, D, D])
    w_gu_d = din("w_gu", [DEPTH, D, 2 * DFF])
    w_down_d = din("w_down", [DEPTH, DFF, D])
    gmix_d = din("gmixT", [DEPTH, 128, 16])
    gffn_d = din("gffnT", [DEPTH, 128, 16])
    ghead_d = din("gheadT", [DEPTH, 128, 16])
    bf_d = din("bfb", [DEPTH, 128, HC])
    cw_d = din("cwT", [DEPTH, 128, 88 * 3])
    cb_d = din("cbT", [DEPTH, 128, 88])
    gfin_d = din("gfinb", [128, D])
    y_d = nc.dram_tensor("y", [T, D], F32, kind="ExternalOutput").ap()
    xres_d = nc.dram_tensor("xres", [T, D], F32).ap()
    oT_d = nc.dram_tensor("oTd", [NH, 128, T], BF16).ap()
    dbg_out = {}
    for k, (shape, dt) in dbg.items():
        dbg_out[k] = nc.dram_tensor("dbg_" + k, list(shape), dt, kind="ExternalOutput").ap()

    S = Sched()
    st = contextlib.ExitStack()
    with st:
        def sb(name, free, dt=F32):
            return st.enter_context(nc.sbuf_tensor(name, [128] + list(free), dt))

        BIGA = sb("BIGA", [16384], F32)
        BIGB = sb("BIGB", [22528], F32)
        XT = sb("XT", [4096], F32)
        WGU = sb("WGU", [6144], F32)
        identb = sb("identb", [128], BF16)
        identf = sb("identf", [128], F32)
        maskLEb = sb("maskLEb", [128], BF16)
        triLEf = sb("triLEf", [128], F32)
        maskLTf = sb("maskLTf", [128], F32)
        onesf = sb("onesf", [128], F32)
        negLEf = sb("negLEf", [128], F32)
        gmix = sb("gmix", [DEPTH, 16], F32)
        gffn = sb("gffn", [DEPTH, 16], F32)
        ghead = sb("ghead", [DEPTH, 16], F32)
        bfb = sb("bfb_s", [DEPTH, HC], F32)
        cw = sb("cw", [DEPTH, 88 * 3], F32)
        cb = sb("cb", [DEPTH, 88], F32)
        carry = sb("carry", [88 * 2], F32)
        stat = sb("stat", [64], F32)
        cneg = sb("cneg", [16 * HC], F32)
        ccar = sb("ccar", [17 * HC], F32)
        biasC = sb("biasC", [4 * 16 * HC], F32)
        lf = sb("lf", [16 * HC], F32)
        epsb = sb("epsb", [1], F32)
        oneb = sb("oneb", [1], F32)

        pS = st.enter_context(nc.psum_tensor("pS", [128, 2, 512], F32))
        pO = st.enter_context(nc.psum_tensor("pO", [128, 4, 512], F32))
        pX = st.enter_context(nc.psum_tensor("pX", [128, 512], F32))
        pT = st.enter_context(nc.psum_tensor("pT", [128, 512], F32))
        r_pS = [Res("pS0"), Res("pS1")]
        r_pO = [Res("pO%d" % i) for i in range(4)]
        r_pX = Res("pX")
        r_pT = Res("pT")

        def vw(region, off_b, nbytes, dt, pat=None, **kw):
            a = region[:, off_b // 4:(off_b + nbytes) // 4]
            if dt != F32:
                a = a.bitcast(dt)
            if pat:
                a = a.rearrange(pat, **kw)
            return a

        dcnt = [0]

        def dbg_dump(key, ap_sb, res, dram_slice=None):
            if key not in dbg_out:
                return
            d = S.dsem()
            tgt = dbg_out[key] if dram_slice is None else dram_slice(dbg_out[key])
            S.dma("sp", lambda g: g.dma_start(out=tgt, in_=ap_sb), d, reads=res)

        r_c = Res("consts")
        r_par = Res("params")
        dpar = S.dsem()
        for (dst, src) in ((gmix, gmix_d), (gffn, gffn_d), (ghead, ghead_d), (bfb, bf_d), (cw, cw_d), (cb, cb_d)):
            for l in range(DEPTH):
                S.dma("sp", lambda g, dst=dst, src=src, l=l: g.dma_start(out=dst[:, l, :], in_=src[l]), dpar,
                      writes=[r_par])
        tmpi = vw(XT, 0, 512, I32)
        tmpf = vw(XT, 512, 512, F32)
        S.op("pool", lambda g: g.iota(tmpi, pattern=[[1, 128]], base=0, channel_multiplier=-1), writes=[r_c])
        S.op("dve", lambda g: g.tensor_copy(out=tmpf, in_=tmpi), reads=[r_c], writes=[r_c])
        S.op("dve", lambda g: g.tensor_single_scalar(out=identf[:], in_=tmpf, scalar=0.0, op=ALU.is_equal), reads=[r_c], writes=[r_c])
        S.op("dve", lambda g: g.tensor_copy(out=identb[:], in_=identf[:]), reads=[r_c], writes=[r_c])
        S.op("dve", lambda g: g.tensor_single_scalar(out=triLEf[:], in_=tmpf, scalar=0.0, op=ALU.is_ge), reads=[r_c], writes=[r_c])
        S.op("dve", lambda g: g.tensor_copy(out=maskLEb[:], in_=triLEf[:]), reads=[r_c], writes=[r_c])
        S.op("dve", lambda g: g.tensor_single_scalar(out=maskLTf[:], in_=tmpf, scalar=0.0, op=ALU.is_lt), reads=[r_c], writes=[r_c])
        S.op("dve", lambda g: g.memset(onesf[:], 1.0), writes=[r_c])
        S.op("dve", lambda g: g.tensor_scalar(out=negLEf[:], in0=triLEf[:], scalar1=-1.0, scalar2=30000.0, op0=ALU.add, op1=ALU.mult),
             reads=[r_c], writes=[r_c])
        S.op("dve", lambda g: g.memset(epsb[:], EPS), writes=[r_c])
        S.op("dve", lambda g: g.memset(oneb[:], 1.0), writes=[r_c])
        S.op("dve", lambda g: g.memset(carry[:], 0.0), writes=[r_c])

        o = 0
        QTs = []
        KTs = []
        for i in range(2):
            QTs.append(vw(BIGB, o, 4096, BF16)); o += 4096
            KTs.append(vw(BIGB, o, 4096, BF16)); o += 4096
        Vts = []
        for i in range(2):
            Vts.append(vw(BIGB, o, 16 * 132 * 2, BF16, "p (c n) -> p c n", n=132)); o += 16 * 132 * 2
        LM0 = vw(BIGB, o, 8192, F32); o += 8192
        LH = vw(BIGB, o, 8192, F32); o += 8192
        LHi = LH.bitcast(I32)
        sfs = [vw(BIGB, o + 2048 * i, 2048, F32) for i in range(2)]; o += 4096
        pts = [vw(BIGB, o + 1024 * i, 1024, BF16) for i in range(3)]; o += 3072
        ebuf = vw(BIGB, o, 8256, F32); o += 8256
        cumbuf = vw(BIGB, o, 8256, F32); o += 8256
        abuf = vw(BIGB, o, 4096, BF16); o += 4096
        aTb = vw(BIGB, o, 4096, BF16, "p (c n) -> p c n", n=128); o += 4096
        oThs = [vw(BIGB, o + 4096 * i, 4096, BF16) for i in range(2)]; o += 8192
        obuf = vw(BIGB, o, 1024, BF16, "p (c n) -> p c n", n=128); o += 1024
        junk = vw(BIGB, o, 4096, BF16); o += 4096
        assert o <= 22528 * 4, o
        hnT = vw(BIGA, 0, 65536, BF16, "p (c n) -> p c n", n=T)
        r_hnT = Res("hnT")
        wslots = [vw(WGU, 4096 * i, 4096, BF16, "p (c n) -> p c n", n=128) for i in range(6)]
        wring = Ring(S, wslots, with_dsem=True)

        r_lm = Res("LM0")

        def build_masks():
            t_i = LHi
            ti2 = ebuf[:, 0:2048].bitcast(I32)
            tf_d = cumbuf[:, 1:2049]
            tf_b = vw(BIGB, 64896, 8192, F32)
            tf_c = vw(BIGB, 73088, 8192, F32)
            R = [r_lm]

            def dv(fn):
                S.op("dve", fn, reads=R, writes=R)
            S.op("pool", lambda g: g.iota(t_i, pattern=[[1, 2048]], base=0, channel_multiplier=-1), reads=R, writes=R)
            dv(lambda g: g.tensor_copy(out=tf_d, in_=t_i))
            dv(lambda g: g.tensor_scalar(out=tf_b, in0=tf_d, scalar1=0.0, scalar2=None, op0=ALU.is_ge))
            dv(lambda g: g.tensor_scalar(out=tf_c, in0=tf_d, scalar1=128.0, scalar2=None, op0=ALU.is_le))
            dv(lambda g: g.tensor_tensor(out=LM0, in0=tf_b, in1=tf_c, op=ALU.mult))
            for (div, lim) in ((4.0, 512.0), (16.0, None)):
                dv(lambda g, div=div: g.tensor_scalar(out=tf_c, in0=tf_d, scalar1=1.0 / div, scalar2=None, op0=ALU.mult))
                dv(lambda g: g.tensor_copy(out=ti2, in_=tf_c))
                dv(lambda g: g.tensor_copy(out=ebuf[:, 0:2048], in_=ti2))
                dv(lambda g: g.tensor_tensor(out=tf_c, in0=tf_c, in1=ebuf[:, 0:2048], op=ALU.is_equal))
                dv(lambda g: g.tensor_tensor(out=tf_c, in0=tf_c, in1=tf_b, op=ALU.mult))
                if lim is not None:
                    dv(lambda g, lim=lim: g.scalar_tensor_tensor(out=tf_c, in0=tf_d, scalar=lim, in1=tf_c, op0=ALU.is_le, op1=ALU.mult))
                dv(lambda g: g.tensor_tensor(out=LM0, in0=LM0, in1=tf_c, op=ALU.add))
            dv(lambda g: g.tensor_scalar(out=tf_b, in0=LM0, scalar1=1.0, scalar2=-1.0, op0=ALU.min, op1=ALU.add))
            dv(lambda g: g.tensor_scalar(out=LM0, in0=LM0, scalar1=1e-18, scalar2=None, op0=ALU.max))
            S.op("act", lambda g: g.activation(out=LM0, in_=LM0, func=AF.Ln), reads=R, writes=R)
            dv(lambda g: g.scalar_tensor_tensor(out=LM0, in0=tf_b, scalar=1000.0, in1=LM0, op0=ALU.mult, op1=ALU.add))
            dv(lambda g: g.tensor_scalar(out=LM0, in0=LM0, scalar1=1.0 / SCALE, scalar2=None, op0=ALU.mult))
            for i in range(2):
                dv(lambda g, i=i: g.memset(Vts[i][:, :, 128:129], 1.0))
            dv(lambda g: g.memset(cumbuf[:, 0:1], 0.0))
            S.barrier()

        S.barrier()

        xt_slots = [(vw(XT, 8192 * i, 8192, F32), Res("xt%d" % i), S.dsem()) for i in range(2)]
        xs_b = junk
        r_xs = Res("xs")
        r_stat = Res("stat")
        ptr_views = [pO[:, 0:2, :].rearrange("p a b -> p (a b)").bitcast(BF16).rearrange("p (c n) -> p c n", n=128),
                     pO[:, 2:4, :].rearrange("p a b -> p (a b)").bitcast(BF16).rearrange("p (c n) -> p c n", n=128)]

        xs_bufs = [junk, vw(BIGB, 73088, 4096, BF16)]
        r_xss = [r_xs, Res("xs2")]
        r_stats = [r_stat, Res("stat2")]

        def norm_transpose(src_ap, r_src, i, gvec, dstT, r_dst, col0):
            k = i % 2
            xt, r_xt, d_xt = xt_slots[k]
            xsb, r_x_s, r_st, sc = xs_bufs[k], r_xss[k], r_stats[k], 40 * k
            S.dma("sp", lambda g: g.dma_start(out=xt, in_=src_ap), d_xt, reads=[r_src], writes=[r_xt])
            S.op("act", lambda g: g.activation(out=xsb, in_=xt, func=AF.Square, accum_out=stat[:, sc:sc + 1]),
                 reads=[r_xt], writes=[r_x_s, r_st])
            S.op("act", lambda g: g.activation(out=stat[:, sc + 1:sc + 2], in_=stat[:, sc:sc + 1], func=AF.Ln, scale=1.0 / D, bias=epsb[:]),
                 reads=[r_st], writes=[r_st])
            S.op("act", lambda g: g.activation(out=stat[:, sc + 2:sc + 3], in_=stat[:, sc + 1:sc + 2], func=AF.Exp, scale=-0.5),
                 reads=[r_st], writes=[r_st])
            S.op("dve", lambda g: g.tensor_scalar(out=xsb, in0=xt, scalar1=stat[:, sc + 2:sc + 3], scalar2=None, op0=ALU.mult),
                 reads=[r_xt, r_st], writes=[r_x_s])
            pv = ptr_views[k]
            rp = [r_pO[2 * k], r_pO[2 * k + 1]]
            for c in range(16):
                S.op("pe", lambda g, c=c: g.transpose(out=pv[:, c, :], in_=xsb[:, c * 128:(c + 1) * 128], identity=identb[:]),
                     reads=[r_x_s, r_c], writes=rp)
            S.op("dve", lambda g: g.tensor_tensor(out=dstT[:, :, col0:col0 + 128], in0=pv,
                                                  in1=gvec.unsqueeze(2).broadcast_to([128, 16, 128]), op=ALU.mult),
                 reads=rp + [r_par], writes=[r_dst])

        r_x = [Res("xrow%d" % i) for i in range(NT)]
        r_oTd = [Res("oTd%d" % h) for h in range(NH)]
        slopes = [2.0 ** (-8.0 * (i + 1) / HA) for i in range(HA)]

        for l in range(n_layers):
            src_d = x_d if l == 0 else xres_d
            for i in range(NT):
                norm_transpose(src_d[i * 128:(i + 1) * 128, :], r_x[i], i, gmix[:, l, :], hnT, r_hnT, i * 128)
            if l == 0:
                dbg_dump("hnT", hnT, [r_hnT])
            wf, r_wf, d_wf = wring.next()
            wfv = wf[:, :, 0:HC]
            S.dma("pool", lambda g: g.dma_start(out=wfv, in_=w_in_d[l].rearrange("(c p) n -> p c n", p=128)[:, :, 6144:6144 + HC]),
                  d_wf, writes=[r_wf])
            pfc = pX[:, 0:16 * HC].rearrange("p (b h) -> p b h", h=HC)
            for tb in range(NT):
                for c in range(16):
                    S.op("pe", lambda g, tb=tb, c=c: g.matmul(pfc[:, tb, :], lhsT=hnT[:, c, tb * 128:(tb + 1) * 128],
                                                               rhs=wfv[:, c, :], start=(c == 0), stop=(c == 15)),
                         reads=[r_hnT, r_wf], writes=[r_pX])
            r_lf = Res("lf")
            lf3 = lf[:].rearrange("p (b h) -> p b h", h=HC)
            S.op("dve", lambda g: g.tensor_tensor(out=lf3, in0=pfc, in1=bfb[:, l, :].unsqueeze(1).broadcast_to([128, 16, HC]), op=ALU.add),
                 reads=[r_pX, r_par], writes=[r_lf])
            S.op("act", lambda g: g.activation(out=lf[:], in_=lf[:], func=AF.Exp, scale=-1.0), reads=[r_lf], writes=[r_lf])
            S.op("act", lambda g: g.activation(out=lf[:], in_=lf[:], func=AF.Ln, bias=oneb[:]), reads=[r_lf], writes=[r_lf])
            pc1 = pT[:, 0:16 * HC]
            pc2 = pT[:, 128:128 + 16 * HC]
            S.op("pe", lambda g: g.matmul(pc1, lhsT=triLEf[:], rhs=lf[:], start=True, stop=True), reads=[r_lf, r_c], writes=[r_pT])
            S.op("pe", lambda g: g.matmul(pc2, lhsT=onesf[:], rhs=lf[:], start=True, stop=True), reads=[r_lf, r_c], writes=[r_pT])
            r_cc = Res("ccar")
            cc3 = ccar[:].rearrange("p (b h) -> p b h", h=HC)
            cn3 = cneg[:].rearrange("p (b h) -> p b h", h=HC)
            pc2v = pc2.rearrange("p (b h) -> p b h", h=HC)
            pc1v = pc1.rearrange("p (b h) -> p b h", h=HC)
            S.op("dve", lambda g: g.memset(ccar[:, 0:HC], 0.0), writes=[r_cc])
            for b in range(16):
                S.op("dve", lambda g, b=b: g.tensor_tensor(out=cc3[:, b + 1, :], in0=pc2v[:, b, :], in1=cc3[:, b, :], op=ALU.add),
                     reads=[r_pT, r_cc], writes=[r_cc])
            S.op("dve", lambda g: g.tensor_tensor(out=cn3, in0=pc1v, in1=cc3[:, 0:16, :], op=ALU.add), reads=[r_pT, r_cc], writes=[r_cc])
            if l == 0:
                dbg_dump("cneg", cneg[:], [r_cc])

            S.barrier()
            build_masks()
            if l == 0:
                dbg_dump("LM0", LM0, [r_lm])
            r_LH = Res("LH")
            sf_ring = Ring(S, sfs)
            pt_ring = Ring(S, pts)
            r_ebuf = Res("ebuf")
            r_cum = Res("cum")
            r_abuf = Res("abuf")
            r_aT = Res("aT")
            r_ob = Res("obuf")
            r_junk = r_xs
            d_oT = [S.dsem(), S.dsem()]
            r_oTh = [Res("oTh0"), Res("oTh1")]
            ps_i = [0]

            def ps_next():
                k = ps_i[0] % 2
                ps_i[0] += 1
                return pS[:, k, :], r_pS[k]

            ev_i = [0]

            def evac(out, in_, reads, writes):
                ev_i[0] += 1
                if ev_i[0] % 2:
                    S.op("act", lambda g: g.activation(out=out, in_=in_, func=AF.Copy), reads=reads, writes=writes)
                else:
                    S.op("dve", lambda g: g.tensor_copy(out=out, in_=in_), reads=reads, writes=writes)

            pTb = pT[:].bitcast(BF16)[:, 0:512].rearrange("p (c n) -> p c n", n=128)

            def epilogue(l, h, nblk, pviews, rviews, qcol0, with_den, oTh, r_oT):
                nb = nblk
                if with_den:
                    for j in range(nb):
                        S.op("dve", lambda g, j=j: g.reciprocal(out=stat[:, 8 + j:9 + j], in_=pviews[j][:, 128:129]),
                             reads=[rviews[j]], writes=[r_stat])
                for j in range(nb):
                    if with_den:
                        S.op("act", lambda g, j=j: g.activation(out=junk[:, 0:128], in_=pviews[j][:, 0:128], func=AF.Square,
                                                               scale=stat[:, 8 + j:9 + j], accum_out=stat[:, 12 + j:13 + j]),
                             reads=[rviews[j], r_stat], writes=[r_junk, r_stat])
                    else:
                        S.op("act", lambda g, j=j: g.activation(out=junk[:, 0:128], in_=pviews[j][:, 0:128], func=AF.Square,
                                                               accum_out=stat[:, 12 + j:13 + j]),
                             reads=[rviews[j]], writes=[r_junk, r_stat])
                S.op("act", lambda g: g.activation(out=stat[:, 16:16 + nb], in_=stat[:, 12:12 + nb], func=AF.Ln, scale=1.0 / HD, bias=epsb[:]),
                     reads=[r_stat], writes=[r_stat])
                S.op("act", lambda g: g.activation(out=stat[:, 20:20 + nb], in_=stat[:, 16:16 + nb], func=AF.Exp, scale=-0.5),
                     reads=[r_stat], writes=[r_stat])
                if with_den:
                    S.op("dve", lambda g: g.tensor_tensor(out=stat[:, 20:20 + nb], in0=stat[:, 20:20 + nb], in1=stat[:, 8:8 + nb], op=ALU.mult),
                         reads=[r_stat], writes=[r_stat])
                for j in range(nb):
                    S.op("dve", lambda g, j=j: g.tensor_scalar(out=obuf[:, j, :], in0=pviews[j][:, 0:128], scalar1=stat[:, 20 + j:21 + j],
                                                               scalar2=None, op0=ALU.mult),
                         reads=[rviews[j], r_stat], writes=[r_ob])
                for j in range(nb):
                    S.op("pe", lambda g, j=j: g.transpose(out=pTb[:, j, :], in_=obuf[:, j, :], identity=identb[:]),
                         reads=[r_ob, r_c], writes=[r_pT])
                S.op("act", lambda g: g.activation(out=oTh[:, qcol0:qcol0 + 128 * nb],
                                                   in_=pTb[:, 0:nb, :].rearrange("p c n -> p (c n)"), func=AF.Copy,
                                                   scale=ghead[:, l, h:h + 1]),
                     reads=[r_pT, r_par], writes=[r_oT])

            r_QTs = [Res("QT0"), Res("QT1")]
            r_KTs = [Res("KT0"), Res("KT1")]
            r_Vts = [Res("Vt0"), Res("Vt1")]
            ebufs = [ebuf, vw(BIGB, 24832, 8256, F32)]
            cumbufs = [cumbuf, vw(BIGB, 24832 + 8256, 8256, F32)]
            abufs = [abuf, vw(XT, 0, 4096, BF16)]
            r_ebufs = [r_ebuf, Res("ebuf2")]
            r_cums = [r_cum, Res("cum2")]
            r_abufs = [r_abuf, Res("abuf2")]
            r_statB = [Res("negtot0"), Res("negtot1")]
            r_statE = Res("statE")
            r_ab3 = Res("abuf3")
            r_aT2 = Res("aT2")

            def proj_gen(h):
                typ, hi, cq, ck, cv = head_info(h)
                QT, KT, Vt = QTs[h % 2], KTs[h % 2], Vts[h % 2]
                r_QT, r_KT, r_Vt = r_QTs[h % 2], r_KTs[h % 2], r_Vts[h % 2]
                wv_in = w_in_d[l].rearrange("(c p) n -> p c n", p=128)
                ws = []
                for col in (cq, ck, cv):
                    w_ap, r_w, d_w = wring.next()
                    S.dma("pool", lambda g, w_ap=w_ap, col=col: g.dma_start(out=w_ap, in_=wv_in[:, :, col:col + 128]), d_w, writes=[r_w])
                    ws.append((w_ap, r_w))
                yield
                for (w_ap, r_w), dst, r_dst in ((ws[0], QT, r_QT), (ws[1], KT, r_KT)):
                    for tg in range(4):
                        ps, r_ps = pX[:, :], r_pX
                        for c in range(16):
                            S.op("pe", lambda g, ps=ps, w_ap=w_ap, c=c, tg=tg: g.matmul(ps, lhsT=w_ap[:, c, :], rhs=hnT[:, c, tg * 512:(tg + 1) * 512],
                                                                                        start=(c == 0), stop=(c == 15)),
                                 reads=[r_w, r_hnT], writes=[r_ps])
                            if c % 4 == 3 and c != 15:
                                yield
                        evac(dst[:, tg * 512:(tg + 1) * 512], ps, [r_ps], [r_dst])
                        yield
                w_ap, r_w = ws[2]
                for tg in range(4):
                    ps, r_ps = pX[:, :], r_pX
                    for tb in range(4):
                        t0 = (tg * 4 + tb) * 128
                        for c in range(16):
                            S.op("pe", lambda g, ps=ps, w_ap=w_ap, c=c, tb=tb, t0=t0: g.matmul(ps[:, tb * 128:(tb + 1) * 128], lhsT=hnT[:, c, t0:t0 + 128],
                                                                                              rhs=w_ap[:, c, :], start=(c == 0), stop=(c == 15)),
                                 reads=[r_w, r_hnT], writes=[r_ps])
                        if tb != 3:
                            yield
                    evac(Vt[:, tg * 4:(tg + 1) * 4, 0:128], ps.rearrange("p (c n) -> p c n", n=128), [r_ps], [r_Vt])
                    yield
                if l == 0 and h in (0, 6, 11):
                    dbg_dump("QT%d" % h, QT, [r_QT])
                    dbg_dump("KT%d" % h, KT, [r_KT])
                    dbg_dump("Vt%d" % h, Vt, [r_Vt])

            LHs = [LH, ebuf[:, 0:2048]]
            r_LHs = [r_LH, Res("LHb")]
            lh_built = set()

            def lh_build_gen(hh):
                lh_built.add(hh)
                typ_, hi_, _, _, _ = head_info(hh)
                L_, r_L = LHs[hh % 2], r_LHs[hh % 2]
                extra = [r_ebufs[1], r_cums[1]] if hh % 2 == 0 else [r_ebufs[0]]
                if typ_ == "A":
                    Li = L_.bitcast(I32)
                    S.op("pool", lambda g: g.iota(Li, pattern=[[1, 2048]], base=0, channel_multiplier=-1), writes=[r_L] + extra)
                    yield
                    S.op("dve", lambda g: g.tensor_copy(out=cumbuf[:, 1:2049], in_=Li), reads=[r_L], writes=[r_cum])
                    yield
                    S.op("dve", lambda g, sl=slopes[hi_]: g.scalar_tensor_tensor(out=L_, in0=cumbuf[:, 1:2049], scalar=-sl / SCALE, in1=LM0,
                                                                                  op0=ALU.mult, op1=ALU.add),
                         reads=[r_cum, r_lm], writes=[r_L])
                    yield
                else:
                    dgs = [abuf[:, 256 * i:256 * (i + 1)].bitcast(F32) for i in range(8)]
                    for bg in range(4):
                        ps, r_ps = ps_next()
                        for b4 in range(4):
                            b = bg * 4 + b4
                            dg, rdg = dgs[b % 8], r_dgs[b % 8]
                            S.op("dve", lambda g, dg=dg, b=b: g.tensor_scalar(out=dg, in0=identf[:], scalar1=cn3[:, b, hi_:hi_ + 1], scalar2=None,
                                                                             op0=ALU.mult),
                                 reads=[r_c, r_cc], writes=[rdg, r_abuf])
                            S.op("pe", lambda g, dg=dg, ps=ps, b4=b4: g.matmul(ps[:, b4 * 128:(b4 + 1) * 128], lhsT=onesf[:], rhs=dg, start=True, stop=True),
                                 reads=[rdg, r_c], writes=[r_ps])
                        S.op("act", lambda g, ps=ps, bg=bg: g.activation(out=L_[:, bg * 512:(bg + 1) * 512], in_=ps, func=AF.Copy, scale=-1.0 / SCALE),
                             reads=[r_ps], writes=[r_L] + extra)
                        yield

            r_dgs = [Res("dg%d" % i) for i in range(8)]
            stage = vw(XT, 12288, 4 * 132 * 4, F32, "p (c n) -> p c n", n=132)
            r_stage = Res("stage")

            def epilogue_ac_1(l, h, qg):
                S.op("act", lambda g: g.activation(out=stage[:, :, 0:129], in_=pO[:, :, 0:129], func=AF.Copy), reads=r_pO, writes=[r_stage])
                S.op("dve", lambda g: g.reciprocal(out=stat[:, 8:12].unsqueeze(2), in_=stage[:, :, 128:129]), reads=[r_stage], writes=[r_stat])
                for j in range(4):
                    S.op("act", lambda g, j=j: g.activation(out=junk[:, 0:128], in_=stage[:, j, 0:128], func=AF.Square,
                                                           scale=stat[:, 8 + j:9 + j], accum_out=stat[:, 12 + j:13 + j]),
                         reads=[r_stage, r_stat], writes=[r_junk, r_stat])
                S.op("act", lambda g: g.activation(out=stat[:, 16:20], in_=stat[:, 12:16], func=AF.Ln, scale=1.0 / HD, bias=epsb[:]),
                     reads=[r_stat], writes=[r_stat])
                S.op("act", lambda g: g.activation(out=stat[:, 20:24], in_=stat[:, 16:20], func=AF.Exp, scale=-0.5), reads=[r_stat], writes=[r_stat])
                S.op("dve", lambda g: g.tensor_tensor(out=stat[:, 20:24], in0=stat[:, 20:24], in1=stat[:, 8:12], op=ALU.mult), reads=[r_stat], writes=[r_stat])
                S.op("dve", lambda g: g.tensor_tensor(out=obuf, in0=stage[:, :, 0:128], in1=stat[:, 20:24].unsqueeze(2).broadcast_to([128, 4, 128]), op=ALU.mult),
                     reads=[r_stage, r_stat], writes=[r_ob])

            def epilogue_ac_2(l, h, qg, oTh, r_oT):
                for j in range(4):
                    S.op("pe", lambda g, j=j: g.transpose(out=pTb[:, j, :], in_=obuf[:, j, :], identity=identb[:]), reads=[r_ob, r_c], writes=[r_pT])
                S.op("act", lambda g: g.activation(out=oTh[:, qg * 512:(qg + 1) * 512], in_=pTb.rearrange("p c n -> p (c n)"), func=AF.Copy,
                                                   scale=ghead[:, l, h:h + 1]),
                     reads=[r_pT, r_par], writes=[r_oT])

            cur_gen = [None]

            def pump():
                gnr = cur_gen[0]
                if gnr is None:
                    return
                try:
                    next(gnr)
                except StopIteration:
                    cur_gen[0] = None

            def drain_gen():
                while cur_gen[0] is not None:
                    pump()

            cur_gen[0] = proj_gen(0)
            drain_gen()
            for h in range(NH):
                typ, hi, cq, ck, cv = head_info(h)
                QT, KT, Vt = QTs[h % 2], KTs[h % 2], Vts[h % 2]
                r_QT, r_KT, r_Vt = r_QTs[h % 2], r_KTs[h % 2], r_Vts[h % 2]
                oTh = oThs[h % 2]
                r_oT = r_oTh[h % 2]
                if h + 1 < NH:
                    cur_gen[0] = proj_gen(h + 1)
                    pump()

                if typ in ("A", "C"):
                    if h not in lh_built:
                        for _ in lh_build_gen(h):
                            pass
                    LHc, r_LHc = LHs[h % 2], r_LHs[h % 2]
                    nxt_gen = None
                    if h + 1 < NH and head_info(h + 1)[0] == typ:
                        nxt_gen = lh_build_gen(h + 1)
                    steps = [(qg, m) for qg in range(4) for m in range(4 * qg + 4)]
                    state = {}

                    def do_s(i):
                        qg, m = steps[i]
                        c0 = max(0, m - 4 * qg) * 128
                        ps, r_ps = ps_next()
                        S.op("pe", lambda g: g.matmul(ps[:, c0:512], lhsT=KT[:, m * 128:(m + 1) * 128], rhs=QT[:, qg * 512 + c0:(qg + 1) * 512],
                                                      start=True, stop=True),
                             reads=[r_KT, r_QT], writes=[r_ps])
                        pt, r_pt, _ = pt_ring.next()
                        sf, r_sf, _ = sf_ring.next()
                        s0 = qg * 512 - m * 128 if typ == "A" else qg * 512
                        S.op("dve", lambda g: g.tensor_tensor(out=sf[:, c0:512], in0=ps[:, c0:512], in1=LHc[:, s0 + c0:s0 + 512], op=ALU.add),
                             reads=[r_ps, r_LHc], writes=[r_sf])
                        if typ == "A":
                            S.op("act", lambda g: g.activation(out=pt[:, c0:512], in_=sf[:, c0:512], func=AF.Exp, scale=SCALE),
                                 reads=[r_sf], writes=[r_pt])
                        else:
                            if m >= 4 * qg:
                                S.op("dve", lambda g: g.tensor_tensor(out=sf[:, c0:c0 + 128], in0=sf[:, c0:c0 + 128], in1=negLEf[:], op=ALU.add),
                                     reads=[r_sf, r_c], writes=[r_sf])
                            S.op("act", lambda g: g.activation(out=pt[:, c0:512], in_=sf[:, c0:512], func=AF.Exp, scale=SCALE,
                                                               bias=cn3[:, m, hi:hi + 1]),
                                 reads=[r_sf, r_cc], writes=[r_pt])
                        state[i] = (pt, r_pt, c0)

                    def do_av(i):
                        qg, m = steps[i]
                        pt, r_pt, c0 = state.pop(i)
                        for j in range(c0 // 128, 4):
                            S.op("pe", lambda g, j=j: g.matmul(pO[:, j, 0:129], lhsT=pt[:, j * 128:(j + 1) * 128], rhs=Vt[:, m, 0:129],
                                                               start=(m == 0), stop=(m == 4 * qg + j)),
                                 reads=[r_pt, r_Vt], writes=[r_pO[j]])
                        if m == 4 * qg + 3:
                            epilogue_ac_1(l, h, qg)
                            deferred.append((i + 3, qg))
                            pump()

                    deferred = []
                    do_s(0)
                    for i in range(len(steps)):
                        if i + 1 < len(steps):
                            do_s(i + 1)
                        pump()
                        do_av(i)
                        while deferred and deferred[0][0] <= i:
                            epilogue_ac_2(l, h, deferred.pop(0)[1], oTh, r_oT)
                        if nxt_gen is not None and i >= 6 and i % 4 == 2:
                            try:
                                next(nxt_gen)
                            except StopIteration:
                                nxt_gen = None
                    while deferred:
                        epilogue_ac_2(l, h, deferred.pop(0)[1], oTh, r_oT)
                    if nxt_gen is not None:
                        for _ in nxt_gen:
                            pass
                else:
                    paT = pO[:, 0:2, :].rearrange("p a b -> p (a b)").bitcast(BF16).rearrange("p (c n) -> p c n", n=128)
                    if hi == 0:
                        S.op("dve", lambda g: g.memset(cumbufs[1][:, 0:1], 0.0), reads=[r_LH, r_lm], writes=[r_cums[1], r_LH, r_LHs[1], r_lm])

                    abufs3 = [abuf, vw(XT, 0, 4096, BF16), vw(XT, 4096, 4096, BF16)]
                    r_abufs3 = [r_abuf, r_abufs[1], r_ab3]
                    aTbs = [aTb, vw(XT, 8192, 4096, BF16, "p (c n) -> p c n", n=128)]
                    r_aTs = [r_aT, r_aT2]

                    r_ebc = [[Res("eb%d_%d" % (k, c)) for c in range(4)] for k in range(2)]

                    def midA(n):
                        eb, cb_, r_cb = ebufs[n % 2], cumbufs[n % 2], r_cums[n % 2]
                        Nk = 128 * (n + 1)
                        S.op("act", lambda g: g.activation(out=eb[:, 0:Nk + 1], in_=cb_[:, 0:Nk + 1], func=AF.Exp, bias=stat[:, 4 + n % 2:5 + n % 2]),
                             reads=[r_cb, r_statB[n % 2]], writes=r_ebc[n % 2] + [r_ebufs[n % 2]])

                    def front(n):
                        eb, cb_, r_cb = ebufs[n % 2], cumbufs[n % 2], r_cums[n % 2]
                        Nk = 128 * (n + 1)
                        nch = (Nk + 511) // 512
                        for ch in range(nch):
                            k0 = ch * 512
                            kw = min(512, Nk - k0)
                            r_e = r_ebc[n % 2][ch]
                            ps, r_ps = ps_next()
                            S.op("pe", lambda g, ps=ps, k0=k0, kw=kw: g.matmul(ps[:, 0:kw], lhsT=QT[:, n * 128:(n + 1) * 128], rhs=KT[:, k0:k0 + kw],
                                                                               start=True, stop=True),
                                 reads=[r_QT, r_KT], writes=[r_ps])
                            S.op("act", lambda g, ps=ps, k0=k0, kw=kw: g.activation(out=eb[:, k0:k0 + kw], in_=ps[:, 0:kw], func=AF.Exp, scale=SCALE),
                                 reads=[r_ps], writes=[r_e])
                            S.op("act", lambda g, k0=k0, kw=kw: g.activation(out=eb[:, k0:k0 + kw], in_=eb[:, k0:k0 + kw], func=AF.Ln, bias=oneb[:]),
                                 reads=[r_e], writes=[r_e])
                            if ch == nch - 1:
                                S.op("dve", lambda g: g.tensor_tensor(out=eb[:, n * 128:(n + 1) * 128], in0=eb[:, n * 128:(n + 1) * 128],
                                                                      in1=maskLTf[:], op=ALU.mult),
                                     reads=[r_e, r_c], writes=[r_e])
                            S.op("dve", lambda g, k0=k0, kw=kw: g.tensor_tensor_scan(out=cb_[:, 1 + k0:1 + k0 + kw], data0=eb[:, k0:k0 + kw],
                                                                                      data1=eb[:, k0:k0 + kw], initial=cb_[:, k0:k0 + 1],
                                                                                      op0=ALU.add, op1=ALU.max),
                                 reads=[r_e, r_cb], writes=[r_cb])
                        S.op("dve", lambda g: g.tensor_scalar(out=stat[:, 4 + n % 2:5 + n % 2], in0=cb_[:, Nk:Nk + 1], scalar1=-1.0, scalar2=None, op0=ALU.mult),
                             reads=[r_cb], writes=[r_statB[n % 2]])

                    def midB(n):
                        eb, ab, r_ab = ebufs[n % 2], abufs3[n % 3], r_abufs3[n % 3]
                        Nk = 128 * (n + 1)
                        S.op("dve", lambda g: g.tensor_tensor(out=ab[:, 0:Nk], in0=eb[:, 1:Nk + 1], in1=eb[:, 0:Nk], op=ALU.subtract),
                             reads=r_ebc[n % 2] + [r_ebufs[n % 2]], writes=[r_ab])

                    def tailA(n):
                        ab, r_ab = abufs3[n % 3], r_abufs3[n % 3]
                        for m in range(n + 1):
                            S.op("pe", lambda g, m=m: g.transpose(out=paT[:, m, :], in_=ab[:, m * 128:(m + 1) * 128], identity=identb[:]),
                                 reads=[r_ab, r_c], writes=[r_pO[0], r_pO[1]])
                        evac(aTbs[n % 2][:, 0:n + 1, :], paT[:, 0:n + 1, :], [r_pO[0], r_pO[1]], [r_aTs[n % 2]])

                    def tailB(n):
                        ko = 2 + n % 2
                        aT_ = aTbs[n % 2]
                        for m in range(n + 1):
                            S.op("pe", lambda g, m=m: g.matmul(pO[:, ko, 0:128], lhsT=aT_[:, m, :], rhs=Vt[:, m, 0:128], start=(m == 0), stop=(m == n)),
                                 reads=[r_aTs[n % 2], r_Vt], writes=[r_pO[ko]])

                    def epiA(n):
                        ko = 2 + n % 2
                        pv = pO[:, ko, :]
                        S.op("act", lambda g: g.activation(out=junk[:, 0:128], in_=pv[:, 0:128], func=AF.Square, accum_out=stat[:, 32:33]),
                             reads=[r_pO[ko]], writes=[r_junk, r_statE])
                        S.op("act", lambda g: g.activation(out=stat[:, 33:34], in_=stat[:, 32:33], func=AF.Ln, scale=1.0 / HD, bias=epsb[:]),
                             reads=[r_statE], writes=[r_statE])
                        S.op("act", lambda g: g.activation(out=stat[:, 34:35], in_=stat[:, 33:34], func=AF.Exp, scale=-0.5),
                             reads=[r_statE], writes=[r_statE])
                        S.op("dve", lambda g: g.tensor_scalar(out=obuf[:, n % 4, :], in0=pv[:, 0:128], scalar1=stat[:, 34:35], scalar2=None, op0=ALU.mult),
                             reads=[r_pO[ko], r_statE], writes=[r_ob])

                    def epiB(n):
                        S.op("pe", lambda g: g.transpose(out=pTb[:, n % 4, :], in_=obuf[:, n % 4, :], identity=identb[:]),
                             reads=[r_ob, r_c], writes=[r_pT])
                        S.op("act", lambda g: g.activation(out=oTh[:, n * 128:(n + 1) * 128], in_=pTb[:, n % 4, :], func=AF.Copy, scale=ghead[:, l, h:h + 1]),
                             reads=[r_pT, r_par], writes=[r_oT])

                    for i in range(NT + 6):
                        pump()
                        pump()
                        if 0 <= i - 1 < NT:
                            midA(i - 1)
                        if i < NT:
                            front(i)
                        if 0 <= i - 1 < NT:
                            midB(i - 1)
                        if 0 <= i - 3 < NT:
                            tailA(i - 3)
                        pump()
                        if 0 <= i - 4 < NT:
                            tailB(i - 4)
                        if 0 <= i - 5 < NT:
                            epiA(i - 5)
                        if 0 <= i - 6 < NT:
                            epiB(i - 6)
                drain_gen()
                S.dma("sp", lambda g, h=h, oTh=oTh: g.dma_start(out=oT_d[h], in_=oTh), d_oT[h % 2], reads=[r_oT], writes=[r_oTd[h]])
            S.barrier()

            wgu_slots = [vw(WGU, 8192 * i, 8192, BF16, "p (a c n) -> p a c n", a=2, n=128) for i in range(3)]
            wgu_ring = Ring(S, wgu_slots, with_dsem=True)
            wgu_v = w_gu_d[l].rearrange("(c p) n -> p c n", p=128)
            wgu_ld = {}

            def issue_wgu(tt, j):
                wgu, r_wgu, d_wgu = wgu_ring.next()
                S.dma("pool", lambda g: g.dma_start(out=wgu[:, 0, :, :], in_=wgu_v[:, :, j * 128:(j + 1) * 128]), d_wgu, writes=[r_wgu])
                S.dma("pool", lambda g: g.dma_start(out=wgu[:, 1, :, :], in_=wgu_v[:, :, DFF + j * 128:DFF + (j + 1) * 128]), d_wgu, writes=[r_wgu])
                wgu_ld[(tt, j)] = (wgu, r_wgu)

            oT = hnT
            r_oTs = Res("oT")
            d_oTl = S.dsem()
            for h in range(NH):
                S.dma("sp", lambda g, h=h: g.dma_start(out=oT[:, h, :], in_=oT_d[h]), d_oTl, reads=[r_oTd[h]], writes=[r_oTs])
            if l == 0:
                dbg_dump("oT", oT, [r_oTs])
            wo_slots = [vw(BIGB, 16384 * i, 16384, BF16, "p (c n) -> p c n", n=512) for i in range(2)]
            wo_ring = Ring(S, wo_slots, with_dsem=True)
            xp_ring = Ring(S, [vw(BIGB, 32768 + 2048 * i, 2048, F32) for i in range(4)], with_dsem=True)
            d_st = [S.dsem() for _ in range(4)]
            acc_i = 0
            wo_v = w_o_d[l].rearrange("(c p) n -> p c n", p=128)
            for cc in range(4):
                wo, r_wo, d_wo = wo_ring.next()
                S.dma("pool", lambda g, wo=wo, cc=cc: g.dma_start(out=wo, in_=wo_v[:, :, cc * 512:(cc + 1) * 512]), d_wo, writes=[r_wo])
                if cc == 1:
                    for jp in range(3):
                        issue_wgu(0, jp)
                for tb in range(NT):
                    k = acc_i % 6
                    acc_i += 1
                    if k < 2:
                        ps, r_ps = pS[:, k, :], r_pS[k]
                    else:
                        ps, r_ps = pO[:, k - 2, :], r_pO[k - 2]
                    for h in range(NH):
                        S.op("pe", lambda g, ps=ps, wo=wo, h=h, tb=tb: g.matmul(ps, lhsT=oT[:, h, tb * 128:(tb + 1) * 128], rhs=wo[:, h, :],
                                                                               start=(h == 0), stop=(h == NH - 1)),
                             reads=[r_oTs, r_wo], writes=[r_ps])
                    xp, r_xp, d_xp = xp_ring.next()
                    S.dma("sp", lambda g, xp=xp, tb=tb, cc=cc: g.dma_start(out=xp, in_=src_d[tb * 128:(tb + 1) * 128, cc * 512:(cc + 1) * 512]),
                          d_xp, reads=[r_x[tb]], writes=[r_xp])
                    S.op("dve", lambda g, xp=xp, ps=ps: g.tensor_tensor(out=xp, in0=ps, in1=xp, op=ALU.add), reads=[r_ps, r_xp], writes=[r_xp])
                    S.dma("sp", lambda g, xp=xp, tb=tb, cc=cc: g.dma_start(out=xres_d[tb * 128:(tb + 1) * 128, cc * 512:(cc + 1) * 512], in_=xp),
                          d_st[acc_i % 4], reads=[r_xp], writes=[r_x[tb]])
            S.barrier()
            if l == 0 and "x1" in dbg_out:
                dd = S.dsem()
                for tb in range(NT):
                    xt, r_xt, d_xt = xt_slots[tb % 2]
                    S.dma("sp", lambda g, xt=xt, tb=tb: g.dma_start(out=xt, in_=xres_d[tb * 128:(tb + 1) * 128, :]), d_xt, reads=[r_x[tb]], writes=[r_xt])
                    S.dma("sp", lambda g, xt=xt, tb=tb: g.dma_start(out=dbg_out["x1"][tb * 128:(tb + 1) * 128, :], in_=xt), dd, reads=[r_xt])
                S.barrier()

            hn2T = vw(BIGA, 0, 32768, BF16, "p (c n) -> p c n", n=1024)
            r_hn2 = Res("hn2T")
            wd_slots = [vw(BIGA, 32768 + 11264 * i, 11264, BF16, "p (c n) -> p c n", n=128) for i in range(2)]
            wd_ring = Ring(S, wd_slots, with_dsem=True)
            ots_ring = Ring(S, [vw(BIGA, 32768 + 22528 + 2048 * i, 2048, F32) for i in range(2)])
            actT = vw(BIGB, 0, 90112, BF16, "p (c n) -> p c n", n=1024)
            r_act = Res("actT")
            cv_ring = Ring(S, [vw(XT, 6144 * i, 6144, F32, "p (a n) -> p a n", a=3) for i in range(2)])
            xq_ring = Ring(S, [vw(XT, 2048 * i, 2048, F32, "p (a n) -> p a n", a=4) for i in range(6)], with_dsem=True)
            d_xst = [S.dsem() for _ in range(6)]
            r_carry = Res("carry")
            cw3 = cw[:, l, :].rearrange("p (j k) -> p j k", k=3)
            wd_v = w_down_d[l].rearrange("(j p) n -> p j n", p=128)
            wd_ld = {}

            def issue_wd(tt, dc):
                wd, r_wd, d_wd = wd_ring.next()
                S.dma("pool", lambda g: g.dma_start(out=wd, in_=wd_v[:, :, dc * 128:(dc + 1) * 128]), d_wd, writes=[r_wd])
                wd_ld[(tt, dc)] = (wd, r_wd)
            pgu_i = 0
            S.op("dve", lambda g: g.memset(carry[:], 0.0), writes=[r_carry])
            for tt in range(2):
                for i in range(8):
                    tb = tt * 8 + i
                    norm_transpose(xres_d[tb * 128:(tb + 1) * 128, :], r_x[tb], i, gffn[:, l, :], hn2T, r_hn2, i * 128)
                S.barrier()
                for j in range(NF):
                    if (tt, j) not in wgu_ld:
                        issue_wgu(tt, j)
                    wgu, r_wgu = wgu_ld.pop((tt, j))
                    if j == 6:
                        issue_wd(tt, 0)
                        issue_wd(tt, 1)
                    for half in range(2):
                        kk = (pgu_i % 2) * 2
                        pgu_i += 1
                        pv2 = [pO[:, kk, :], pO[:, kk + 1, :]]
                        rv2 = [r_pO[kk], r_pO[kk + 1]]
                        for a in range(2):
                            for c in range(16):
                                S.op("pe", lambda g, a=a, c=c, wgu=wgu, pv2=pv2, half=half: g.matmul(pv2[a], lhsT=wgu[:, a, c, :],
                                                                                                    rhs=hn2T[:, c, half * 512:(half + 1) * 512],
                                                                                                    start=(c == 0), stop=(c == 15)),
                                     reads=[r_wgu, r_hn2], writes=[rv2[a]])
                        cvb, r_cv, _ = cv_ring.next()
                        first = (tt == 0 and half == 0)
                        for a in range(2):
                            jj = a * NF + j
                            ph = pv2[a]
                            acc = cvb[:, a, :]
                            S.op("act", lambda g, acc=acc, ph=ph, jj=jj: g.activation(out=acc, in_=ph, func=AF.Identity, scale=cw3[:, jj, 2:3],
                                                                                    bias=cb[:, l, jj:jj + 1]),
                                 reads=[rv2[a], r_par], writes=[r_cv])
                            S.op("dve", lambda g, acc=acc, ph=ph, jj=jj: g.scalar_tensor_tensor(out=acc[:, 1:512], in0=ph[:, 0:511], scalar=cw3[:, jj, 1:2],
                                                                                               in1=acc[:, 1:512], op0=ALU.mult, op1=ALU.add),
                                 reads=[rv2[a], r_par, r_cv], writes=[r_cv])
                            S.op("dve", lambda g, acc=acc, ph=ph, jj=jj: g.scalar_tensor_tensor(out=acc[:, 2:512], in0=ph[:, 0:510], scalar=cw3[:, jj, 0:1],
                                                                                               in1=acc[:, 2:512], op0=ALU.mult, op1=ALU.add),
                                 reads=[rv2[a], r_par, r_cv], writes=[r_cv])
                            if not first:
                                cr = carry[:, 2 * jj:2 * jj + 2]
                                S.op("dve", lambda g, acc=acc, cr=cr, jj=jj: g.scalar_tensor_tensor(out=acc[:, 0:2], in0=cr, scalar=cw3[:, jj, 0:1],
                                                                                                   in1=acc[:, 0:2], op0=ALU.mult, op1=ALU.add),
                                     reads=[r_carry, r_par, r_cv], writes=[r_cv])
                                S.op("dve", lambda g, acc=acc, cr=cr, jj=jj: g.scalar_tensor_tensor(out=acc[:, 0:1], in0=cr[:, 1:2], scalar=cw3[:, jj, 1:2],
                                                                                                   in1=acc[:, 0:1], op0=ALU.mult, op1=ALU.add),
                                     reads=[r_carry, r_par, r_cv], writes=[r_cv])
                            S.op("dve", lambda g, ph=ph, jj=jj: g.tensor_copy(out=carry[:, 2 * jj:2 * jj + 2], in_=ph[:, 510:512]),
                                 reads=[rv2[a], r_cv], writes=[r_carry])
                        S.op("act", lambda g, cvb=cvb: g.activation(out=cvb[:, 2, :], in_=cvb[:, 0, :], func=AF.Silu), reads=[r_cv], writes=[r_cv])
                        S.op("dve", lambda g, cvb=cvb, j=j, half=half: g.tensor_tensor(out=actT[:, j, half * 512:(half + 1) * 512], in0=cvb[:, 2, :],
                                                                                      in1=cvb[:, 1, :], op=ALU.mult),
                             reads=[r_cv], writes=[r_act])
                if l == 0 and tt == 0:
                    dbg_dump("actT", actT, [r_act])
                S.barrier()
                po_i = 0
                ptk = 0
                for dc in range(16):
                    if (tt, dc) not in wd_ld:
                        issue_wd(tt, dc)
                    wd, r_wd = wd_ld.pop((tt, dc))
                    if tt == 0 and dc == 8:
                        for jp in range(3):
                            issue_wgu(1, jp)
                    for half in range(2):
                        k = po_i % 2
                        po_i += 1
                        po, r_po = pS[:, k, :], r_pS[k]
                        for j in range(NF):
                            S.op("pe", lambda g, po=po, wd=wd, j=j, half=half: g.matmul(po, lhsT=wd[:, j, :], rhs=actT[:, j, half * 512:(half + 1) * 512],
                                                                                       start=(j == 0), stop=(j == NF - 1)),
                                 reads=[r_wd, r_act], writes=[r_po])
                        ots, r_ots, _ = ots_ring.next()
                        S.op("act", lambda g, ots=ots, po=po: g.activation(out=ots, in_=po, func=AF.Copy), reads=[r_po], writes=[r_ots])
                        ptv, r_ptv = ((pX, r_pX), (pT, r_pT))[ptk % 2]
                        ptk += 1
                        for b in range(4):
                            S.op("pe", lambda g, ptv=ptv, ots=ots, b=b: g.transpose(out=ptv[:, b * 128:(b + 1) * 128], in_=ots[:, b * 128:(b + 1) * 128],
                                                                                     identity=identf[:]),
                                 reads=[r_ots, r_c], writes=[r_ptv])
                        xq, r_xq, d_xq = xq_ring.next()
                        t0 = tt * 1024 + half * 512
                        rows = [r_x[(t0 // 128) + b] for b in range(4)]
                        S.dma("sp", lambda g, xq=xq, t0=t0, dc=dc: g.dma_start(
                            out=xq, in_=xres_d[t0:t0 + 512, dc * 128:(dc + 1) * 128].rearrange("(b p) n -> p b n", p=128)),
                            d_xq, reads=rows, writes=[r_xq])
                        S.op("dve", lambda g, xq=xq, ptv=ptv: g.tensor_tensor(out=xq, in0=ptv.rearrange("p (b n) -> p b n", n=128), in1=xq, op=ALU.add),
                             reads=[r_ptv, r_xq], writes=[r_xq])
                        S.dma("sp", lambda g, xq=xq, t0=t0, dc=dc: g.dma_start(
                            out=xres_d[t0:t0 + 512, dc * 128:(dc + 1) * 128].rearrange("(b p) n -> p b n", p=128), in_=xq),
                            d_xst[(po_i - 1) % 6], reads=[r_xq], writes=rows)
                S.barrier()
            if l == 0 and "x2" in dbg_out:
                dd = S.dsem()
                for tb in range(NT):
                    xt, r_xt, d_xt = xt_slots[tb % 2]
                    S.dma("sp", lambda g, xt=xt, tb=tb: g.dma_start(out=xt, in_=xres_d[tb * 128:(tb + 1) * 128, :]), d_xt, reads=[r_x[tb]], writes=[r_xt])
                    S.dma("sp", lambda g, xt=xt, tb=tb: g.dma_start(out=dbg_out["x2"][tb * 128:(tb + 1) * 128, :], in_=xt), dd, reads=[r_xt])
                S.barrier()

        gfb = vw(BIGA, 0, 8192, F32)
        r_gfb = Res("gfb")
        d_g = S.dsem()
        S.dma("sp", lambda g: g.dma_start(out=gfb, in_=gfin_d), d_g, writes=[r_gfb])
        yo_ring = Ring(S, [vw(BIGB, 8192 * i, 8192, F32) for i in range(2)], with_dsem=True)
        src_fin = xres_d if n_layers > 0 else x_d
        for i in range(NT):
            xt, r_xt, d_xt = xt_slots[i % 2]
            S.dma("sp", lambda g, xt=xt, i=i: g.dma_start(out=xt, in_=src_fin[i * 128:(i + 1) * 128, :]), d_xt, reads=[r_x[i]], writes=[r_xt])
            S.op("act", lambda g, xt=xt: g.activation(out=xs_b, in_=xt, func=AF.Square, accum_out=stat[:, 0:1]), reads=[r_xt], writes=[r_xs, r_stat])
            S.op("act", lambda g: g.activation(out=stat[:, 1:2], in_=stat[:, 0:1], func=AF.Ln, scale=1.0 / D, bias=epsb[:]), reads=[r_stat], writes=[r_stat])
            S.op("act", lambda g: g.activation(out=stat[:, 2:3], in_=stat[:, 1:2], func=AF.Exp, scale=-0.5), reads=[r_stat], writes=[r_stat])
            yo, r_yo, d_yo = yo_ring.next()
            S.op("dve", lambda g, xt=xt, yo=yo: g.scalar_tensor_tensor(out=yo, in0=xt, scalar=stat[:, 2:3], in1=gfb, op0=ALU.mult, op1=ALU.mult),
                 reads=[r_xt, r_stat, r_gfb], writes=[r_yo])
            S.dma("sp", lambda g, yo=yo, i=i: g.dma_start(out=y_d[i * 128:(i + 1) * 128, :], in_=yo), d_yo, reads=[r_yo])
        S.emit(nc)
    return nc


_NC_CACHE = {}


def _host_params(g_mix, b_f, g_head, g_ffn, conv_w, conv_b, g_final):
    f = np.float32

    def colT(v):
        return np.ascontiguousarray(np.asarray(v, f).reshape(DEPTH, 16, 128).transpose(0, 2, 1))

    gheadT = np.ascontiguousarray(np.asarray(g_head, f).transpose(0, 2, 1))
    bfb = np.ascontiguousarray(np.broadcast_to(np.asarray(b_f, f)[:, None, :], (DEPTH, 128, HC)))
    cwT = np.ascontiguousarray(np.asarray(conv_w, f).reshape(DEPTH, 3, 88, 128).transpose(0, 3, 2, 1)).reshape(DEPTH, 128, 88 * 3)
    cbT = np.ascontiguousarray(np.asarray(conv_b, f).reshape(DEPTH, 88, 128).transpose(0, 2, 1))
    gfinb = np.ascontiguousarray(np.broadcast_to(np.asarray(g_final, f)[None, :], (128, D)))
    return dict(gmixT=colT(g_mix), gffnT=colT(g_ffn), gheadT=gheadT, bfb=bfb, cwT=cwT, cbT=cbT, gfinb=gfinb)


def kernel(x, g_mix, w_in, b_f, g_head, w_o, g_ffn, w_gu, conv_w, conv_b, w_down, g_final):
    if "nc" not in _NC_CACHE:
        _NC_CACHE["nc"] = build_nc()
    nc = _NC_CACHE["nc"]
    par = _host_params(g_mix, b_f, g_head, g_ffn, conv_w, conv_b, g_final)
    shared = dict(w_in=np.ascontiguousarray(w_in, np.float32), w_o=np.ascontiguousarray(w_o, np.float32),
                  w_gu=np.ascontiguousarray(w_gu, np.float32), w_down=np.ascontiguousarray(w_down, np.float32), **par)
    x = np.asarray(x, np.float32)
    in_maps = [dict(x=np.ascontiguousarray(x[b]), **shared) for b in range(8)]
    res = run_bass_kernel_spmd(nc, in_maps, core_ids=list(range(8)))
    return np.stack([res.results[b]["y"] for b in range(8)], axis=0)
```

```python
import math
import contextlib
import numpy as np
import concourse.bass as bass
import concourse.mybir as mybir
from concourse.bass_utils import run_bass_kernel_spmd

F32 = mybir.dt.float32
BF16 = mybir.dt.bfloat16
I32 = mybir.dt.int32
AF = mybir.ActivationFunctionType
ALU = mybir.AluOpType

ENGS = ("pe", "act", "dve", "pool", "sp")

T = 2048
D = 2048
NT = 16
HD = 128
NH = 16
HA, HB, HC = 6, 5, 5
DFF = 5632
NF = 44
N_IN = 6149
EPS = 1e-6
SCALE = 1.0 / math.sqrt(HD)
DEPTH = 2


class Res:
    __slots__ = ("name", "w", "readers")

    def __init__(self, name=""):
        self.name = name
        self.w = None
        self.readers = []


class DSem:
    __slots__ = ("idx", "val", "handle", "res")

    def __init__(self, idx):
        self.idx = idx
        self.val = 0
        self.handle = None
        self.res = Res("dsem%d" % idx)


class Op:
    __slots__ = ("eng", "call", "deps", "need_inc", "seq", "is_dma", "dsem", "dval")


class _Rec:
    def __init__(self):
        self.call = None

    def __getattr__(self, name):
        def f(*a, **k):
            assert self.call is None
            self.call = (name, a, k)
            return None
        return f


class Sched:
    def __init__(self):
        self.ops = {e: [] for e in ENGS}
        self.dsems = []
        self.bar_res = {e: Res("bar_" + e) for e in ENGS}

    def dsem(self):
        d = DSem(len(self.dsems))
        self.dsems.append(d)
        return d

    def _add(self, eng, fn, reads, writes, is_dma, dsem):
        op = Op()
        op.eng = eng
        rec = _Rec()
        fn(rec)
        op.call = rec.call
        op.is_dma = is_dma
        op.need_inc = False
        op.seq = None
        op.dsem = dsem
        deps = {}
        for t in reads:
            w = t.w
            if w is not None:
                deps[id(w)] = w
        for t in writes:
            w = t.w
            if w is not None and (w.is_dma or w.eng != eng):
                deps[id(w)] = w
            for r in t.readers:
                if r.is_dma or r.eng != eng:
                    deps[id(r)] = r
        op.deps = list(deps.values())
        if is_dma:
            dsem.val += 16
            op.dval = dsem.val
        else:
            op.dval = None
        for t in writes:
            t.w = op
            t.readers = []
        for t in reads:
            rs = [r for r in t.readers if r.is_dma or r.eng != eng]
            rs.append(op)
            t.readers = rs
        self.ops[eng].append(op)
        return op

    def op(self, eng, fn, reads=(), writes=()):
        return self._add(eng, fn, reads, writes, False, None)

    def dma(self, eng, fn, dsem, reads=(), writes=()):
        return self._add(eng, fn, reads, list(writes) + [dsem.res], True, dsem)

    def barrier(self):
        lasts = []
        for e in ENGS:
            for op in reversed(self.ops[e]):
                if not op.is_dma:
                    lasts.append(op)
                    break
        for d in self.dsems:
            if d.res.w is not None:
                lasts.append(d.res.w)
        for e in ENGS:
            op = self.op(e, lambda g: g.nop())
            op.deps = [o for o in lasts if o.is_dma or o.eng != e]

    def emit(self, nc):
        for e in ENGS:
            for op in self.ops[e]:
                for d in op.deps:
                    if not d.is_dma:
                        d.need_inc = True
        for e in ENGS:
            c = 0
            for op in self.ops[e]:
                if (not op.is_dma) and op.need_inc:
                    c += 1
                    op.seq = c
        with contextlib.ExitStack() as st:
            esem = {e: st.enter_context(nc.semaphore("s_" + e)) for e in ENGS}
            for d in self.dsems:
                d.handle = st.enter_context(nc.semaphore("d%d" % d.idx))
            block = st.enter_context(nc.Block())
            hmap = {"pe": "tensor", "act": "scalar", "dve": "vector", "pool": "gpsimd", "sp": "sync"}

            def make(e):
                ops = self.ops[e]

                def body(eng):
                    waited = {}
                    for op in ops:
                        need = {}
                        for d in op.deps:
                            if d.is_dma:
                                k = ("d", d.dsem.idx)
                                h = d.dsem.handle
                                v = d.dval
                            else:
                                k = ("e", d.eng)
                                h = esem[d.eng]
                                v = d.seq
                            if waited.get(k, 0) >= v:
                                continue
                            if k not in need or need[k][1] < v:
                                need[k] = (h, v)
                        for k, (h, v) in need.items():
                            eng.wait_ge(h, v)
                            waited[k] = v
                        name, a, k = op.call
                        inst = getattr(eng, name)(*a, **k)
                        if op.is_dma:
                            inst.then_inc(op.dsem.handle, 16)
                        elif op.need_inc:
                            inst.then_inc(esem[e], 1)
                    if e == "sp":
                        for d in self.dsems:
                            if d.val > 0:
                                eng.wait_ge(d.handle, d.val)
                return body

            for e in ENGS:
                getattr(block, hmap[e])(make(e))


class Ring:
    def __init__(self, S, aps, with_dsem=False):
        self.slots = [(ap, Res(), S.dsem() if with_dsem else None) for ap in aps]
        self.i = 0

    def next(self):
        s = self.slots[self.i % len(self.slots)]
        self.i += 1
        return s


def head_info(h):
    if h < HA:
        base, n, i, typ = 0, HA, h, "A"
    elif h < HA + HB:
        base, n, i, typ = 3 * HA * HD, HB, h - HA, "B"
    else:
        base, n, i, typ = 3 * HA * HD + 3 * HB * HD, HC, h - HA - HB, "C"
    return typ, i, base + i * HD, base + n * HD + i * HD, base + 2 * n * HD + i * HD


def build_nc(dbg=None, n_layers=DEPTH):
    dbg = dbg or {}
    nc = bass.Bass("TRN2", target_bir_lowering=False)

    def din(name, shape, dt=F32):
        return nc.dram_tensor(name, list(shape), dt, kind="ExternalInput").ap()

    x_d = din("x", [T, D])
    w_in_d = din("w_in", [DEPTH, D, N_IN])
    w_o_d = din("w_o", [DEPTH, D, D])
    w_gu_d = din("w_gu", [DEPTH, D, 2 * DFF])
    w_down_d = din("w_down", [DEPTH, DFF, D])
    gmix_d = din("gmixT", [DEPTH, 128, 16])
    gffn_d = din("gffnT", [DEPTH, 128, 16])
    ghead_d = din("gheadT", [DEPTH, 128, 16])
    bf_d = din("bfb", [DEPTH, 128, HC])
    cw_d = din("cwT", [DEPTH, 128, 88 * 3])
    cb_d = din("cbT", [DEPTH, 128, 88])
    gfin_d = din("gfinb", [128, D])
    y_d = nc.dram_tensor("y", [T, D], F32, kind="ExternalOutput").ap()
    xres_d = nc.dram_tensor("xres", [T, D], F32).ap()
    oT_d = nc.dram_tensor("oTd", [NH, 128, T], BF16).ap()
    dbg_out = {}
    for k, (shape, dt) in dbg.items():
        dbg_out[k] = nc.dram_tensor("dbg_" + k, list(shape), dt, kind="ExternalOutput").ap()

    S = Sched()
    st = contextlib.ExitStack()
    with st:
        def sb(name, free, dt=F32):
            return st.enter_context(nc.sbuf_tensor(name, [128] + list(free), dt))

        BIGA = sb("BIGA", [16384], F32)
        BIGB = sb("BIGB", [22528], F32)
        XT = sb("XT", [4096], F32)
        WGU = sb("WGU", [6144], F32)
        identb = sb("identb", [128], BF16)
        identf = sb("identf", [128], F32)
        maskLEb = sb("maskLEb", [128], BF16)
        triLEf = sb("triLEf", [128], F32)
        maskLTf = sb("maskLTf", [128], F32)
        onesf = sb("onesf", [128], F32)
        negLEf = sb("negLEf", [128], F32)
        gmix = sb("gmix", [DEPTH, 16], F32)
        gffn = sb("gffn", [DEPTH, 16], F32)
        ghead = sb("ghead", [DEPTH, 16], F32)
        bfb = sb("bfb_s", [DEPTH, HC], F32)
        cw = sb("cw", [DEPTH, 88 * 3], F32)
        cb = sb("cb", [DEPTH, 88], F32)
        carry = sb("carry", [88 * 2], F32)
        stat = sb("stat", [64], F32)
        cneg = sb("cneg", [16 * HC], F32)
        ccar = sb("ccar", [17 * HC], F32)
        biasC = sb("biasC", [4 * 16 * HC], F32)
        lf = sb("lf", [16 * HC], F32)
        epsb = sb("epsb", [1], F32)
        oneb = sb("oneb", [1], F32)
        mhalf = sb("mhalf", [1], F32)

        pS = st.enter_context(nc.psum_tensor("pS", [128, 2, 512], F32))
        pO = st.enter_context(nc.psum_tensor("pO", [128, 4, 512], F32))
        pX = st.enter_context(nc.psum_tensor("pX", [128, 512], F32))
        pT = st.enter_context(nc.psum_tensor("pT", [128, 512], F32))
        r_pS = [Res("pS0"), Res("pS1")]
        r_pO = [Res("pO%d" % i) for i in range(4)]
        r_pX = Res("pX")
        r_pT = Res("pT")

        def vw(region, off_b, nbytes, dt, pat=None, **kw):
            a = region[:, off_b // 4:(off_b + nbytes) // 4]
            if dt != F32:
                a = a.bitcast(dt)
            if pat:
                a = a.rearrange(pat, **kw)
            return a

        dcnt = [0]

        def dbg_dump(key, ap_sb, res, dram_slice=None):
            if key not in dbg_out:
                return
            d = S.dsem()
            tgt = dbg_out[key] if dram_slice is None else dram_slice(dbg_out[key])
            S.dma("sp", lambda g: g.dma_start(out=tgt, in_=ap_sb), d, reads=res)

        r_c = Res("consts")
        r_par = Res("params")
        dpar = S.dsem()
        for (dst, src) in ((gmix, gmix_d), (gffn, gffn_d), (ghead, ghead_d), (bfb, bf_d), (cw, cw_d), (cb, cb_d)):
            for l in range(DEPTH):
                S.dma("sp", lambda g, dst=dst, src=src, l=l: g.dma_start(out=dst[:, l, :], in_=src[l]), dpar,
                      writes=[r_par])
        tmpi = vw(XT, 0, 512, I32)
        tmpf = vw(XT, 512, 512, F32)
        S.op("pool", lambda g: g.iota(tmpi, pattern=[[1, 128]], base=0, channel_multiplier=-1), writes=[r_c])
        S.op("dve", lambda g: g.tensor_copy(out=tmpf, in_=tmpi), reads=[r_c], writes=[r_c])
        S.op("dve", lambda g: g.tensor_single_scalar(out=identf[:], in_=tmpf, scalar=0.0, op=ALU.is_equal), reads=[r_c], writes=[r_c])
        S.op("dve", lambda g: g.tensor_copy(out=identb[:], in_=identf[:]), reads=[r_c], writes=[r_c])
        S.op("dve", lambda g: g.tensor_single_scalar(out=triLEf[:], in_=tmpf, scalar=0.0, op=ALU.is_ge), reads=[r_c], writes=[r_c])
        S.op("dve", lambda g: g.tensor_copy(out=maskLEb[:], in_=triLEf[:]), reads=[r_c], writes=[r_c])
        S.op("dve", lambda g: g.tensor_single_scalar(out=maskLTf[:], in_=tmpf, scalar=0.0, op=ALU.is_lt), reads=[r_c], writes=[r_c])
        S.op("dve", lambda g: g.memset(onesf[:], 1.0), writes=[r_c])
        S.op("dve", lambda g: g.tensor_scalar(out=negLEf[:], in0=triLEf[:], scalar1=-1.0, scalar2=30000.0, op0=ALU.add, op1=ALU.mult),
             reads=[r_c], writes=[r_c])
        S.op("dve", lambda g: g.memset(epsb[:], EPS), writes=[r_c])
        S.op("dve", lambda g: g.memset(oneb[:], 1.0), writes=[r_c])
        S.op("dve", lambda g: g.memset(mhalf[:], -0.5), writes=[r_c])
        S.op("dve", lambda g: g.memset(carry[:], 0.0), writes=[r_c])

        o = 0
        QTs = []
        KTs = []
        for i in range(2):
            QTs.append(vw(BIGB, o, 4096, BF16)); o += 4096
            KTs.append(vw(BIGB, o, 4096, BF16)); o += 4096
        Vts = []
        for i in range(2):
            Vts.append(vw(BIGB, o, 16 * 132 * 2, BF16, "p (c n) -> p c n", n=132)); o += 16 * 132 * 2
        LM0 = vw(BIGB, o, 8192, F32); o += 8192
        LH = vw(BIGB, o, 8192, F32); o += 8192
        LHi = LH.bitcast(I32)
        sfs = [vw(BIGB, o + 2048 * i, 2048, F32) for i in range(2)]; o += 4096
        pts = [vw(BIGB, o + 1024 * i, 1024, BF16) for i in range(3)]; o += 3072
        ebuf = vw(BIGB, o, 8256, F32); o += 8256
        cumbuf = vw(BIGB, o, 8256, F32); o += 8256
        abuf = vw(BIGB, o, 4096, BF16); o += 4096
        aTb = vw(BIGB, o, 4096, BF16, "p (c n) -> p c n", n=128); o += 4096
        oThs = [vw(BIGB, o + 4096 * i, 4096, BF16) for i in range(2)]; o += 8192
        obuf = vw(BIGB, o, 1024, BF16, "p (c n) -> p c n", n=128); o += 1024
        junk = vw(BIGB, o, 4096, BF16); o += 4096
        assert o <= 22528 * 4, o
        hnT = vw(BIGA, 0, 65536, BF16, "p (c n) -> p c n", n=T)
        r_hnT = Res("hnT")
        wslots = [vw(WGU, 4096 * i, 4096, BF16, "p (c n) -> p c n", n=128) for i in range(6)]
        wring = Ring(S, wslots, with_dsem=True)

        r_lm = Res("LM0")

        def build_masks():
            t_i = LHi
            ti2 = ebuf[:, 0:2048].bitcast(I32)
            tf_d = cumbuf[:, 1:2049]
            tf_b = vw(BIGB, 64896, 8192, F32)
            tf_c = vw(BIGB, 73088, 8192, F32)
            R = [r_lm]

            def dv(fn):
                S.op("dve", fn, reads=R, writes=R)
            S.op("pool", lambda g: g.iota(t_i, pattern=[[1, 2048]], base=0, channel_multiplier=-1), reads=R, writes=R)
            dv(lambda g: g.tensor_copy(out=tf_d, in_=t_i))
            dv(lambda g: g.tensor_scalar(out=tf_b, in0=tf_d, scalar1=0.0, scalar2=None, op0=ALU.is_ge))
            dv(lambda g: g.tensor_scalar(out=tf_c, in0=tf_d, scalar1=128.0, scalar2=None, op0=ALU.is_le))
            dv(lambda g: g.tensor_tensor(out=LM0, in0=tf_b, in1=tf_c, op=ALU.mult))
            for (div, lim) in ((4.0, 512.0), (16.0, None)):
                dv(lambda g, div=div: g.tensor_scalar(out=tf_c, in0=tf_d, scalar1=1.0 / div, scalar2=None, op0=ALU.mult))
                dv(lambda g: g.tensor_copy(out=ti2, in_=tf_c))
                dv(lambda g: g.tensor_copy(out=ebuf[:, 0:2048], in_=ti2))
                dv(lambda g: g.tensor_tensor(out=tf_c, in0=tf_c, in1=ebuf[:, 0:2048], op=ALU.is_equal))
                dv(lambda g: g.tensor_tensor(out=tf_c, in0=tf_c, in1=tf_b, op=ALU.mult))
                if lim is not None:
                    dv(lambda g, lim=lim: g.scalar_tensor_tensor(out=tf_c, in0=tf_d, scalar=lim, in1=tf_c, op0=ALU.is_le, op1=ALU.mult))
                dv(lambda g: g.tensor_tensor(out=LM0, in0=LM0, in1=tf_c, op=ALU.add))
            dv(lambda g: g.tensor_scalar(out=tf_b, in0=LM0, scalar1=1.0, scalar2=-1.0, op0=ALU.min, op1=ALU.add))
            dv(lambda g: g.tensor_scalar(out=LM0, in0=LM0, scalar1=1e-18, scalar2=None, op0=ALU.max))
            S.op("act", lambda g: g.activation(out=LM0, in_=LM0, func=AF.Ln), reads=R, writes=R)
            dv(lambda g: g.scalar_tensor_tensor(out=LM0, in0=tf_b, scalar=1000.0, in1=LM0, op0=ALU.mult, op1=ALU.add))
            dv(lambda g: g.tensor_scalar(out=LM0, in0=LM0, scalar1=1.0 / SCALE, scalar2=None, op0=ALU.mult))
            for i in range(2):
                dv(lambda g, i=i: g.memset(Vts[i][:, :, 128:129], 1.0))
            dv(lambda g: g.memset(cumbuf[:, 0:1], 0.0))
            S.barrier()

        S.barrier()

        xt_slots = [(vw(XT, 8192 * i, 8192, F32), Res("xt%d" % i), S.dsem()) for i in range(2)]
        xs_b = junk
        r_xs = Res("xs")
        r_stat = Res("stat")
        ptr_views = [pO[:, 0:2, :].rearrange("p a b -> p (a b)").bitcast(BF16).rearrange("p (c n) -> p c n", n=128),
                     pO[:, 2:4, :].rearrange("p a b -> p (a b)").bitcast(BF16).rearrange("p (c n) -> p c n", n=128)]

        xs_bufs = [junk, vw(BIGB, 73088, 4096, BF16)]
        r_xss = [r_xs, Res("xs2")]
        r_stats = [r_stat, Res("stat2")]

        def norm_transpose(src_ap, r_src, i, gvec, dstT, r_dst, col0):
            k = i % 2
            xt, r_xt, d_xt = xt_slots[k]
            xsb, r_x_s, r_st, sc = xs_bufs[k], r_xss[k], r_stats[k], 40 * k
            S.dma("sp", lambda g: g.dma_start(out=xt, in_=src_ap), d_xt, reads=[r_src], writes=[r_xt])
            S.op("act", lambda g: g.activation(out=xsb, in_=xt, func=AF.Square, accum_out=stat[:, sc:sc + 1]),
                 reads=[r_xt], writes=[r_x_s, r_st])
            S.op("act", lambda g: g.activation(out=stat[:, sc + 1:sc + 2], in_=stat[:, sc:sc + 1], func=AF.Ln, scale=1.0 / D, bias=epsb[:]),
                 reads=[r_st], writes=[r_st])
            S.op("act", lambda g: g.activation(out=stat[:, sc + 2:sc + 3], in_=stat[:, sc + 1:sc + 2], func=AF.Exp, scale=-0.5),
                 reads=[r_st], writes=[r_st])
            S.op("dve", lambda g: g.tensor_scalar(out=xsb, in0=xt, scalar1=stat[:, sc + 2:sc + 3], scalar2=None, op0=ALU.mult),
                 reads=[r_xt, r_st], writes=[r_x_s])
            pv = ptr_views[k]
            rp = [r_pO[2 * k], r_pO[2 * k + 1]]
            for c in range(16):
                S.op("pe", lambda g, c=c: g.transpose(out=pv[:, c, :], in_=xsb[:, c * 128:(c + 1) * 128], identity=identb[:]),
                     reads=[r_x_s, r_c], writes=rp)
            S.op("dve", lambda g: g.tensor_tensor(out=dstT[:, :, col0:col0 + 128], in0=pv,
                                                  in1=gvec.unsqueeze(2).broadcast_to([128, 16, 128]), op=ALU.mult),
                 reads=rp + [r_par], writes=[r_dst])

        r_x = [Res("xrow%d" % i) for i in range(NT)]
        r_oTd = [Res("oTd%d" % h) for h in range(NH)]
        slopes = [2.0 ** (-8.0 * (i + 1) / HA) for i in range(HA)]

        for l in range(n_layers):
            src_d = x_d if l == 0 else xres_d
            for i in range(NT):
                norm_transpose(src_d[i * 128:(i + 1) * 128, :], r_x[i], i, gmix[:, l, :], hnT, r_hnT, i * 128)
            if l == 0:
                dbg_dump("hnT", hnT, [r_hnT])
            wf, r_wf, d_wf = wring.next()
            wfv = wf[:, :, 0:HC]
            S.dma("pool", lambda g: g.dma_start(out=wfv, in_=w_in_d[l].rearrange("(c p) n -> p c n", p=128)[:, :, 6144:6144 + HC]),
                  d_wf, writes=[r_wf])
            pfc = pX[:, 0:16 * HC].rearrange("p (b h) -> p b h", h=HC)
            for tb in range(NT):
                for c in range(16):
                    S.op("pe", lambda g, tb=tb, c=c: g.matmul(pfc[:, tb, :], lhsT=hnT[:, c, tb * 128:(tb + 1) * 128],
                                                               rhs=wfv[:, c, :], start=(c == 0), stop=(c == 15)),
                         reads=[r_hnT, r_wf], writes=[r_pX])
            r_lf = Res("lf")
            lf3 = lf[:].rearrange("p (b h) -> p b h", h=HC)
            S.op("dve", lambda g: g.tensor_tensor(out=lf3, in0=pfc, in1=bfb[:, l, :].unsqueeze(1).broadcast_to([128, 16, HC]), op=ALU.add),
                 reads=[r_pX, r_par], writes=[r_lf])
            S.op("act", lambda g: g.activation(out=lf[:], in_=lf[:], func=AF.Exp, scale=-1.0), reads=[r_lf], writes=[r_lf])
            S.op("act", lambda g: g.activation(out=lf[:], in_=lf[:], func=AF.Ln, bias=oneb[:]), reads=[r_lf], writes=[r_lf])
            pc1 = pT[:, 0:16 * HC]
            pc2 = pT[:, 128:128 + 16 * HC]
            S.op("pe", lambda g: g.matmul(pc1, lhsT=triLEf[:], rhs=lf[:], start=True, stop=True), reads=[r_lf, r_c], writes=[r_pT])
            S.op("pe", lambda g: g.matmul(pc2, lhsT=onesf[:], rhs=lf[:], start=True, stop=True), reads=[r_lf, r_c], writes=[r_pT])
            r_cc = Res("ccar")
            cc3 = ccar[:].rearrange("p (b h) -> p b h", h=HC)
            cn3 = cneg[:].rearrange("p (b h) -> p b h", h=HC)
            pc2v = pc2.rearrange("p (b h) -> p b h", h=HC)
            pc1v = pc1.rearrange("p (b h) -> p b h", h=HC)
            S.op("dve", lambda g: g.memset(ccar[:, 0:HC], 0.0), writes=[r_cc])
            for b in range(16):
                S.op("dve", lambda g, b=b: g.tensor_tensor(out=cc3[:, b + 1, :], in0=pc2v[:, b, :], in1=cc3[:, b, :], op=ALU.add),
                     reads=[r_pT, r_cc], writes=[r_cc])
            S.op("dve", lambda g: g.tensor_tensor(out=cn3, in0=pc1v, in1=cc3[:, 0:16, :], op=ALU.add), reads=[r_pT, r_cc], writes=[r_cc])
            if l == 0:
                dbg_dump("cneg", cneg[:], [r_cc])

            S.barrier()
            build_masks()
            if l == 0:
                dbg_dump("LM0", LM0, [r_lm])
            r_LH = Res("LH")
            sf_ring = Ring(S, sfs)
            pt_ring = Ring(S, pts)
            r_ebuf = Res("ebuf")
            r_cum = Res("cum")
            r_abuf = Res("abuf")
            r_aT = Res("aT")
            r_ob = Res("obuf")
            r_junk = r_xs
            d_oT = [S.dsem(), S.dsem()]
            r_oTh = [Res("oTh0"), Res("oTh1")]
            ps_i = [0]

            def ps_next():
                k = ps_i[0] % 2
                ps_i[0] += 1
                return pS[:, k, :], r_pS[k]

            ev_i = [0]

            def evac(out, in_, reads, writes):
                ev_i[0] += 1
                if ev_i[0] % 2:
                    S.op("act", lambda g: g.activation(out=out, in_=in_, func=AF.Copy), reads=reads, writes=writes)
                else:
                    S.op("dve", lambda g: g.tensor_copy(out=out, in_=in_), reads=reads, writes=writes)

            pTb = pT[:].bitcast(BF16)[:, 0:512].rearrange("p (c n) -> p c n", n=128)

            def epilogue(l, h, nblk, pviews, rviews, qcol0, with_den, oTh, r_oT):
                nb = nblk
                if with_den:
                    for j in range(nb):
                        S.op("dve", lambda g, j=j: g.reciprocal(out=stat[:, 8 + j:9 + j], in_=pviews[j][:, 128:129]),
                             reads=[rviews[j]], writes=[r_stat])
                for j in range(nb):
                    if with_den:
                        S.op("act", lambda g, j=j: g.activation(out=junk[:, 0:128], in_=pviews[j][:, 0:128], func=AF.Square,
                                                               scale=stat[:, 8 + j:9 + j], accum_out=stat[:, 12 + j:13 + j]),
                             reads=[rviews[j], r_stat], writes=[r_junk, r_stat])
                    else:
                        S.op("act", lambda g, j=j: g.activation(out=junk[:, 0:128], in_=pviews[j][:, 0:128], func=AF.Square,
                                                               accum_out=stat[:, 12 + j:13 + j]),
                             reads=[rviews[j]], writes=[r_junk, r_stat])
                S.op("act", lambda g: g.activation(out=stat[:, 16:16 + nb], in_=stat[:, 12:12 + nb], func=AF.Ln, scale=1.0 / HD, bias=epsb[:]),
                     reads=[r_stat], writes=[r_stat])
                S.op("act", lambda g: g.activation(out=stat[:, 20:20 + nb], in_=stat[:, 16:16 + nb], func=AF.Exp, scale=-0.5),
                     reads=[r_stat], writes=[r_stat])
                if with_den:
                    S.op("dve", lambda g: g.tensor_tensor(out=stat[:, 20:20 + nb], in0=stat[:, 20:20 + nb], in1=stat[:, 8:8 + nb], op=ALU.mult),
                         reads=[r_stat], writes=[r_stat])
                for j in range(nb):
                    S.op("dve", lambda g, j=j: g.tensor_scalar(out=obuf[:, j, :], in0=pviews[j][:, 0:128], scalar1=stat[:, 20 + j:21 + j],
                                                               scalar2=None, op0=ALU.mult),
                         reads=[rviews[j], r_stat], writes=[r_ob])
                for j in range(nb):
                    S.op("pe", lambda g, j=j: g.transpose(out=pTb[:, j, :], in_=obuf[:, j, :], identity=identb[:]),
                         reads=[r_ob, r_c], writes=[r_pT])
                S.op("act", lambda g: g.activation(out=oTh[:, qcol0:qcol0 + 128 * nb],
                                                   in_=pTb[:, 0:nb, :].rearrange("p c n -> p (c n)"), func=AF.Copy,
                                                   scale=ghead[:, l, h:h + 1]),
                     reads=[r_pT, r_par], writes=[r_oT])

            r_QTs = [Res("QT0"), Res("QT1")]
            r_KTs = [Res("KT0"), Res("KT1")]
            r_Vts = [Res("Vt0"), Res("Vt1")]
            ebufs = [ebuf, vw(BIGB, 24832, 8256, F32)]
            cumbufs = [cumbuf, vw(BIGB, 24832 + 8256, 8256, F32)]
            abufs = [abuf, vw(XT, 0, 4096, BF16)]
            r_ebufs = [r_ebuf, Res("ebuf2")]
            r_cums = [r_cum, Res("cum2")]
            r_abufs = [r_abuf, Res("abuf2")]
            r_statB = [Res("negtot0"), Res("negtot1")]
            r_statE = Res("statE")
            r_ab3 = Res("abuf3")
            r_aT2 = Res("aT2")

            def proj_gen(h):
                typ, hi, cq, ck, cv = head_info(h)
                QT, KT, Vt = QTs[h % 2], KTs[h % 2], Vts[h % 2]
                r_QT, r_KT, r_Vt = r_QTs[h % 2], r_KTs[h % 2], r_Vts[h % 2]
                wv_in = w_in_d[l].rearrange("(c p) n -> p c n", p=128)
                ws = []
                for col in (cq, ck, cv):
                    w_ap, r_w, d_w = wring.next()
                    S.dma("pool", lambda g, w_ap=w_ap, col=col: g.dma_start(out=w_ap, in_=wv_in[:, :, col:col + 128]), d_w, writes=[r_w])
                    ws.append((w_ap, r_w))
                yield
                for (w_ap, r_w), dst, r_dst in ((ws[0], QT, r_QT), (ws[1], KT, r_KT)):
                    for tg in range(4):
                        ps, r_ps = pX[:, :], r_pX
                        for c in range(16):
                            S.op("pe", lambda g, ps=ps, w_ap=w_ap, c=c, tg=tg: g.matmul(ps, lhsT=w_ap[:, c, :], rhs=hnT[:, c, tg * 512:(tg + 1) * 512],
                                                                                        start=(c == 0), stop=(c == 15)),
                                 reads=[r_w, r_hnT], writes=[r_ps])
                            if c % 4 == 3 and c != 15:
                                yield
                        evac(dst[:, tg * 512:(tg + 1) * 512], ps, [r_ps], [r_dst])
                        yield
                w_ap, r_w = ws[2]
                for tg in range(4):
                    ps, r_ps = pX[:, :], r_pX
                    for tb in range(4):
                        t0 = (tg * 4 + tb) * 128
                        for c in range(16):
                            S.op("pe", lambda g, ps=ps, w_ap=w_ap, c=c, tb=tb, t0=t0: g.matmul(ps[:, tb * 128:(tb + 1) * 128], lhsT=hnT[:, c, t0:t0 + 128],
                                                                                              rhs=w_ap[:, c, :], start=(c == 0), stop=(c == 15)),
                                 reads=[r_w, r_hnT], writes=[r_ps])
                        if tb != 3:
                            yield
                    evac(Vt[:, tg * 4:(tg + 1) * 4, 0:128], ps.rearrange("p (c n) -> p c n", n=128), [r_ps], [r_Vt])
                    yield
                if l == 0 and h in (0, 6, 11):
                    dbg_dump("QT%d" % h, QT, [r_QT])
                    dbg_dump("KT%d" % h, KT, [r_KT])
                    dbg_dump("Vt%d" % h, Vt, [r_Vt])

            LHs = [LH, ebuf[:, 0:2048]]
            r_LHs = [r_LH, Res("LHb")]
            lh_built = set()

            def lh_build_gen(hh):
                lh_built.add(hh)
                typ_, hi_, _, _, _ = head_info(hh)
                L_, r_L = LHs[hh % 2], r_LHs[hh % 2]
                extra = [r_ebufs[1], r_cums[1]] if hh % 2 == 0 else [r_ebufs[0]]
                if typ_ == "A":
                    Li = L_.bitcast(I32)
                    S.op("pool", lambda g: g.iota(Li, pattern=[[1, 2048]], base=0, channel_multiplier=-1), writes=[r_L] + extra)
                    yield
                    S.op("dve", lambda g: g.tensor_copy(out=cumbuf[:, 1:2049], in_=Li), reads=[r_L], writes=[r_cum])
                    yield
                    S.op("dve", lambda g, sl=slopes[hi_]: g.scalar_tensor_tensor(out=L_, in0=cumbuf[:, 1:2049], scalar=-sl / SCALE, in1=LM0,
                                                                                  op0=ALU.mult, op1=ALU.add),
                         reads=[r_cum, r_lm], writes=[r_L])
                    yield
                else:
                    dgs = [abuf[:, 256 * i:256 * (i + 1)].bitcast(F32) for i in range(8)]
                    for bg in range(4):
                        ps, r_ps = ps_next()
                        for b4 in range(4):
                            b = bg * 4 + b4
                            dg, rdg = dgs[b % 8], r_dgs[b % 8]
                            S.op("dve", lambda g, dg=dg, b=b: g.tensor_scalar(out=dg, in0=identf[:], scalar1=cn3[:, b, hi_:hi_ + 1], scalar2=None,
                                                                             op0=ALU.mult),
                                 reads=[r_c, r_cc], writes=[rdg, r_abuf])
                            S.op("pe", lambda g, dg=dg, ps=ps, b4=b4: g.matmul(ps[:, b4 * 128:(b4 + 1) * 128], lhsT=onesf[:], rhs=dg, start=True, stop=True),
                                 reads=[rdg, r_c], writes=[r_ps])
                        S.op("act", lambda g, ps=ps, bg=bg: g.activation(out=L_[:, bg * 512:(bg + 1) * 512], in_=ps, func=AF.Copy, scale=-1.0 / SCALE),
                             reads=[r_ps], writes=[r_L] + extra)
                        yield

            r_dgs = [Res("dg%d" % i) for i in range(8)]
            stage = vw(XT, 12288, 4 * 132 * 4, F32, "p (c n) -> p c n", n=132)
            r_stage = Res("stage")

            def epilogue_ac_1(l, h, qg):
                S.op("act", lambda g: g.activation(out=stage[:, :, 0:129], in_=pO[:, :, 0:129], func=AF.Copy), reads=r_pO, writes=[r_stage])
                S.op("dve", lambda g: g.reciprocal(out=stat[:, 8:12].unsqueeze(2), in_=stage[:, :, 128:129]), reads=[r_stage], writes=[r_stat])
                for j in range(4):
                    S.op("act", lambda g, j=j: g.activation(out=junk[:, 0:128], in_=stage[:, j, 0:128], func=AF.Square,
                                                           scale=stat[:, 8 + j:9 + j], accum_out=stat[:, 12 + j:13 + j]),
                         reads=[r_stage, r_stat], writes=[r_junk, r_stat])
                S.op("act", lambda g: g.activation(out=stat[:, 16:20], in_=stat[:, 12:16], func=AF.Ln, scale=1.0 / HD, bias=epsb[:]),
                     reads=[r_stat], writes=[r_stat])
                S.op("act", lambda g: g.activation(out=stat[:, 20:24], in_=stat[:, 16:20], func=AF.Exp, scale=-0.5), reads=[r_stat], writes=[r_stat])
                S.op("dve", lambda g: g.tensor_tensor(out=stat[:, 20:24], in0=stat[:, 20:24], in1=stat[:, 8:12], op=ALU.mult), reads=[r_stat], writes=[r_stat])
                S.op("dve", lambda g: g.tensor_tensor(out=obuf, in0=stage[:, :, 0:128], in1=stat[:, 20:24].unsqueeze(2).broadcast_to([128, 4, 128]), op=ALU.mult),
                     reads=[r_stage, r_stat], writes=[r_ob])

            def epilogue_ac_2(l, h, qg, oTh, r_oT):
                for j in range(4):
                    S.op("pe", lambda g, j=j: g.transpose(out=pTb[:, j, :], in_=obuf[:, j, :], identity=identb[:]), reads=[r_ob, r_c], writes=[r_pT])
                S.op("act", lambda g: g.activation(out=oTh[:, qg * 512:(qg + 1) * 512], in_=pTb.rearrange("p c n -> p (c n)"), func=AF.Copy,
                                                   scale=ghead[:, l, h:h + 1]),
                     reads=[r_pT, r_par], writes=[r_oT])

            cur_gen = [None]

            def pump():
                gnr = cur_gen[0]
                if gnr is None:
                    return
                try:
                    next(gnr)
                except StopIteration:
                    cur_gen[0] = None

            def drain_gen():
                while cur_gen[0] is not None:
                    pump()

            cur_gen[0] = proj_gen(0)
            drain_gen()
            for h in range(NH):
                typ, hi, cq, ck, cv = head_info(h)
                QT, KT, Vt = QTs[h % 2], KTs[h % 2], Vts[h % 2]
                r_QT, r_KT, r_Vt = r_QTs[h % 2], r_KTs[h % 2], r_Vts[h % 2]
                oTh = oThs[h % 2]
                r_oT = r_oTh[h % 2]
                if h + 1 < NH:
                    cur_gen[0] = proj_gen(h + 1)
                    pump()

                if typ in ("A", "C"):
                    if h not in lh_built:
                        for _ in lh_build_gen(h):
                            pass
                    LHc, r_LHc = LHs[h % 2], r_LHs[h % 2]
                    nxt_gen = None
                    if h + 1 < NH and head_info(h + 1)[0] == typ:
                        nxt_gen = lh_build_gen(h + 1)
                    steps = [(qg, m) for qg in range(4) for m in range(4 * qg + 4)]
                    state = {}

                    def do_s(i):
                        qg, m = steps[i]
                        c0 = max(0, m - 4 * qg) * 128
                        ps, r_ps = ps_next()
                        S.op("pe", lambda g: g.matmul(ps[:, c0:512], lhsT=KT[:, m * 128:(m + 1) * 128], rhs=QT[:, qg * 512 + c0:(qg + 1) * 512],
                                                      start=True, stop=True),
                             reads=[r_KT, r_QT], writes=[r_ps])
                        pt, r_pt, _ = pt_ring.next()
                        sf, r_sf, _ = sf_ring.next()
                        s0 = qg * 512 - m * 128 if typ == "A" else qg * 512
                        S.op("dve", lambda g: g.tensor_tensor(out=sf[:, c0:512], in0=ps[:, c0:512], in1=LHc[:, s0 + c0:s0 + 512], op=ALU.add),
                             reads=[r_ps, r_LHc], writes=[r_sf])
                        if typ == "A":
                            S.op("act", lambda g: g.activation(out=pt[:, c0:512], in_=sf[:, c0:512], func=AF.Exp, scale=SCALE),
                                 reads=[r_sf], writes=[r_pt])
                        else:
                            if m >= 4 * qg:
                                S.op("dve", lambda g: g.tensor_tensor(out=sf[:, c0:c0 + 128], in0=sf[:, c0:c0 + 128], in1=negLEf[:], op=ALU.add),
                                     reads=[r_sf, r_c], writes=[r_sf])
                            S.op("act", lambda g: g.activation(out=pt[:, c0:512], in_=sf[:, c0:512], func=AF.Exp, scale=SCALE,
                                                               bias=cn3[:, m, hi:hi + 1]),
                                 reads=[r_sf, r_cc], writes=[r_pt])
                        state[i] = (pt, r_pt, c0)

                    def do_av(i):
                        qg, m = steps[i]
                        pt, r_pt, c0 = state.pop(i)
                        for j in range(c0 // 128, 4):
                            S.op("pe", lambda g, j=j: g.matmul(pO[:, j, 0:129], lhsT=pt[:, j * 128:(j + 1) * 128], rhs=Vt[:, m, 0:129],
                                                               start=(m == 0), stop=(m == 4 * qg + j)),
                                 reads=[r_pt, r_Vt], writes=[r_pO[j]])
                        if m == 4 * qg + 3:
                            epilogue_ac_1(l, h, qg)
                            deferred.append((i + 3, qg))
                            pump()

                    deferred = []
                    do_s(0)
                    for i in range(len(steps)):
                        if i + 1 < len(steps):
                            do_s(i + 1)
                        pump()
                        do_av(i)
                        while deferred and deferred[0][0] <= i:
                            epilogue_ac_2(l, h, deferred.pop(0)[1], oTh, r_oT)
                        if nxt_gen is not None and i >= 6 and i % 4 == 2:
                            try:
                                next(nxt_gen)
                            except StopIteration:
                                nxt_gen = None
                    while deferred:
                        epilogue_ac_2(l, h, deferred.pop(0)[1], oTh, r_oT)
                    if nxt_gen is not None:
                        for _ in nxt_gen:
                            pass
                else:
                    paT = pO[:, 0:2, :].rearrange("p a b -> p (a b)").bitcast(BF16).rearrange("p (c n) -> p c n", n=128)
                    if hi == 0:
                        S.op("dve", lambda g: g.memset(cumbufs[1][:, 0:1], 0.0), reads=[r_LH, r_lm], writes=[r_cums[1], r_LH, r_LHs[1], r_lm])

                    abufs3 = [abuf, vw(XT, 0, 4096, BF16), vw(XT, 4096, 4096, BF16)]
                    r_abufs3 = [r_abuf, r_abufs[1], r_ab3]
                    aTbs = [aTb, vw(XT, 8192, 4096, BF16, "p (c n) -> p c n", n=128)]
                    r_aTs = [r_aT, r_aT2]

                    r_ebc = [[Res("eb%d_%d" % (k, c)) for c in range(4)] for k in range(2)]

                    def front(n):
                        eb, cb_, r_cb = ebufs[n % 2], cumbufs[n % 2], r_cums[n % 2]
                        Nk = 128 * (n + 1)
                        nch = (Nk + 511) // 512
                        S.op("dve", lambda g: g.memset(cb_[:, Nk:Nk + 1], 1.0), writes=[r_cb])
                        for ch in range(nch - 1, -1, -1):
                            k0 = ch * 512
                            kw = min(512, Nk - k0)
                            r_e = r_ebc[n % 2][ch]
                            ps, r_ps = ps_next()
                            S.op("pe", lambda g, ps=ps, k0=k0, kw=kw: g.matmul(ps[:, 0:kw], lhsT=QT[:, n * 128:(n + 1) * 128], rhs=KT[:, k0:k0 + kw],
                                                                               start=True, stop=True),
                                 reads=[r_QT, r_KT], writes=[r_ps])
                            S.op("act", lambda g, ps=ps, k0=k0, kw=kw: g.activation(out=eb[:, k0:k0 + kw], in_=ps[:, 0:kw], func=AF.Sigmoid, scale=-SCALE),
                                 reads=[r_ps], writes=[r_e])
                            if ch == nch - 1:
                                S.op("dve", lambda g: g.tensor_tensor(out=eb[:, n * 128:(n + 1) * 128], in0=eb[:, n * 128:(n + 1) * 128],
                                                                      in1=triLEf[:], op=ALU.max),
                                     reads=[r_e, r_c], writes=[r_e])
                            init = 1.0 if ch == nch - 1 else cb_[:, k0 + kw:k0 + kw + 1]
                            S.op("dve", lambda g, k0=k0, kw=kw, init=init: g.tensor_tensor_scan(out=cb_[:, k0:k0 + kw][:, ::-1], data0=eb[:, k0:k0 + kw][:, ::-1],
                                                                                                 data1=eb[:, k0:k0 + kw][:, ::-1], initial=init,
                                                                                                 op0=ALU.mult, op1=ALU.min),
                                 reads=[r_e, r_cb], writes=[r_cb])

                    def midB(n):
                        cb_, r_cb, ab, r_ab = cumbufs[n % 2], r_cums[n % 2], abufs3[n % 3], r_abufs3[n % 3]
                        Nk = 128 * (n + 1)
                        S.op("dve", lambda g: g.tensor_tensor(out=ab[:, 0:Nk], in0=cb_[:, 1:Nk + 1], in1=cb_[:, 0:Nk], op=ALU.subtract),
                             reads=[r_cb], writes=[r_ab])

                    def tailA(n):
                        ab, r_ab = abufs3[n % 3], r_abufs3[n % 3]
                        for m in range(n + 1):
                            S.op("pe", lambda g, m=m: g.transpose(out=paT[:, m, :], in_=ab[:, m * 128:(m + 1) * 128], identity=identb[:]),
                                 reads=[r_ab, r_c], writes=[r_pO[0], r_pO[1]])
                        evac(aTbs[n % 2][:, 0:n + 1, :], paT[:, 0:n + 1, :], [r_pO[0], r_pO[1]], [r_aTs[n % 2]])

                    def tailB(n):
                        ko = 2 + n % 2
                        aT_ = aTbs[n % 2]
                        for m in range(n + 1):
                            S.op("pe", lambda g, m=m: g.matmul(pO[:, ko, 0:128], lhsT=aT_[:, m, :], rhs=Vt[:, m, 0:128], start=(m == 0), stop=(m == n)),
                                 reads=[r_aTs[n % 2], r_Vt], writes=[r_pO[ko]])

                    def epiA(n):
                        ko = 2 + n % 2
                        pv = pO[:, ko, :]
                        S.op("act", lambda g: g.activation(out=junk[:, 0:128], in_=pv[:, 0:128], func=AF.Square, accum_out=stat[:, 32:33]),
                             reads=[r_pO[ko]], writes=[r_junk, r_statE])
                        S.op("dve", lambda g: g.tensor_scalar(out=stat[:, 33:34], in0=stat[:, 32:33], scalar1=1.0 / HD, scalar2=EPS, op0=ALU.mult, op1=ALU.add),
                             reads=[r_statE], writes=[r_statE])
                        S.op("pool", lambda g: g.tensor_tensor(out=stat[:, 34:35], in0=stat[:, 33:34], in1=mhalf[:], op=ALU.pow),
                             reads=[r_statE, r_c], writes=[r_statE])
                        S.op("dve", lambda g: g.tensor_scalar(out=obuf[:, n % 4, :], in0=pv[:, 0:128], scalar1=stat[:, 34:35], scalar2=None, op0=ALU.mult),
                             reads=[r_pO[ko], r_statE], writes=[r_ob])

                    def epiB(n):
                        S.op("pe", lambda g: g.transpose(out=pTb[:, n % 4, :], in_=obuf[:, n % 4, :], identity=identb[:]),
                             reads=[r_ob, r_c], writes=[r_pT])
                        S.op("act", lambda g: g.activation(out=oTh[:, n * 128:(n + 1) * 128], in_=pTb[:, n % 4, :], func=AF.Copy, scale=ghead[:, l, h:h + 1]),
                             reads=[r_pT, r_par], writes=[r_oT])

                    for i in range(NT + 6):
                        pump()
                        pump()
                        if i < NT:
                            front(i)
                        if 0 <= i - 1 < NT:
                            midB(i - 1)
                        if 0 <= i - 3 < NT:
                            tailA(i - 3)
                        pump()
                        if 0 <= i - 4 < NT:
                            tailB(i - 4)
                        if 0 <= i - 5 < NT:
                            epiA(i - 5)
                        if 0 <= i - 6 < NT:
                            epiB(i - 6)
                drain_gen()
                S.dma("sp", lambda g, h=h, oTh=oTh: g.dma_start(out=oT_d[h], in_=oTh), d_oT[h % 2], reads=[r_oT], writes=[r_oTd[h]])
            S.barrier()

            wgu_slots = [vw(WGU, 8192 * i, 8192, BF16, "p (a c n) -> p a c n", a=2, n=128) for i in range(3)]
            wgu_ring = Ring(S, wgu_slots, with_dsem=True)
            wgu_v = w_gu_d[l].rearrange("(c p) n -> p c n", p=128)
            wgu_ld = {}

            def issue_wgu(tt, j):
                wgu, r_wgu, d_wgu = wgu_ring.next()
                S.dma("pool", lambda g: g.dma_start(out=wgu[:, 0, :, :], in_=wgu_v[:, :, j * 128:(j + 1) * 128]), d_wgu, writes=[r_wgu])
                S.dma("pool", lambda g: g.dma_start(out=wgu[:, 1, :, :], in_=wgu_v[:, :, DFF + j * 128:DFF + (j + 1) * 128]), d_wgu, writes=[r_wgu])
                wgu_ld[(tt, j)] = (wgu, r_wgu)

            oT = hnT
            r_oTs = Res("oT")
            d_oTl = S.dsem()
            for h in range(NH):
                S.dma("sp", lambda g, h=h: g.dma_start(out=oT[:, h, :], in_=oT_d[h]), d_oTl, reads=[r_oTd[h]], writes=[r_oTs])
            if l == 0:
                dbg_dump("oT", oT, [r_oTs])
            wo_slots = [vw(BIGB, 16384 * i, 16384, BF16, "p (c n) -> p c n", n=512) for i in range(2)]
            wo_ring = Ring(S, wo_slots, with_dsem=True)
            xp_ring = Ring(S, [vw(BIGB, 32768 + 2048 * i, 2048, F32) for i in range(4)], with_dsem=True)
            d_st = [S.dsem() for _ in range(4)]
            acc_i = 0
            wo_v = w_o_d[l].rearrange("(c p) n -> p c n", p=128)
            for cc in range(4):
                wo, r_wo, d_wo = wo_ring.next()
                S.dma("pool", lambda g, wo=wo, cc=cc: g.dma_start(out=wo, in_=wo_v[:, :, cc * 512:(cc + 1) * 512]), d_wo, writes=[r_wo])
                if cc == 1:
                    for jp in range(3):
                        issue_wgu(0, jp)
                for tb in range(NT):
                    k = acc_i % 6
                    acc_i += 1
                    if k < 2:
                        ps, r_ps = pS[:, k, :], r_pS[k]
                    else:
                        ps, r_ps = pO[:, k - 2, :], r_pO[k - 2]
                    for h in range(NH):
                        S.op("pe", lambda g, ps=ps, wo=wo, h=h, tb=tb: g.matmul(ps, lhsT=oT[:, h, tb * 128:(tb + 1) * 128], rhs=wo[:, h, :],
                                                                               start=(h == 0), stop=(h == NH - 1)),
                             reads=[r_oTs, r_wo], writes=[r_ps])
                    xp, r_xp, d_xp = xp_ring.next()
                    S.dma("sp", lambda g, xp=xp, tb=tb, cc=cc: g.dma_start(out=xp, in_=src_d[tb * 128:(tb + 1) * 128, cc * 512:(cc + 1) * 512]),
                          d_xp, reads=[r_x[tb]], writes=[r_xp])
                    S.op("dve", lambda g, xp=xp, ps=ps: g.tensor_tensor(out=xp, in0=ps, in1=xp, op=ALU.add), reads=[r_ps, r_xp], writes=[r_xp])
                    S.dma("sp", lambda g, xp=xp, tb=tb, cc=cc: g.dma_start(out=xres_d[tb * 128:(tb + 1) * 128, cc * 512:(cc + 1) * 512], in_=xp),
                          d_st[acc_i % 4], reads=[r_xp], writes=[r_x[tb]])
            S.barrier()
            if l == 0 and "x1" in dbg_out:
                dd = S.dsem()
                for tb in range(NT):
                    xt, r_xt, d_xt = xt_slots[tb % 2]
                    S.dma("sp", lambda g, xt=xt, tb=tb: g.dma_start(out=xt, in_=xres_d[tb * 128:(tb + 1) * 128, :]), d_xt, reads=[r_x[tb]], writes=[r_xt])
                    S.dma("sp", lambda g, xt=xt, tb=tb: g.dma_start(out=dbg_out["x1"][tb * 128:(tb + 1) * 128, :], in_=xt), dd, reads=[r_xt])
                S.barrier()

            hn2T = vw(BIGA, 0, 32768, BF16, "p (c n) -> p c n", n=1024)
            r_hn2 = Res("hn2T")
            wd_slots = [vw(BIGA, 32768 + 11264 * i, 11264, BF16, "p (c n) -> p c n", n=128) for i in range(2)]
            wd_ring = Ring(S, wd_slots, with_dsem=True)
            ots_ring = Ring(S, [vw(BIGA, 32768 + 22528 + 2048 * i, 2048, F32) for i in range(2)])
            actT = vw(BIGB, 0, 90112, BF16, "p (c n) -> p c n", n=1024)
            r_act = Res("actT")
            cv_ring = Ring(S, [vw(XT, 6144 * i, 6144, F32, "p (a n) -> p a n", a=3) for i in range(2)])
            xq_ring = Ring(S, [vw(XT, 2048 * i, 2048, F32, "p (a n) -> p a n", a=4) for i in range(6)], with_dsem=True)
            d_xst = [S.dsem() for _ in range(6)]
            r_carry = Res("carry")
            cw3 = cw[:, l, :].rearrange("p (j k) -> p j k", k=3)
            wd_v = w_down_d[l].rearrange("(j p) n -> p j n", p=128)
            wd_ld = {}

            def issue_wd(tt, dc):
                wd, r_wd, d_wd = wd_ring.next()
                S.dma("pool", lambda g: g.dma_start(out=wd, in_=wd_v[:, :, dc * 128:(dc + 1) * 128]), d_wd, writes=[r_wd])
                wd_ld[(tt, dc)] = (wd, r_wd)
            pgu_i = 0
            S.op("dve", lambda g: g.memset(carry[:], 0.0), writes=[r_carry])
            for tt in range(2):
                for i in range(8):
                    tb = tt * 8 + i
                    norm_transpose(xres_d[tb * 128:(tb + 1) * 128, :], r_x[tb], i, gffn[:, l, :], hn2T, r_hn2, i * 128)
                S.barrier()
                for j in range(NF):
                    if (tt, j) not in wgu_ld:
                        issue_wgu(tt, j)
                    wgu, r_wgu = wgu_ld.pop((tt, j))
                    if j == 6:
                        issue_wd(tt, 0)
                        issue_wd(tt, 1)
                    for half in range(2):
                        kk = (pgu_i % 2) * 2
                        pgu_i += 1
                        pv2 = [pO[:, kk, :], pO[:, kk + 1, :]]
                        rv2 = [r_pO[kk], r_pO[kk + 1]]
                        for a in range(2):
                            for c in range(16):
                                S.op("pe", lambda g, a=a, c=c, wgu=wgu, pv2=pv2, half=half: g.matmul(pv2[a], lhsT=wgu[:, a, c, :],
                                                                                                    rhs=hn2T[:, c, half * 512:(half + 1) * 512],
                                                                                                    start=(c == 0), stop=(c == 15)),
                                     reads=[r_wgu, r_hn2], writes=[rv2[a]])
                        cvb, r_cv, _ = cv_ring.next()
                        first = (tt == 0 and half == 0)
                        for a in range(2):
                            jj = a * NF + j
                            ph = pv2[a]
                            acc = cvb[:, a, :]
                            S.op("act", lambda g, acc=acc, ph=ph, jj=jj: g.activation(out=acc, in_=ph, func=AF.Identity, scale=cw3[:, jj, 2:3],
                                                                                    bias=cb[:, l, jj:jj + 1]),
                                 reads=[rv2[a], r_par], writes=[r_cv])
                            S.op("dve", lambda g, acc=acc, ph=ph, jj=jj: g.scalar_tensor_tensor(out=acc[:, 1:512], in0=ph[:, 0:511], scalar=cw3[:, jj, 1:2],
                                                                                               in1=acc[:, 1:512], op0=ALU.mult, op1=ALU.add),
                                 reads=[rv2[a], r_par, r_cv], writes=[r_cv])
                            S.op("dve", lambda g, acc=acc, ph=ph, jj=jj: g.scalar_tensor_tensor(out=acc[:, 2:512], in0=ph[:, 0:510], scalar=cw3[:, jj, 0:1],
                                                                                               in1=acc[:, 2:512], op0=ALU.mult, op1=ALU.add),
                                 reads=[rv2[a], r_par, r_cv], writes=[r_cv])
                            if not first:
                                cr = carry[:, 2 * jj:2 * jj + 2]
                                S.op("dve", lambda g, acc=acc, cr=cr, jj=jj: g.scalar_tensor_tensor(out=acc[:, 0:2], in0=cr, scalar=cw3[:, jj, 0:1],
                                                                                                   in1=acc[:, 0:2], op0=ALU.mult, op1=ALU.add),
                                     reads=[r_carry, r_par, r_cv], writes=[r_cv])
                                S.op("dve", lambda g, acc=acc, cr=cr, jj=jj: g.scalar_tensor_tensor(out=acc[:, 0:1], in0=cr[:, 1:2], scalar=cw3[:, jj, 1:2],
                                                                                                   in1=acc[:, 0:1], op0=ALU.mult, op1=ALU.add),
                                     reads=[r_carry, r_par, r_cv], writes=[r_cv])
                            S.op("dve", lambda g, ph=ph, jj=jj: g.tensor_copy(out=carry[:, 2 * jj:2 * jj + 2], in_=ph[:, 510:512]),
                                 reads=[rv2[a], r_cv], writes=[r_carry])
                        S.op("act", lambda g, cvb=cvb: g.activation(out=cvb[:, 2, :], in_=cvb[:, 0, :], func=AF.Silu), reads=[r_cv], writes=[r_cv])
                        S.op("dve", lambda g, cvb=cvb, j=j, half=half: g.tensor_tensor(out=actT[:, j, half * 512:(half + 1) * 512], in0=cvb[:, 2, :],
                                                                                      in1=cvb[:, 1, :], op=ALU.mult),
                             reads=[r_cv], writes=[r_act])
                if l == 0 and tt == 0:
                    dbg_dump("actT", actT, [r_act])
                S.barrier()
                po_i = 0
                ptk = 0
                for dc in range(16):
                    if (tt, dc) not in wd_ld:
                        issue_wd(tt, dc)
                    wd, r_wd = wd_ld.pop((tt, dc))
                    if tt == 0 and dc == 8:
                        for jp in range(3):
                            issue_wgu(1, jp)
                    for half in range(2):
                        k = po_i % 2
                        po_i += 1
                        po, r_po = pS[:, k, :], r_pS[k]
                        for j in range(NF):
                            S.op("pe", lambda g, po=po, wd=wd, j=j, half=half: g.matmul(po, lhsT=wd[:, j, :], rhs=actT[:, j, half * 512:(half + 1) * 512],
                                                                                       start=(j == 0), stop=(j == NF - 1)),
                                 reads=[r_wd, r_act], writes=[r_po])
                        ots, r_ots, _ = ots_ring.next()
                        S.op("act", lambda g, ots=ots, po=po: g.activation(out=ots, in_=po, func=AF.Copy), reads=[r_po], writes=[r_ots])
                        ptv, r_ptv = ((pX, r_pX), (pT, r_pT))[ptk % 2]
                        ptk += 1
                        for b in range(4):
                            S.op("pe", lambda g, ptv=ptv, ots=ots, b=b: g.transpose(out=ptv[:, b * 128:(b + 1) * 128], in_=ots[:, b * 128:(b + 1) * 128],
                                                                                     identity=identf[:]),
                                 reads=[r_ots, r_c], writes=[r_ptv])
                        xq, r_xq, d_xq = xq_ring.next()
                        t0 = tt * 1024 + half * 512
                        rows = [r_x[(t0 // 128) + b] for b in range(4)]
                        S.dma("sp", lambda g, xq=xq, t0=t0, dc=dc: g.dma_start(
                            out=xq, in_=xres_d[t0:t0 + 512, dc * 128:(dc + 1) * 128].rearrange("(b p) n -> p b n", p=128)),
                            d_xq, reads=rows, writes=[r_xq])
                        S.op("dve", lambda g, xq=xq, ptv=ptv: g.tensor_tensor(out=xq, in0=ptv.rearrange("p (b n) -> p b n", n=128), in1=xq, op=ALU.add),
                             reads=[r_ptv, r_xq], writes=[r_xq])
                        S.dma("sp", lambda g, xq=xq, t0=t0, dc=dc: g.dma_start(
                            out=xres_d[t0:t0 + 512, dc * 128:(dc + 1) * 128].rearrange("(b p) n -> p b n", p=128), in_=xq),
                            d_xst[(po_i - 1) % 6], reads=[r_xq], writes=rows)
                S.barrier()
            if l == 0 and "x2" in dbg_out:
                dd = S.dsem()
                for tb in range(NT):
                    xt, r_xt, d_xt = xt_slots[tb % 2]
                    S.dma("sp", lambda g, xt=xt, tb=tb: g.dma_start(out=xt, in_=xres_d[tb * 128:(tb + 1) * 128, :]), d_xt, reads=[r_x[tb]], writes=[r_xt])
                    S.dma("sp", lambda g, xt=xt, tb=tb: g.dma_start(out=dbg_out["x2"][tb * 128:(tb + 1) * 128, :], in_=xt), dd, reads=[r_xt])
                S.barrier()

        gfb = vw(BIGA, 0, 8192, F32)
        r_gfb = Res("gfb")
        d_g = S.dsem()
        S.dma("sp", lambda g: g.dma_start(out=gfb, in_=gfin_d), d_g, writes=[r_gfb])
        yo_ring = Ring(S, [vw(BIGB, 8192 * i, 8192, F32) for i in range(2)], with_dsem=True)
        src_fin = xres_d if n_layers > 0 else x_d
        for i in range(NT):
            xt, r_xt, d_xt = xt_slots[i % 2]
            S.dma("sp", lambda g, xt=xt, i=i: g.dma_start(out=xt, in_=src_fin[i * 128:(i + 1) * 128, :]), d_xt, reads=[r_x[i]], writes=[r_xt])
            S.op("act", lambda g, xt=xt: g.activation(out=xs_b, in_=xt, func=AF.Square, accum_out=stat[:, 0:1]), reads=[r_xt], writes=[r_xs, r_stat])
            S.op("act", lambda g: g.activation(out=stat[:, 1:2], in_=stat[:, 0:1], func=AF.Ln, scale=1.0 / D, bias=epsb[:]), reads=[r_stat], writes=[r_stat])
            S.op("act", lambda g: g.activation(out=stat[:, 2:3], in_=stat[:, 1:2], func=AF.Exp, scale=-0.5), reads=[r_stat], writes=[r_stat])
            yo, r_yo, d_yo = yo_ring.next()
            S.op("dve", lambda g, xt=xt, yo=yo: g.scalar_tensor_tensor(out=yo, in0=xt, scalar=stat[:, 2:3], in1=gfb, op0=ALU.mult, op1=ALU.mult),
                 reads=[r_xt, r_stat, r_gfb], writes=[r_yo])
            S.dma("sp", lambda g, yo=yo, i=i: g.dma_start(out=y_d[i * 128:(i + 1) * 128, :], in_=yo), d_yo, reads=[r_yo])
        S.emit(nc)
    return nc


_NC_CACHE = {}


def _host_params(g_mix, b_f, g_head, g_ffn, conv_w, conv_b, g_final):
    f = np.float32

    def colT(v):
        return np.ascontiguousarray(np.asarray(v, f).reshape(DEPTH, 16, 128).transpose(0, 2, 1))

    gheadT = np.ascontiguousarray(np.asarray(g_head, f).transpose(0, 2, 1))
    bfb = np.ascontiguousarray(np.broadcast_to(np.asarray(b_f, f)[:, None, :], (DEPTH, 128, HC)))
    cwT = np.ascontiguousarray(np.asarray(conv_w, f).reshape(DEPTH, 3, 88, 128).transpose(0, 3, 2, 1)).reshape(DEPTH, 128, 88 * 3)
    cbT = np.ascontiguousarray(np.asarray(conv_b, f).reshape(DEPTH, 88, 128).transpose(0, 2, 1))
    gfinb = np.ascontiguousarray(np.broadcast_to(np.asarray(g_final, f)[None, :], (128, D)))
    return dict(gmixT=colT(g_mix), gffnT=colT(g_ffn), gheadT=gheadT, bfb=bfb, cwT=cwT, cbT=cbT, gfinb=gfinb)


def kernel(x, g_mix, w_in, b_f, g_head, w_o, g_ffn, w_gu, conv_w, conv_b, w_down, g_final):
    if "nc" not in _NC_CACHE:
        _NC_CACHE["nc"] = build_nc()
    nc = _NC_CACHE["nc"]
    par = _host_params(g_mix, b_f, g_head, g_ffn, conv_w, conv_b, g_final)
    shared = dict(w_in=np.ascontiguousarray(w_in, np.float32), w_o=np.ascontiguousarray(w_o, np.float32),
                  w_gu=np.ascontiguousarray(w_gu, np.float32), w_down=np.ascontiguousarray(w_down, np.float32), **par)
    x = np.asarray(x, np.float32)
    in_maps = [dict(x=np.ascontiguousarray(x[b]), **shared) for b in range(8)]
    res = run_bass_kernel_spmd(nc, in_maps, core_ids=list(range(8)))
    return np.stack([res.results[b]["y"] for b in range(8)], axis=0)
```

```python
import math
import contextlib
import numpy as np
import concourse.bass as bass
import concourse.mybir as mybir
from concourse.bass_utils import run_bass_kernel_spmd

F32 = mybir.dt.float32
BF16 = mybir.dt.bfloat16
I32 = mybir.dt.int32
AF = mybir.ActivationFunctionType
ALU = mybir.AluOpType

ENGS = ("pe", "act", "dve", "pool", "sp")

T = 2048
D = 2048
NT = 16
HD = 128
NH = 16
HA, HB, HC = 6, 5, 5
DFF = 5632
NF = 44
N_IN = 6149
EPS = 1e-6
SCALE = 1.0 / math.sqrt(HD)
DEPTH = 2


class Res:
    __slots__ = ("name", "w", "readers")

    def __init__(self, name=""):
        self.name = name
        self.w = None
        self.readers = []


class DSem:
    __slots__ = ("idx", "val", "handle", "res")

    def __init__(self, idx):
        self.idx = idx
        self.val = 0
        self.handle = None
        self.res = Res("dsem%d" % idx)


class Op:
    __slots__ = ("eng", "call", "deps", "need_inc", "seq", "is_dma", "dsem", "dval")


class _Rec:
    def __init__(self):
        self.call = None

    def __getattr__(self, name):
        def f(*a, **k):
            assert self.call is None
            self.call = (name, a, k)
            return None
        return f


class Sched:
    def __init__(self):
        self.ops = {e: [] for e in ENGS}
        self.dsems = []
        self.bar_res = {e: Res("bar_" + e) for e in ENGS}

    def dsem(self):
        d = DSem(len(self.dsems))
        self.dsems.append(d)
        return d

    def _add(self, eng, fn, reads, writes, is_dma, dsem):
        op = Op()
        op.eng = eng
        rec = _Rec()
        fn(rec)
        op.call = rec.call
        op.is_dma = is_dma
        op.need_inc = False
        op.seq = None
        op.dsem = dsem
        deps = {}
        for t in reads:
            w = t.w
            if w is not None:
                deps[id(w)] = w
        for t in writes:
            w = t.w
            if w is not None and (w.is_dma or w.eng != eng):
                deps[id(w)] = w
            for r in t.readers:
                if r.is_dma or r.eng != eng:
                    deps[id(r)] = r
        op.deps = list(deps.values())
        if is_dma:
            dsem.val += 16
            op.dval = dsem.val
        else:
            op.dval = None
        for t in writes:
            t.w = op
            t.readers = []
        for t in reads:
            rs = [r for r in t.readers if r.is_dma or r.eng != eng]
            rs.append(op)
            t.readers = rs
        self.ops[eng].append(op)
        return op

    def op(self, eng, fn, reads=(), writes=()):
        return self._add(eng, fn, reads, writes, False, None)

    def dma(self, eng, fn, dsem, reads=(), writes=()):
        return self._add(eng, fn, reads, list(writes) + [dsem.res], True, dsem)

    def barrier(self):
        lasts = []
        for e in ENGS:
            for op in reversed(self.ops[e]):
                if not op.is_dma:
                    lasts.append(op)
                    break
        for d in self.dsems:
            if d.res.w is not None:
                lasts.append(d.res.w)
        for e in ENGS:
            op = self.op(e, lambda g: g.nop())
            op.deps = [o for o in lasts if o.is_dma or o.eng != e]

    def emit(self, nc):
        for e in ENGS:
            for op in self.ops[e]:
                for d in op.deps:
                    if not d.is_dma:
                        d.need_inc = True
        for e in ENGS:
            c = 0
            for op in self.ops[e]:
                if (not op.is_dma) and op.need_inc:
                    c += 1
                    op.seq = c
        with contextlib.ExitStack() as st:
            esem = {e: st.enter_context(nc.semaphore("s_" + e)) for e in ENGS}
            for d in self.dsems:
                d.handle = st.enter_context(nc.semaphore("d%d" % d.idx))
            block = st.enter_context(nc.Block())
            hmap = {"pe": "tensor", "act": "scalar", "dve": "vector", "pool": "gpsimd", "sp": "sync"}

            def make(e):
                ops = self.ops[e]

                def body(eng):
                    waited = {}
                    for op in ops:
                        need = {}
                        for d in op.deps:
                            if d.is_dma:
                                k = ("d", d.dsem.idx)
                                h = d.dsem.handle
                                v = d.dval
                            else:
                                k = ("e", d.eng)
                                h = esem[d.eng]
                                v = d.seq
                            if waited.get(k, 0) >= v:
                                continue
                            if k not in need or need[k][1] < v:
                                need[k] = (h, v)
                        for k, (h, v) in need.items():
                            eng.wait_ge(h, v)
                            waited[k] = v
                        name, a, k = op.call
                        inst = getattr(eng, name)(*a, **k)
                        if op.is_dma:
                            inst.then_inc(op.dsem.handle, 16)
                        elif op.need_inc:
                            inst.then_inc(esem[e], 1)
                    if e == "sp":
                        for d in self.dsems:
                            if d.val > 0:
                                eng.wait_ge(d.handle, d.val)
                return body

            for e in ENGS:
                getattr(block, hmap[e])(make(e))


class Ring:
    def __init__(self, S, aps, with_dsem=False):
        self.slots = [(ap, Res(), S.dsem() if with_dsem else None) for ap in aps]
        self.i = 0

    def next(self):
        s = self.slots[self.i % len(self.slots)]
        self.i += 1
        return s


def head_info(h):
    if h < HA:
        base, n, i, typ = 0, HA, h, "A"
    elif h < HA + HB:
        base, n, i, typ = 3 * HA * HD, HB, h - HA, "B"
    else:
        base, n, i, typ = 3 * HA * HD + 3 * HB * HD, HC, h - HA - HB, "C"
    return typ, i, base + i * HD, base + n * HD + i * HD, base + 2 * n * HD + i * HD


def build_nc(dbg=None, n_layers=DEPTH):
    dbg = dbg or {}
    nc = bass.Bass("TRN2", target_bir_lowering=False)

    def din(name, shape, dt=F32):
        return nc.dram_tensor(name, list(shape), dt, kind="ExternalInput").ap()

    x_d = din("x", [T, D])
    w_in_d = din("w_in", [DEPTH, D, N_IN])
    w_o_d = din("w_o", [DEPTH, D, D])
    w_gu_d = din("w_gu", [DEPTH, D, 2 * DFF])
    w_down_d = din("w_down", [DEPTH, DFF, D])
    gmix_d = din("gmixT", [DEPTH, 128, 16])
    gffn_d = din("gffnT", [DEPTH, 128, 16])
    ghead_d = din("gheadT", [DEPTH, 128, 16])
    bf_d = din("bfb", [DEPTH, 128, HC])
    cw_d = din("cwT", [DEPTH, 128, 88 * 3])
    cb_d = din("cbT", [DEPTH, 128, 88])
    gfin_d = din("gfinb", [128, D])
    y_d = nc.dram_tensor("y", [T, D], F32, kind="ExternalOutput").ap()
    xres_d = nc.dram_tensor("xres", [T, D], F32).ap()
    oT_d = nc.dram_tensor("oTd", [NH, 128, T], BF16).ap()
    dbg_out = {}
    for k, (shape, dt) in dbg.items():
        dbg_out[k] = nc.dram_tensor("dbg_" + k, list(shape), dt, kind="ExternalOutput").ap()

    S = Sched()
    st = contextlib.ExitStack()
    with st:
        def sb(name, free, dt=F32):
            return st.enter_context(nc.sbuf_tensor(name, [128] + list(free), dt))

        BIGA = sb("BIGA", [16384], F32)
        BIGB = sb("BIGB", [22528], F32)
        XT = sb("XT", [4096], F32)
        WGU = sb("WGU", [6144], F32)
        identb = sb("identb", [128], BF16)
        identf = sb("identf", [128], F32)
        maskLEb = sb("maskLEb", [128], BF16)
        triLEf = sb("triLEf", [128], F32)
        maskLTf = sb("maskLTf", [128], F32)
        onesf = sb("onesf", [128], F32)
        negLEf = sb("negLEf", [128], F32)
        gmix = sb("gmix", [DEPTH, 16], F32)
        gffn = sb("gffn", [DEPTH, 16], F32)
        ghead = sb("ghead", [DEPTH, 16], F32)
        bfb = sb("bfb_s", [DEPTH, HC], F32)
        cw = sb("cw", [DEPTH, 88 * 3], F32)
        cb = sb("cb", [DEPTH, 88], F32)
        carry = sb("carry", [88 * 2], F32)
        stat = sb("stat", [64], F32)
        cneg = sb("cneg", [16 * HC], F32)
        ccar = sb("ccar", [17 * HC], F32)
        biasC = sb("biasC", [4 * 16 * HC], F32)
        lf = sb("lf", [16 * HC], F32)
        epsb = sb("epsb", [1], F32)
        oneb = sb("oneb", [1], F32)
        mhalf = sb("mhalf", [1], F32)

        pS = st.enter_context(nc.psum_tensor("pS", [128, 2, 512], F32))
        pO = st.enter_context(nc.psum_tensor("pO", [128, 4, 512], F32))
        pX = st.enter_context(nc.psum_tensor("pX", [128, 512], F32))
        pT = st.enter_context(nc.psum_tensor("pT", [128, 512], F32))
        r_pS = [Res("pS0"), Res("pS1")]
        r_pO = [Res("pO%d" % i) for i in range(4)]
        r_pX = Res("pX")
        r_pT = Res("pT")

        def vw(region, off_b, nbytes, dt, pat=None, **kw):
            a = region[:, off_b // 4:(off_b + nbytes) // 4]
            if dt != F32:
                a = a.bitcast(dt)
            if pat:
                a = a.rearrange(pat, **kw)
            return a

        dcnt = [0]

        def dbg_dump(key, ap_sb, res, dram_slice=None):
            if key not in dbg_out:
                return
            d = S.dsem()
            tgt = dbg_out[key] if dram_slice is None else dram_slice(dbg_out[key])
            S.dma("sp", lambda g: g.dma_start(out=tgt, in_=ap_sb), d, reads=res)

        r_c = Res("consts")
        r_par = Res("params")
        dpar = S.dsem()
        for (dst, src) in ((gmix, gmix_d), (gffn, gffn_d), (ghead, ghead_d), (bfb, bf_d), (cw, cw_d), (cb, cb_d)):
            for l in range(DEPTH):
                S.dma("sp", lambda g, dst=dst, src=src, l=l: g.dma_start(out=dst[:, l, :], in_=src[l]), dpar,
                      writes=[r_par])
        tmpi = vw(XT, 0, 512, I32)
        tmpf = vw(XT, 512, 512, F32)
        S.op("pool", lambda g: g.iota(tmpi, pattern=[[1, 128]], base=0, channel_multiplier=-1), writes=[r_c])
        S.op("dve", lambda g: g.tensor_copy(out=tmpf, in_=tmpi), reads=[r_c], writes=[r_c])
        S.op("dve", lambda g: g.tensor_single_scalar(out=identf[:], in_=tmpf, scalar=0.0, op=ALU.is_equal), reads=[r_c], writes=[r_c])
        S.op("dve", lambda g: g.tensor_copy(out=identb[:], in_=identf[:]), reads=[r_c], writes=[r_c])
        S.op("dve", lambda g: g.tensor_single_scalar(out=triLEf[:], in_=tmpf, scalar=0.0, op=ALU.is_ge), reads=[r_c], writes=[r_c])
        S.op("dve", lambda g: g.tensor_copy(out=maskLEb[:], in_=triLEf[:]), reads=[r_c], writes=[r_c])
        S.op("dve", lambda g: g.tensor_single_scalar(out=maskLTf[:], in_=tmpf, scalar=0.0, op=ALU.is_lt), reads=[r_c], writes=[r_c])
        S.op("dve", lambda g: g.memset(onesf[:], 1.0), writes=[r_c])
        S.op("dve", lambda g: g.tensor_scalar(out=negLEf[:], in0=triLEf[:], scalar1=-1.0, scalar2=30000.0, op0=ALU.add, op1=ALU.mult),
             reads=[r_c], writes=[r_c])
        S.op("dve", lambda g: g.memset(epsb[:], EPS), writes=[r_c])
        S.op("dve", lambda g: g.memset(oneb[:], 1.0), writes=[r_c])
        S.op("dve", lambda g: g.memset(mhalf[:], -0.5), writes=[r_c])
        S.op("dve", lambda g: g.memset(carry[:], 0.0), writes=[r_c])

        o = 0
        QTs = []
        KTs = []
        for i in range(2):
            QTs.append(vw(BIGB, o, 4096, BF16)); o += 4096
            KTs.append(vw(BIGB, o, 4096, BF16)); o += 4096
        Vts = []
        for i in range(2):
            Vts.append(vw(BIGB, o, 16 * 132 * 2, BF16, "p (c n) -> p c n", n=132)); o += 16 * 132 * 2
        LM0 = vw(BIGB, o, 8192, F32); o += 8192
        LH = vw(BIGB, o, 8192, F32); o += 8192
        LHi = LH.bitcast(I32)
        sfs = [vw(BIGB, o + 2048 * i, 2048, F32) for i in range(2)]; o += 4096
        pts = [vw(BIGB, o + 1024 * i, 1024, BF16) for i in range(3)]; o += 3072
        ebuf = vw(BIGB, o, 8256, F32); o += 8256
        cumbuf = vw(BIGB, o, 8256, F32); o += 8256
        abuf = vw(BIGB, o, 4096, BF16); o += 4096
        aTb = vw(BIGB, o, 4096, BF16, "p (c n) -> p c n", n=128); o += 4096
        oThs = [vw(BIGB, o + 4096 * i, 4096, BF16) for i in range(2)]; o += 8192
        obuf = vw(BIGB, o, 1024, BF16, "p (c n) -> p c n", n=128); o += 1024
        junk = vw(BIGB, o, 4096, BF16); o += 4096
        assert o <= 22528 * 4, o
        hnT = vw(BIGA, 0, 65536, BF16, "p (c n) -> p c n", n=T)
        r_hnT = Res("hnT")
        wslots = [vw(WGU, 4096 * i, 4096, BF16, "p (c n) -> p c n", n=128) for i in range(6)]
        wring = Ring(S, wslots, with_dsem=True)

        r_lm = Res("LM0")

        def build_masks():
            t_i = LHi
            ti2 = ebuf[:, 0:2048].bitcast(I32)
            tf_d = cumbuf[:, 1:2049]
            tf_b = vw(BIGB, 64896, 8192, F32)
            tf_c = vw(BIGB, 73088, 8192, F32)
            R = [r_lm]

            def dv(fn):
                S.op("dve", fn, reads=R, writes=R)
            S.op("pool", lambda g: g.iota(t_i, pattern=[[1, 2048]], base=0, channel_multiplier=-1), reads=R, writes=R)
            dv(lambda g: g.tensor_copy(out=tf_d, in_=t_i))
            dv(lambda g: g.tensor_scalar(out=tf_b, in0=tf_d, scalar1=0.0, scalar2=None, op0=ALU.is_ge))
            dv(lambda g: g.tensor_scalar(out=tf_c, in0=tf_d, scalar1=128.0, scalar2=None, op0=ALU.is_le))
            dv(lambda g: g.tensor_tensor(out=LM0, in0=tf_b, in1=tf_c, op=ALU.mult))
            for (div, lim) in ((4.0, 512.0), (16.0, None)):
                dv(lambda g, div=div: g.tensor_scalar(out=tf_c, in0=tf_d, scalar1=1.0 / div, scalar2=None, op0=ALU.mult))
                dv(lambda g: g.tensor_copy(out=ti2, in_=tf_c))
                dv(lambda g: g.tensor_copy(out=ebuf[:, 0:2048], in_=ti2))
                dv(lambda g: g.tensor_tensor(out=tf_c, in0=tf_c, in1=ebuf[:, 0:2048], op=ALU.is_equal))
                dv(lambda g: g.tensor_tensor(out=tf_c, in0=tf_c, in1=tf_b, op=ALU.mult))
                if lim is not None:
                    dv(lambda g, lim=lim: g.scalar_tensor_tensor(out=tf_c, in0=tf_d, scalar=lim, in1=tf_c, op0=ALU.is_le, op1=ALU.mult))
                dv(lambda g: g.tensor_tensor(out=LM0, in0=LM0, in1=tf_c, op=ALU.add))
            dv(lambda g: g.tensor_scalar(out=tf_b, in0=LM0, scalar1=1.0, scalar2=-1.0, op0=ALU.min, op1=ALU.add))
            dv(lambda g: g.tensor_scalar(out=LM0, in0=LM0, scalar1=1e-18, scalar2=None, op0=ALU.max))
            S.op("act", lambda g: g.activation(out=LM0, in_=LM0, func=AF.Ln), reads=R, writes=R)
            dv(lambda g: g.scalar_tensor_tensor(out=LM0, in0=tf_b, scalar=1000.0, in1=LM0, op0=ALU.mult, op1=ALU.add))
            dv(lambda g: g.tensor_scalar(out=LM0, in0=LM0, scalar1=1.0 / SCALE, scalar2=None, op0=ALU.mult))
            for i in range(2):
                dv(lambda g, i=i: g.memset(Vts[i][:, :, 128:129], 1.0))
            dv(lambda g: g.memset(cumbuf[:, 0:1], 0.0))
            S.barrier()

        S.barrier()

        xt_slots = [(vw(XT, 8192 * i, 8192, F32), Res("xt%d" % i), S.dsem()) for i in range(2)]
        xs_b = junk
        r_xs = Res("xs")
        r_stat = Res("stat")
        ptr_views = [pO[:, 0:2, :].rearrange("p a b -> p (a b)").bitcast(BF16).rearrange("p (c n) -> p c n", n=128),
                     pO[:, 2:4, :].rearrange("p a b -> p (a b)").bitcast(BF16).rearrange("p (c n) -> p c n", n=128)]

        xs_bufs = [junk, vw(BIGB, 73088, 4096, BF16)]
        r_xss = [r_xs, Res("xs2")]
        r_stats = [r_stat, Res("stat2")]

        def norm_transpose(src_ap, r_src, i, gvec, dstT, r_dst, col0):
            k = i % 2
            xt, r_xt, d_xt = xt_slots[k]
            xsb, r_x_s, r_st, sc = xs_bufs[k], r_xss[k], r_stats[k], 40 * k
            S.dma("sp", lambda g: g.dma_start(out=xt, in_=src_ap), d_xt, reads=[r_src], writes=[r_xt])
            S.op("act", lambda g: g.activation(out=xsb, in_=xt, func=AF.Square, accum_out=stat[:, sc:sc + 1]),
                 reads=[r_xt], writes=[r_x_s, r_st])
            S.op("act", lambda g: g.activation(out=stat[:, sc + 1:sc + 2], in_=stat[:, sc:sc + 1], func=AF.Ln, scale=1.0 / D, bias=epsb[:]),
                 reads=[r_st], writes=[r_st])
            S.op("act", lambda g: g.activation(out=stat[:, sc + 2:sc + 3], in_=stat[:, sc + 1:sc + 2], func=AF.Exp, scale=-0.5),
                 reads=[r_st], writes=[r_st])
            S.op("dve", lambda g: g.tensor_scalar(out=xsb, in0=xt, scalar1=stat[:, sc + 2:sc + 3], scalar2=None, op0=ALU.mult),
                 reads=[r_xt, r_st], writes=[r_x_s])
            pv = ptr_views[k]
            rp = [r_pO[2 * k], r_pO[2 * k + 1]]
            for c in range(16):
                S.op("pe", lambda g, c=c: g.transpose(out=pv[:, c, :], in_=xsb[:, c * 128:(c + 1) * 128], identity=identb[:]),
                     reads=[r_x_s, r_c], writes=rp)
            S.op("dve", lambda g: g.tensor_tensor(out=dstT[:, :, col0:col0 + 128], in0=pv,
                                                  in1=gvec.unsqueeze(2).broadcast_to([128, 16, 128]), op=ALU.mult),
                 reads=rp + [r_par], writes=[r_dst])

        r_x = [Res("xrow%d" % i) for i in range(NT)]
        r_oTd = [Res("oTd%d" % h) for h in range(NH)]
        slopes = [2.0 ** (-8.0 * (i + 1) / HA) for i in range(HA)]

        for l in range(n_layers):
            src_d = x_d if l == 0 else xres_d
            for i in range(NT):
                norm_transpose(src_d[i * 128:(i + 1) * 128, :], r_x[i], i, gmix[:, l, :], hnT, r_hnT, i * 128)
            if l == 0:
                dbg_dump("hnT", hnT, [r_hnT])
            wf, r_wf, d_wf = wring.next()
            wfv = wf[:, :, 0:HC]
            S.dma("pool", lambda g: g.dma_start(out=wfv, in_=w_in_d[l].rearrange("(c p) n -> p c n", p=128)[:, :, 6144:6144 + HC]),
                  d_wf, writes=[r_wf])
            pfc = pX[:, 0:16 * HC].rearrange("p (b h) -> p b h", h=HC)
            for tb in range(NT):
                for c in range(16):
                    S.op("pe", lambda g, tb=tb, c=c: g.matmul(pfc[:, tb, :], lhsT=hnT[:, c, tb * 128:(tb + 1) * 128],
                                                               rhs=wfv[:, c, :], start=(c == 0), stop=(c == 15)),
                         reads=[r_hnT, r_wf], writes=[r_pX])
            r_lf = Res("lf")
            lf3 = lf[:].rearrange("p (b h) -> p b h", h=HC)
            S.op("dve", lambda g: g.tensor_tensor(out=lf3, in0=pfc, in1=bfb[:, l, :].unsqueeze(1).broadcast_to([128, 16, HC]), op=ALU.add),
                 reads=[r_pX, r_par], writes=[r_lf])
            S.op("act", lambda g: g.activation(out=lf[:], in_=lf[:], func=AF.Exp, scale=-1.0), reads=[r_lf], writes=[r_lf])
            S.op("act", lambda g: g.activation(out=lf[:], in_=lf[:], func=AF.Ln, bias=oneb[:]), reads=[r_lf], writes=[r_lf])
            pc1 = pT[:, 0:16 * HC]
            pc2 = pT[:, 128:128 + 16 * HC]
            S.op("pe", lambda g: g.matmul(pc1, lhsT=triLEf[:], rhs=lf[:], start=True, stop=True), reads=[r_lf, r_c], writes=[r_pT])
            S.op("pe", lambda g: g.matmul(pc2, lhsT=onesf[:], rhs=lf[:], start=True, stop=True), reads=[r_lf, r_c], writes=[r_pT])
            r_cc = Res("ccar")
            cc3 = ccar[:].rearrange("p (b h) -> p b h", h=HC)
            cn3 = cneg[:].rearrange("p (b h) -> p b h", h=HC)
            pc2v = pc2.rearrange("p (b h) -> p b h", h=HC)
            pc1v = pc1.rearrange("p (b h) -> p b h", h=HC)
            S.op("dve", lambda g: g.memset(ccar[:, 0:HC], 0.0), writes=[r_cc])
            for b in range(16):
                S.op("dve", lambda g, b=b: g.tensor_tensor(out=cc3[:, b + 1, :], in0=pc2v[:, b, :], in1=cc3[:, b, :], op=ALU.add),
                     reads=[r_pT, r_cc], writes=[r_cc])
            S.op("dve", lambda g: g.tensor_tensor(out=cn3, in0=pc1v, in1=cc3[:, 0:16, :], op=ALU.add), reads=[r_pT, r_cc], writes=[r_cc])
            if l == 0:
                dbg_dump("cneg", cneg[:], [r_cc])

            S.barrier()
            build_masks()
            if l == 0:
                dbg_dump("LM0", LM0, [r_lm])
            r_LH = Res("LH")
            sf_ring = Ring(S, sfs)
            pt_ring = Ring(S, pts)
            r_ebuf = Res("ebuf")
            r_cum = Res("cum")
            r_abuf = Res("abuf")
            r_aT = Res("aT")
            r_ob = Res("obuf")
            r_junk = r_xs
            d_oT = [S.dsem(), S.dsem()]
            r_oTh = [Res("oTh0"), Res("oTh1")]
            ps_i = [0]

            def ps_next():
                k = ps_i[0] % 2
                ps_i[0] += 1
                return pS[:, k, :], r_pS[k]

            ev_i = [0]

            evac_pref = [None]

            def evac(out, in_, reads, writes):
                ev_i[0] += 1
                if evac_pref[0] == "act" or (evac_pref[0] is None and ev_i[0] % 2):
                    S.op("act", lambda g: g.activation(out=out, in_=in_, func=AF.Copy), reads=reads, writes=writes)
                else:
                    S.op("dve", lambda g: g.tensor_copy(out=out, in_=in_), reads=reads, writes=writes)

            pTb = pT[:].bitcast(BF16)[:, 0:512].rearrange("p (c n) -> p c n", n=128)

            def epilogue(l, h, nblk, pviews, rviews, qcol0, with_den, oTh, r_oT):
                nb = nblk
                if with_den:
                    for j in range(nb):
                        S.op("dve", lambda g, j=j: g.reciprocal(out=stat[:, 8 + j:9 + j], in_=pviews[j][:, 128:129]),
                             reads=[rviews[j]], writes=[r_stat])
                for j in range(nb):
                    if with_den:
                        S.op("act", lambda g, j=j: g.activation(out=junk[:, 0:128], in_=pviews[j][:, 0:128], func=AF.Square,
                                                               scale=stat[:, 8 + j:9 + j], accum_out=stat[:, 12 + j:13 + j]),
                             reads=[rviews[j], r_stat], writes=[r_junk, r_stat])
                    else:
                        S.op("act", lambda g, j=j: g.activation(out=junk[:, 0:128], in_=pviews[j][:, 0:128], func=AF.Square,
                                                               accum_out=stat[:, 12 + j:13 + j]),
                             reads=[rviews[j]], writes=[r_junk, r_stat])
                S.op("act", lambda g: g.activation(out=stat[:, 16:16 + nb], in_=stat[:, 12:12 + nb], func=AF.Ln, scale=1.0 / HD, bias=epsb[:]),
                     reads=[r_stat], writes=[r_stat])
                S.op("act", lambda g: g.activation(out=stat[:, 20:20 + nb], in_=stat[:, 16:16 + nb], func=AF.Exp, scale=-0.5),
                     reads=[r_stat], writes=[r_stat])
                if with_den:
                    S.op("dve", lambda g: g.tensor_tensor(out=stat[:, 20:20 + nb], in0=stat[:, 20:20 + nb], in1=stat[:, 8:8 + nb], op=ALU.mult),
                         reads=[r_stat], writes=[r_stat])
                for j in range(nb):
                    S.op("dve", lambda g, j=j: g.tensor_scalar(out=obuf[:, j, :], in0=pviews[j][:, 0:128], scalar1=stat[:, 20 + j:21 + j],
                                                               scalar2=None, op0=ALU.mult),
                         reads=[rviews[j], r_stat], writes=[r_ob])
                for j in range(nb):
                    S.op("pe", lambda g, j=j: g.transpose(out=pTb[:, j, :], in_=obuf[:, j, :], identity=identb[:]),
                         reads=[r_ob, r_c], writes=[r_pT])
                S.op("act", lambda g: g.activation(out=oTh[:, qcol0:qcol0 + 128 * nb],
                                                   in_=pTb[:, 0:nb, :].rearrange("p c n -> p (c n)"), func=AF.Copy,
                                                   scale=ghead[:, l, h:h + 1]),
                     reads=[r_pT, r_par], writes=[r_oT])

            r_QTs = [Res("QT0"), Res("QT1")]
            r_KTs = [Res("KT0"), Res("KT1")]
            r_Vts = [Res("Vt0"), Res("Vt1")]
            ebufs = [ebuf, vw(BIGB, 24832, 8256, F32)]
            cumbufs = [cumbuf, vw(BIGB, 24832 + 8256, 8256, F32)]
            abufs = [abuf, vw(XT, 0, 4096, BF16)]
            r_ebufs = [r_ebuf, Res("ebuf2")]
            r_cums = [r_cum, Res("cum2")]
            r_abufs = [r_abuf, Res("abuf2")]
            r_statB = [Res("negtot0"), Res("negtot1")]
            r_statE = Res("statE")
            r_ab3 = Res("abuf3")
            r_aT2 = Res("aT2")

            def proj_gen(h):
                typ, hi, cq, ck, cv = head_info(h)
                QT, KT, Vt = QTs[h % 2], KTs[h % 2], Vts[h % 2]
                r_QT, r_KT, r_Vt = r_QTs[h % 2], r_KTs[h % 2], r_Vts[h % 2]
                wv_in = w_in_d[l].rearrange("(c p) n -> p c n", p=128)
                ws = []
                for col in (cq, ck, cv):
                    w_ap, r_w, d_w = wring.next()
                    S.dma("pool", lambda g, w_ap=w_ap, col=col: g.dma_start(out=w_ap, in_=wv_in[:, :, col:col + 128]), d_w, writes=[r_w])
                    ws.append((w_ap, r_w))
                yield
                for (w_ap, r_w), dst, r_dst in ((ws[0], QT, r_QT), (ws[1], KT, r_KT)):
                    for tg in range(4):
                        ps, r_ps = pX[:, :], r_pX
                        for c in range(16):
                            S.op("pe", lambda g, ps=ps, w_ap=w_ap, c=c, tg=tg: g.matmul(ps, lhsT=w_ap[:, c, :], rhs=hnT[:, c, tg * 512:(tg + 1) * 512],
                                                                                        start=(c == 0), stop=(c == 15)),
                                 reads=[r_w, r_hnT], writes=[r_ps])
                            if c % 4 == 3 and c != 15:
                                yield
                        evac(dst[:, tg * 512:(tg + 1) * 512], ps, [r_ps], [r_dst])
                        yield
                w_ap, r_w = ws[2]
                for tg in range(4):
                    ps, r_ps = pX[:, :], r_pX
                    for tb in range(4):
                        t0 = (tg * 4 + tb) * 128
                        for c in range(16):
                            S.op("pe", lambda g, ps=ps, w_ap=w_ap, c=c, tb=tb, t0=t0: g.matmul(ps[:, tb * 128:(tb + 1) * 128], lhsT=hnT[:, c, t0:t0 + 128],
                                                                                              rhs=w_ap[:, c, :], start=(c == 0), stop=(c == 15)),
                                 reads=[r_w, r_hnT], writes=[r_ps])
                        if tb != 3:
                            yield
                    evac(Vt[:, tg * 4:(tg + 1) * 4, 0:128], ps.rearrange("p (c n) -> p c n", n=128), [r_ps], [r_Vt])
                    yield
                if l == 0 and h in (0, 6, 11):
                    dbg_dump("QT%d" % h, QT, [r_QT])
                    dbg_dump("KT%d" % h, KT, [r_KT])
                    dbg_dump("Vt%d" % h, Vt, [r_Vt])

            LHs = [LH, ebuf[:, 0:2048]]
            r_LHs = [r_LH, Res("LHb")]
            lh_built = set()

            def lh_build_gen(hh):
                lh_built.add(hh)
                typ_, hi_, _, _, _ = head_info(hh)
                L_, r_L = LHs[hh % 2], r_LHs[hh % 2]
                extra = [r_ebufs[1], r_cums[1]] if hh % 2 == 0 else [r_ebufs[0]]
                if typ_ == "A":
                    Li = L_.bitcast(I32)
                    S.op("pool", lambda g: g.iota(Li, pattern=[[1, 2048]], base=0, channel_multiplier=-1), writes=[r_L] + extra)
                    yield
                    S.op("dve", lambda g: g.tensor_copy(out=cumbuf[:, 1:2049], in_=Li), reads=[r_L], writes=[r_cum])
                    yield
                    S.op("dve", lambda g, sl=slopes[hi_]: g.scalar_tensor_tensor(out=L_, in0=cumbuf[:, 1:2049], scalar=-sl / SCALE, in1=LM0,
                                                                                  op0=ALU.mult, op1=ALU.add),
                         reads=[r_cum, r_lm], writes=[r_L])
                    yield
                else:
                    dgs = [abuf[:, 256 * i:256 * (i + 1)].bitcast(F32) for i in range(8)]
                    for bg in range(4):
                        ps, r_ps = ps_next()
                        for b4 in range(4):
                            b = bg * 4 + b4
                            dg, rdg = dgs[b % 8], r_dgs[b % 8]
                            S.op("dve", lambda g, dg=dg, b=b: g.tensor_scalar(out=dg, in0=identf[:], scalar1=cn3[:, b, hi_:hi_ + 1], scalar2=None,
                                                                             op0=ALU.mult),
                                 reads=[r_c, r_cc], writes=[rdg, r_abuf])
                            S.op("pe", lambda g, dg=dg, ps=ps, b4=b4: g.matmul(ps[:, b4 * 128:(b4 + 1) * 128], lhsT=onesf[:], rhs=dg, start=True, stop=True),
                                 reads=[rdg, r_c], writes=[r_ps])
                        S.op("act", lambda g, ps=ps, bg=bg: g.activation(out=L_[:, bg * 512:(bg + 1) * 512], in_=ps, func=AF.Copy, scale=-1.0 / SCALE),
                             reads=[r_ps], writes=[r_L] + extra)
                        yield

            r_dgs = [Res("dg%d" % i) for i in range(8)]
            stage = vw(XT, 12288, 4 * 132 * 4, F32, "p (c n) -> p c n", n=132)
            r_stage = Res("stage")

            def epilogue_ac_1(l, h, qg):
                S.op("act", lambda g: g.activation(out=stage[:, :, 0:129], in_=pO[:, :, 0:129], func=AF.Copy), reads=r_pO, writes=[r_stage])
                S.op("dve", lambda g: g.reciprocal(out=stat[:, 8:12].unsqueeze(2), in_=stage[:, :, 128:129]), reads=[r_stage], writes=[r_stat])
                for j in range(4):
                    S.op("act", lambda g, j=j: g.activation(out=junk[:, 0:128], in_=stage[:, j, 0:128], func=AF.Square,
                                                           scale=stat[:, 8 + j:9 + j], accum_out=stat[:, 12 + j:13 + j]),
                         reads=[r_stage, r_stat], writes=[r_junk, r_stat])
                S.op("act", lambda g: g.activation(out=stat[:, 16:20], in_=stat[:, 12:16], func=AF.Ln, scale=1.0 / HD, bias=epsb[:]),
                     reads=[r_stat], writes=[r_stat])
                S.op("act", lambda g: g.activation(out=stat[:, 20:24], in_=stat[:, 16:20], func=AF.Exp, scale=-0.5), reads=[r_stat], writes=[r_stat])
                S.op("dve", lambda g: g.tensor_tensor(out=stat[:, 20:24], in0=stat[:, 20:24], in1=stat[:, 8:12], op=ALU.mult), reads=[r_stat], writes=[r_stat])
                S.op("dve", lambda g: g.tensor_tensor(out=obuf, in0=stage[:, :, 0:128], in1=stat[:, 20:24].unsqueeze(2).broadcast_to([128, 4, 128]), op=ALU.mult),
                     reads=[r_stage, r_stat], writes=[r_ob])

            def epilogue_ac_2(l, h, qg, oTh, r_oT):
                for j in range(4):
                    S.op("pe", lambda g, j=j: g.transpose(out=pTb[:, j, :], in_=obuf[:, j, :], identity=identb[:]), reads=[r_ob, r_c], writes=[r_pT])
                S.op("act", lambda g: g.activation(out=oTh[:, qg * 512:(qg + 1) * 512], in_=pTb.rearrange("p c n -> p (c n)"), func=AF.Copy,
                                                   scale=ghead[:, l, h:h + 1]),
                     reads=[r_pT, r_par], writes=[r_oT])

            cur_gen = [None]

            def pump():
                gnr = cur_gen[0]
                if gnr is None:
                    return
                try:
                    next(gnr)
                except StopIteration:
                    cur_gen[0] = None

            def drain_gen():
                while cur_gen[0] is not None:
                    pump()

            cur_gen[0] = proj_gen(0)
            drain_gen()
            for h in range(NH):
                typ, hi, cq, ck, cv = head_info(h)
                QT, KT, Vt = QTs[h % 2], KTs[h % 2], Vts[h % 2]
                r_QT, r_KT, r_Vt = r_QTs[h % 2], r_KTs[h % 2], r_Vts[h % 2]
                oTh = oThs[h % 2]
                r_oT = r_oTh[h % 2]
                evac_pref[0] = "act" if typ == "B" else None
                if h + 1 < NH:
                    cur_gen[0] = proj_gen(h + 1)
                    pump()

                if typ in ("A", "C"):
                    if h not in lh_built:
                        for _ in lh_build_gen(h):
                            pass
                    LHc, r_LHc = LHs[h % 2], r_LHs[h % 2]
                    nxt_gen = None
                    if h + 1 < NH and head_info(h + 1)[0] == typ:
                        nxt_gen = lh_build_gen(h + 1)
                    steps = [(qg, m) for qg in range(4) for m in range(4 * qg + 4)]
                    state = {}

                    def do_s(i):
                        qg, m = steps[i]
                        c0 = max(0, m - 4 * qg) * 128
                        ps, r_ps = ps_next()
                        S.op("pe", lambda g: g.matmul(ps[:, c0:512], lhsT=KT[:, m * 128:(m + 1) * 128], rhs=QT[:, qg * 512 + c0:(qg + 1) * 512],
                                                      start=True, stop=True),
                             reads=[r_KT, r_QT], writes=[r_ps])
                        pt, r_pt, _ = pt_ring.next()
                        sf, r_sf, _ = sf_ring.next()
                        s0 = qg * 512 - m * 128 if typ == "A" else qg * 512
                        S.op("dve", lambda g: g.tensor_tensor(out=sf[:, c0:512], in0=ps[:, c0:512], in1=LHc[:, s0 + c0:s0 + 512], op=ALU.add),
                             reads=[r_ps, r_LHc], writes=[r_sf])
                        if typ == "A":
                            S.op("act", lambda g: g.activation(out=pt[:, c0:512], in_=sf[:, c0:512], func=AF.Exp, scale=SCALE),
                                 reads=[r_sf], writes=[r_pt])
                        else:
                            if m >= 4 * qg:
                                S.op("dve", lambda g: g.tensor_tensor(out=sf[:, c0:c0 + 128], in0=sf[:, c0:c0 + 128], in1=negLEf[:], op=ALU.add),
                                     reads=[r_sf, r_c], writes=[r_sf])
                            S.op("act", lambda g: g.activation(out=pt[:, c0:512], in_=sf[:, c0:512], func=AF.Exp, scale=SCALE,
                                                               bias=cn3[:, m, hi:hi + 1]),
                                 reads=[r_sf, r_cc], writes=[r_pt])
                        state[i] = (pt, r_pt, c0)

                    def do_av(i):
                        qg, m = steps[i]
                        pt, r_pt, c0 = state.pop(i)
                        for j in range(c0 // 128, 4):
                            S.op("pe", lambda g, j=j: g.matmul(pO[:, j, 0:129], lhsT=pt[:, j * 128:(j + 1) * 128], rhs=Vt[:, m, 0:129],
                                                               start=(m == 0), stop=(m == 4 * qg + j)),
                                 reads=[r_pt, r_Vt], writes=[r_pO[j]])
                        if m == 4 * qg + 3:
                            epilogue_ac_1(l, h, qg)
                            deferred.append((i + 3, qg))
                            pump()

                    deferred = []
                    do_s(0)
                    for i in range(len(steps)):
                        if i + 1 < len(steps):
                            do_s(i + 1)
                        pump()
                        do_av(i)
                        while deferred and deferred[0][0] <= i:
                            epilogue_ac_2(l, h, deferred.pop(0)[1], oTh, r_oT)
                        if nxt_gen is not None and i >= 6 and i % 4 == 2:
                            try:
                                next(nxt_gen)
                            except StopIteration:
                                nxt_gen = None
                    while deferred:
                        epilogue_ac_2(l, h, deferred.pop(0)[1], oTh, r_oT)
                    if nxt_gen is not None:
                        for _ in nxt_gen:
                            pass
                else:
                    paT = pO[:, 0:2, :].rearrange("p a b -> p (a b)").bitcast(BF16).rearrange("p (c n) -> p c n", n=128)
                    if hi == 0:
                        S.op("dve", lambda g: g.memset(cumbufs[1][:, 0:1], 0.0), reads=[r_LH, r_lm], writes=[r_cums[1], r_LH, r_LHs[1], r_lm])

                    abufs3 = [abuf, vw(XT, 0, 4096, BF16), vw(XT, 4096, 4096, BF16)]
                    r_abufs3 = [r_abuf, r_abufs[1], r_ab3]
                    aTbs = [aTb, vw(XT, 8192, 4096, BF16, "p (c n) -> p c n", n=128)]
                    r_aTs = [r_aT, r_aT2]

                    r_ebc = [[Res("eb%d_%d" % (k, c)) for c in range(4)] for k in range(2)]

                    def front(n):
                        eb, cb_, r_cb = ebufs[n % 2], cumbufs[n % 2], r_cums[n % 2]
                        Nk = 128 * (n + 1)
                        nch = (Nk + 511) // 512
                        S.op("dve", lambda g: g.memset(cb_[:, Nk:Nk + 1], 1.0), writes=[r_cb])
                        for ch in range(nch - 1, -1, -1):
                            k0 = ch * 512
                            kw = min(512, Nk - k0)
                            r_e = r_ebc[n % 2][ch]
                            ps, r_ps = ps_next()
                            S.op("pe", lambda g, ps=ps, k0=k0, kw=kw: g.matmul(ps[:, 0:kw], lhsT=QT[:, n * 128:(n + 1) * 128], rhs=KT[:, k0:k0 + kw],
                                                                               start=True, stop=True),
                                 reads=[r_QT, r_KT], writes=[r_ps])
                            S.op("act", lambda g, ps=ps, k0=k0, kw=kw: g.activation(out=eb[:, k0:k0 + kw], in_=ps[:, 0:kw], func=AF.Sigmoid, scale=-SCALE),
                                 reads=[r_ps], writes=[r_e])
                            if ch == nch - 1:
                                S.op("dve", lambda g: g.tensor_tensor(out=eb[:, n * 128:(n + 1) * 128], in0=eb[:, n * 128:(n + 1) * 128],
                                                                      in1=triLEf[:], op=ALU.max),
                                     reads=[r_e, r_c], writes=[r_e])
                            init = 1.0 if ch == nch - 1 else cb_[:, k0 + kw:k0 + kw + 1]
                            S.op("dve", lambda g, k0=k0, kw=kw, init=init: g.tensor_tensor_scan(out=cb_[:, k0:k0 + kw][:, ::-1], data0=eb[:, k0:k0 + kw][:, ::-1],
                                                                                                 data1=eb[:, k0:k0 + kw][:, ::-1], initial=init,
                                                                                                 op0=ALU.mult, op1=ALU.min),
                                 reads=[r_e, r_cb], writes=[r_cb])

                    def midB(n):
                        cb_, r_cb, ab, r_ab = cumbufs[n % 2], r_cums[n % 2], abufs3[n % 3], r_abufs3[n % 3]
                        Nk = 128 * (n + 1)
                        S.op("dve", lambda g: g.tensor_tensor(out=ab[:, 0:Nk], in0=cb_[:, 1:Nk + 1], in1=cb_[:, 0:Nk], op=ALU.subtract),
                             reads=[r_cb], writes=[r_ab])

                    def tailA(n):
                        ab, r_ab = abufs3[n % 3], r_abufs3[n % 3]
                        for m in range(n + 1):
                            S.op("pe", lambda g, m=m: g.transpose(out=paT[:, m, :], in_=ab[:, m * 128:(m + 1) * 128], identity=identb[:]),
                                 reads=[r_ab, r_c], writes=[r_pO[0], r_pO[1]])
                        evac(aTbs[n % 2][:, 0:n + 1, :], paT[:, 0:n + 1, :], [r_pO[0], r_pO[1]], [r_aTs[n % 2]])

                    def tailB(n):
                        ko = 2 + n % 2
                        aT_ = aTbs[n % 2]
                        for m in range(n + 1):
                            S.op("pe", lambda g, m=m: g.matmul(pO[:, ko, 0:128], lhsT=aT_[:, m, :], rhs=Vt[:, m, 0:128], start=(m == 0), stop=(m == n)),
                                 reads=[r_aTs[n % 2], r_Vt], writes=[r_pO[ko]])

                    def epiA(n):
                        ko = 2 + n % 2
                        pv = pO[:, ko, :]
                        S.op("act", lambda g: g.activation(out=junk[:, 0:128], in_=pv[:, 0:128], func=AF.Square, accum_out=stat[:, 32:33]),
                             reads=[r_pO[ko]], writes=[r_junk, r_statE])
                        S.op("dve", lambda g: g.tensor_scalar(out=stat[:, 33:34], in0=stat[:, 32:33], scalar1=1.0 / HD, scalar2=EPS, op0=ALU.mult, op1=ALU.add),
                             reads=[r_statE], writes=[r_statE])
                        S.op("pool", lambda g: g.tensor_tensor(out=stat[:, 34:35], in0=stat[:, 33:34], in1=mhalf[:], op=ALU.pow),
                             reads=[r_statE, r_c], writes=[r_statE])
                        S.op("dve", lambda g: g.tensor_scalar(out=obuf[:, n % 4, :], in0=pv[:, 0:128], scalar1=stat[:, 34:35], scalar2=None, op0=ALU.mult),
                             reads=[r_pO[ko], r_statE], writes=[r_ob])

                    def epiB(n):
                        S.op("pe", lambda g: g.transpose(out=pTb[:, n % 4, :], in_=obuf[:, n % 4, :], identity=identb[:]),
                             reads=[r_ob, r_c], writes=[r_pT])
                        S.op("act", lambda g: g.activation(out=oTh[:, n * 128:(n + 1) * 128], in_=pTb[:, n % 4, :], func=AF.Copy, scale=ghead[:, l, h:h + 1]),
                             reads=[r_pT, r_par], writes=[r_oT])

                    for i in range(NT + 6):
                        pump()
                        pump()
                        if i < NT:
                            front(i)
                        if 0 <= i - 1 < NT:
                            midB(i - 1)
                        if 0 <= i - 3 < NT:
                            tailA(i - 3)
                        pump()
                        if 0 <= i - 4 < NT:
                            tailB(i - 4)
                        if 0 <= i - 5 < NT:
                            epiA(i - 5)
                        if 0 <= i - 6 < NT:
                            epiB(i - 6)
                drain_gen()
                S.dma("sp", lambda g, h=h, oTh=oTh: g.dma_start(out=oT_d[h], in_=oTh), d_oT[h % 2], reads=[r_oT], writes=[r_oTd[h]])
            S.barrier()

            wgu_slots = [vw(WGU, 8192 * i, 8192, BF16, "p (a c n) -> p a c n", a=2, n=128) for i in range(3)]
            wgu_ring = Ring(S, wgu_slots, with_dsem=True)
            wgu_v = w_gu_d[l].rearrange("(c p) n -> p c n", p=128)
            wgu_ld = {}

            def issue_wgu(tt, j):
                wgu, r_wgu, d_wgu = wgu_ring.next()
                S.dma("pool", lambda g: g.dma_start(out=wgu[:, 0, :, :], in_=wgu_v[:, :, j * 128:(j + 1) * 128]), d_wgu, writes=[r_wgu])
                S.dma("pool", lambda g: g.dma_start(out=wgu[:, 1, :, :], in_=wgu_v[:, :, DFF + j * 128:DFF + (j + 1) * 128]), d_wgu, writes=[r_wgu])
                wgu_ld[(tt, j)] = (wgu, r_wgu)

            oT = hnT
            r_oTs = Res("oT")
            d_oTl = S.dsem()
            for h in range(NH):
                S.dma("sp", lambda g, h=h: g.dma_start(out=oT[:, h, :], in_=oT_d[h]), d_oTl, reads=[r_oTd[h]], writes=[r_oTs])
            if l == 0:
                dbg_dump("oT", oT, [r_oTs])
            wo_slots = [vw(BIGB, 16384 * i, 16384, BF16, "p (c n) -> p c n", n=512) for i in range(2)]
            wo_ring = Ring(S, wo_slots, with_dsem=True)
            xp_ring = Ring(S, [vw(BIGB, 32768 + 2048 * i, 2048, F32) for i in range(4)], with_dsem=True)
            d_st = [S.dsem() for _ in range(4)]
            acc_i = 0
            wo_v = w_o_d[l].rearrange("(c p) n -> p c n", p=128)
            for cc in range(4):
                wo, r_wo, d_wo = wo_ring.next()
                S.dma("pool", lambda g, wo=wo, cc=cc: g.dma_start(out=wo, in_=wo_v[:, :, cc * 512:(cc + 1) * 512]), d_wo, writes=[r_wo])
                if cc == 1:
                    for jp in range(3):
                        issue_wgu(0, jp)
                for tb in range(NT):
                    k = acc_i % 6
                    acc_i += 1
                    if k < 2:
                        ps, r_ps = pS[:, k, :], r_pS[k]
                    else:
                        ps, r_ps = pO[:, k - 2, :], r_pO[k - 2]
                    for h in range(NH):
                        S.op("pe", lambda g, ps=ps, wo=wo, h=h, tb=tb: g.matmul(ps, lhsT=oT[:, h, tb * 128:(tb + 1) * 128], rhs=wo[:, h, :],
                                                                               start=(h == 0), stop=(h == NH - 1)),
                             reads=[r_oTs, r_wo], writes=[r_ps])
                    xp, r_xp, d_xp = xp_ring.next()
                    S.dma("sp", lambda g, xp=xp, tb=tb, cc=cc: g.dma_start(out=xp, in_=src_d[tb * 128:(tb + 1) * 128, cc * 512:(cc + 1) * 512]),
                          d_xp, reads=[r_x[tb]], writes=[r_xp])
                    S.op("dve", lambda g, xp=xp, ps=ps: g.tensor_tensor(out=xp, in0=ps, in1=xp, op=ALU.add), reads=[r_ps, r_xp], writes=[r_xp])
                    S.dma("sp", lambda g, xp=xp, tb=tb, cc=cc: g.dma_start(out=xres_d[tb * 128:(tb + 1) * 128, cc * 512:(cc + 1) * 512], in_=xp),
                          d_st[acc_i % 4], reads=[r_xp], writes=[r_x[tb]])
            S.barrier()
            if l == 0 and "x1" in dbg_out:
                dd = S.dsem()
                for tb in range(NT):
                    xt, r_xt, d_xt = xt_slots[tb % 2]
                    S.dma("sp", lambda g, xt=xt, tb=tb: g.dma_start(out=xt, in_=xres_d[tb * 128:(tb + 1) * 128, :]), d_xt, reads=[r_x[tb]], writes=[r_xt])
                    S.dma("sp", lambda g, xt=xt, tb=tb: g.dma_start(out=dbg_out["x1"][tb * 128:(tb + 1) * 128, :], in_=xt), dd, reads=[r_xt])
                S.barrier()

            hn2T = vw(BIGA, 0, 32768, BF16, "p (c n) -> p c n", n=1024)
            r_hn2 = Res("hn2T")
            wd_slots = [vw(BIGA, 32768 + 11264 * i, 11264, BF16, "p (c n) -> p c n", n=128) for i in range(2)]
            wd_ring = Ring(S, wd_slots, with_dsem=True)
            ots_ring = Ring(S, [vw(BIGA, 32768 + 22528 + 2048 * i, 2048, F32) for i in range(2)])
            actT = vw(BIGB, 0, 90112, BF16, "p (c n) -> p c n", n=1024)
            r_act = Res("actT")
            cv_ring = Ring(S, [vw(XT, 6144 * i, 6144, F32, "p (a n) -> p a n", a=3) for i in range(2)])
            xq_ring = Ring(S, [vw(XT, 2048 * i, 2048, F32, "p (a n) -> p a n", a=4) for i in range(6)], with_dsem=True)
            d_xst = [S.dsem() for _ in range(6)]
            r_carry = Res("carry")
            cw3 = cw[:, l, :].rearrange("p (j k) -> p j k", k=3)
            wd_v = w_down_d[l].rearrange("(j p) n -> p j n", p=128)
            wd_ld = {}

            def issue_wd(tt, dc):
                wd, r_wd, d_wd = wd_ring.next()
                S.dma("pool", lambda g: g.dma_start(out=wd, in_=wd_v[:, :, dc * 128:(dc + 1) * 128]), d_wd, writes=[r_wd])
                wd_ld[(tt, dc)] = (wd, r_wd)
            pgu_i = 0
            S.op("dve", lambda g: g.memset(carry[:], 0.0), writes=[r_carry])
            for tt in range(2):
                for i in range(8):
                    tb = tt * 8 + i
                    norm_transpose(xres_d[tb * 128:(tb + 1) * 128, :], r_x[tb], i, gffn[:, l, :], hn2T, r_hn2, i * 128)
                S.barrier()
                for j in range(NF):
                    if (tt, j) not in wgu_ld:
                        issue_wgu(tt, j)
                    wgu, r_wgu = wgu_ld.pop((tt, j))
                    if j == 6:
                        issue_wd(tt, 0)
                        issue_wd(tt, 1)
                    for half in range(2):
                        kk = (pgu_i % 2) * 2
                        pgu_i += 1
                        pv2 = [pO[:, kk, :], pO[:, kk + 1, :]]
                        rv2 = [r_pO[kk], r_pO[kk + 1]]
                        for a in range(2):
                            for c in range(16):
                                S.op("pe", lambda g, a=a, c=c, wgu=wgu, pv2=pv2, half=half: g.matmul(pv2[a], lhsT=wgu[:, a, c, :],
                                                                                                    rhs=hn2T[:, c, half * 512:(half + 1) * 512],
                                                                                                    start=(c == 0), stop=(c == 15)),
                                     reads=[r_wgu, r_hn2], writes=[rv2[a]])
                        cvb, r_cv, _ = cv_ring.next()
                        first = (tt == 0 and half == 0)
                        for a in range(2):
                            jj = a * NF + j
                            ph = pv2[a]
                            acc = cvb[:, a, :]
                            S.op("act", lambda g, acc=acc, ph=ph, jj=jj: g.activation(out=acc, in_=ph, func=AF.Identity, scale=cw3[:, jj, 2:3],
                                                                                    bias=cb[:, l, jj:jj + 1]),
                                 reads=[rv2[a], r_par], writes=[r_cv])
                            S.op("dve", lambda g, acc=acc, ph=ph, jj=jj: g.scalar_tensor_tensor(out=acc[:, 1:512], in0=ph[:, 0:511], scalar=cw3[:, jj, 1:2],
                                                                                               in1=acc[:, 1:512], op0=ALU.mult, op1=ALU.add),
                                 reads=[rv2[a], r_par, r_cv], writes=[r_cv])
                            S.op("dve", lambda g, acc=acc, ph=ph, jj=jj: g.scalar_tensor_tensor(out=acc[:, 2:512], in0=ph[:, 0:510], scalar=cw3[:, jj, 0:1],
                                                                                               in1=acc[:, 2:512], op0=ALU.mult, op1=ALU.add),
                                 reads=[rv2[a], r_par, r_cv], writes=[r_cv])
                            if not first:
                                cr = carry[:, 2 * jj:2 * jj + 2]
                                S.op("dve", lambda g, acc=acc, cr=cr, jj=jj: g.scalar_tensor_tensor(out=acc[:, 0:2], in0=cr, scalar=cw3[:, jj, 0:1],
                                                                                                   in1=acc[:, 0:2], op0=ALU.mult, op1=ALU.add),
                                     reads=[r_carry, r_par, r_cv], writes=[r_cv])
                                S.op("dve", lambda g, acc=acc, cr=cr, jj=jj: g.scalar_tensor_tensor(out=acc[:, 0:1], in0=cr[:, 1:2], scalar=cw3[:, jj, 1:2],
                                                                                                   in1=acc[:, 0:1], op0=ALU.mult, op1=ALU.add),
                                     reads=[r_carry, r_par, r_cv], writes=[r_cv])
                            S.op("dve", lambda g, ph=ph, jj=jj: g.tensor_copy(out=carry[:, 2 * jj:2 * jj + 2], in_=ph[:, 510:512]),
                                 reads=[rv2[a], r_cv], writes=[r_carry])
                        S.op("act", lambda g, cvb=cvb: g.activation(out=cvb[:, 2, :], in_=cvb[:, 0, :], func=AF.Silu), reads=[r_cv], writes=[r_cv])
                        S.op("dve", lambda g, cvb=cvb, j=j, half=half: g.tensor_tensor(out=actT[:, j, half * 512:(half + 1) * 512], in0=cvb[:, 2, :],
                                                                                      in1=cvb[:, 1, :], op=ALU.mult),
                             reads=[r_cv], writes=[r_act])
                if l == 0 and tt == 0:
                    dbg_dump("actT", actT, [r_act])
                S.barrier()
                po_i = 0
                ptk = 0
                for dc in range(16):
                    if (tt, dc) not in wd_ld:
                        issue_wd(tt, dc)
                    wd, r_wd = wd_ld.pop((tt, dc))
                    if tt == 0 and dc == 8:
                        for jp in range(3):
                            issue_wgu(1, jp)
                    for half in range(2):
                        k = po_i % 2
                        po_i += 1
                        po, r_po = pS[:, k, :], r_pS[k]
                        for j in range(NF):
                            S.op("pe", lambda g, po=po, wd=wd, j=j, half=half: g.matmul(po, lhsT=wd[:, j, :], rhs=actT[:, j, half * 512:(half + 1) * 512],
                                                                                       start=(j == 0), stop=(j == NF - 1)),
                                 reads=[r_wd, r_act], writes=[r_po])
                        ots, r_ots, _ = ots_ring.next()
                        S.op("act", lambda g, ots=ots, po=po: g.activation(out=ots, in_=po, func=AF.Copy), reads=[r_po], writes=[r_ots])
                        ptv, r_ptv = ((pX, r_pX), (pT, r_pT))[ptk % 2]
                        ptk += 1
                        for b in range(4):
                            S.op("pe", lambda g, ptv=ptv, ots=ots, b=b: g.transpose(out=ptv[:, b * 128:(b + 1) * 128], in_=ots[:, b * 128:(b + 1) * 128],
                                                                                     identity=identf[:]),
                                 reads=[r_ots, r_c], writes=[r_ptv])
                        xq, r_xq, d_xq = xq_ring.next()
                        t0 = tt * 1024 + half * 512
                        rows = [r_x[(t0 // 128) + b] for b in range(4)]
                        S.dma("sp", lambda g, xq=xq, t0=t0, dc=dc: g.dma_start(
                            out=xq, in_=xres_d[t0:t0 + 512, dc * 128:(dc + 1) * 128].rearrange("(b p) n -> p b n", p=128)),
                            d_xq, reads=rows, writes=[r_xq])
                        S.op("dve", lambda g, xq=xq, ptv=ptv: g.tensor_tensor(out=xq, in0=ptv.rearrange("p (b n) -> p b n", n=128), in1=xq, op=ALU.add),
                             reads=[r_ptv, r_xq], writes=[r_xq])
                        S.dma("sp", lambda g, xq=xq, t0=t0, dc=dc: g.dma_start(
                            out=xres_d[t0:t0 + 512, dc * 128:(dc + 1) * 128].rearrange("(b p) n -> p b n", p=128), in_=xq),
                            d_xst[(po_i - 1) % 6], reads=[r_xq], writes=rows)
                S.barrier()
            if l == 0 and "x2" in dbg_out:
                dd = S.dsem()
                for tb in range(NT):
                    xt, r_xt, d_xt = xt_slots[tb % 2]
                    S.dma("sp", lambda g, xt=xt, tb=tb: g.dma_start(out=xt, in_=xres_d[tb * 128:(tb + 1) * 128, :]), d_xt, reads=[r_x[tb]], writes=[r_xt])
                    S.dma("sp", lambda g, xt=xt, tb=tb: g.dma_start(out=dbg_out["x2"][tb * 128:(tb + 1) * 128, :], in_=xt), dd, reads=[r_xt])
                S.barrier()

        gfb = vw(BIGA, 0, 8192, F32)
        r_gfb = Res("gfb")
        d_g = S.dsem()
        S.dma("sp", lambda g: g.dma_start(out=gfb, in_=gfin_d), d_g, writes=[r_gfb])
        yo_ring = Ring(S, [vw(BIGB, 8192 * i, 8192, F32) for i in range(2)], with_dsem=True)
        src_fin = xres_d if n_layers > 0 else x_d
        for i in range(NT):
            xt, r_xt, d_xt = xt_slots[i % 2]
            S.dma("sp", lambda g, xt=xt, i=i: g.dma_start(out=xt, in_=src_fin[i * 128:(i + 1) * 128, :]), d_xt, reads=[r_x[i]], writes=[r_xt])
            S.op("act", lambda g, xt=xt: g.activation(out=xs_b, in_=xt, func=AF.Square, accum_out=stat[:, 0:1]), reads=[r_xt], writes=[r_xs, r_stat])
            S.op("act", lambda g: g.activation(out=stat[:, 1:2], in_=stat[:, 0:1], func=AF.Ln, scale=1.0 / D, bias=epsb[:]), reads=[r_stat], writes=[r_stat])
            S.op("act", lambda g: g.activation(out=stat[:, 2:3], in_=stat[:, 1:2], func=AF.Exp, scale=-0.5), reads=[r_stat], writes=[r_stat])
            yo, r_yo, d_yo = yo_ring.next()
            S.op("dve", lambda g, xt=xt, yo=yo: g.scalar_tensor_tensor(out=yo, in0=xt, scalar=stat[:, 2:3], in1=gfb, op0=ALU.mult, op1=ALU.mult),
                 reads=[r_xt, r_stat, r_gfb], writes=[r_yo])
            S.dma("sp", lambda g, yo=yo, i=i: g.dma_start(out=y_d[i * 128:(i + 1) * 128, :], in_=yo), d_yo, reads=[r_yo])
        S.emit(nc)
    return nc


_NC_CACHE = {}


def _host_params(g_mix, b_f, g_head, g_ffn, conv_w, conv_b, g_final):
    f = np.float32

    def colT(v):
        return np.ascontiguousarray(np.asarray(v, f).reshape(DEPTH, 16, 128).transpose(0, 2, 1))

    gheadT = np.ascontiguousarray(np.asarray(g_head, f).transpose(0, 2, 1))
    bfb = np.ascontiguousarray(np.broadcast_to(np.asarray(b_f, f)[:, None, :], (DEPTH, 128, HC)))
    cwT = np.ascontiguousarray(np.asarray(conv_w, f).reshape(DEPTH, 3, 88, 128).transpose(0, 3, 2, 1)).reshape(DEPTH, 128, 88 * 3)
    cbT = np.ascontiguousarray(np.asarray(conv_b, f).reshape(DEPTH, 88, 128).transpose(0, 2, 1))
    gfinb = np.ascontiguousarray(np.broadcast_to(np.asarray(g_final, f)[None, :], (128, D)))
    return dict(gmixT=colT(g_mix), gffnT=colT(g_ffn), gheadT=gheadT, bfb=bfb, cwT=cwT, cbT=cbT, gfinb=gfinb)


def kernel(x, g_mix, w_in, b_f, g_head, w_o, g_ffn, w_gu, conv_w, conv_b, w_down, g_final):
    if "nc" not in _NC_CACHE:
        _NC_CACHE["nc"] = build_nc()
    nc = _NC_CACHE["nc"]
    par = _host_params(g_mix, b_f, g_head, g_ffn, conv_w, conv_b, g_final)
    shared = dict(w_in=np.ascontiguousarray(w_in, np.float32), w_o=np.ascontiguousarray(w_o, np.float32),
                  w_gu=np.ascontiguousarray(w_gu, np.float32), w_down=np.ascontiguousarray(w_down, np.float32), **par)
    x = np.asarray(x, np.float32)
    in_maps = [dict(x=np.ascontiguousarray(x[b]), **shared) for b in range(8)]
    res = run_bass_kernel_spmd(nc, in_maps, core_ids=list(range(8)))
    return np.stack([res.results[b]["y"] for b in range(8)], axis=0)
```

```python
import math
import contextlib
import numpy as np
import concourse.bass as bass
import concourse.mybir as mybir
from concourse.bass_utils import run_bass_kernel_spmd

F32 = mybir.dt.float32
BF16 = mybir.dt.bfloat16
I32 = mybir.dt.int32
AF = mybir.ActivationFunctionType
ALU = mybir.AluOpType

ENGS = ("pe", "act", "dve", "pool", "sp")

T = 2048
D = 2048
NT = 16
HD = 128
NH = 16
HA, HB, HC = 6, 5, 5
DFF = 5632
NF = 44
N_IN = 6149
EPS = 1e-6
SCALE = 1.0 / math.sqrt(HD)
DEPTH = 2


class Res:
    __slots__ = ("name", "w", "readers")

    def __init__(self, name=""):
        self.name = name
        self.w = None
        self.readers = []


class DSem:
    __slots__ = ("idx", "val", "handle", "res")

    def __init__(self, idx):
        self.idx = idx
        self.val = 0
        self.handle = None
        self.res = Res("dsem%d" % idx)


class Op:
    __slots__ = ("eng", "call", "deps", "need_inc", "seq", "is_dma", "dsem", "dval")


class _Rec:
    def __init__(self):
        self.call = None

    def __getattr__(self, name):
        def f(*a, **k):
            assert self.call is None
            self.call = (name, a, k)
            return None
        return f


class Sched:
    def __init__(self):
        self.ops = {e: [] for e in ENGS}
        self.dsems = []
        self.bar_res = {e: Res("bar_" + e) for e in ENGS}

    def dsem(self):
        d = DSem(len(self.dsems))
        self.dsems.append(d)
        return d

    def _add(self, eng, fn, reads, writes, is_dma, dsem):
        op = Op()
        op.eng = eng
        rec = _Rec()
        fn(rec)
        op.call = rec.call
        op.is_dma = is_dma
        op.need_inc = False
        op.seq = None
        op.dsem = dsem
        deps = {}
        for t in reads:
            w = t.w
            if w is not None:
                deps[id(w)] = w
        for t in writes:
            w = t.w
            if w is not None and (w.is_dma or w.eng != eng):
                deps[id(w)] = w
            for r in t.readers:
                if r.is_dma or r.eng != eng:
                    deps[id(r)] = r
        op.deps = list(deps.values())
        if is_dma:
            dsem.val += 16
            op.dval = dsem.val
        else:
            op.dval = None
        for t in writes:
            t.w = op
            t.readers = []
        for t in reads:
            rs = [r for r in t.readers if r.is_dma or r.eng != eng]
            rs.append(op)
            t.readers = rs
        self.ops[eng].append(op)
        return op

    def op(self, eng, fn, reads=(), writes=()):
        return self._add(eng, fn, reads, writes, False, None)

    def dma(self, eng, fn, dsem, reads=(), writes=()):
        return self._add(eng, fn, reads, list(writes) + [dsem.res], True, dsem)

    def barrier(self):
        lasts = []
        for e in ENGS:
            for op in reversed(self.ops[e]):
                if not op.is_dma:
                    lasts.append(op)
                    break
        for d in self.dsems:
            if d.res.w is not None:
                lasts.append(d.res.w)
        for e in ENGS:
            op = self.op(e, lambda g: g.nop())
            op.deps = [o for o in lasts if o.is_dma or o.eng != e]

    def emit(self, nc):
        for e in ENGS:
            for op in self.ops[e]:
                for d in op.deps:
                    if not d.is_dma:
                        d.need_inc = True
        for e in ENGS:
            c = 0
            for op in self.ops[e]:
                if (not op.is_dma) and op.need_inc:
                    c += 1
                    op.seq = c
        with contextlib.ExitStack() as st:
            esem = {e: st.enter_context(nc.semaphore("s_" + e)) for e in ENGS}
            for d in self.dsems:
                d.handle = st.enter_context(nc.semaphore("d%d" % d.idx))
            block = st.enter_context(nc.Block())
            hmap = {"pe": "tensor", "act": "scalar", "dve": "vector", "pool": "gpsimd", "sp": "sync"}

            def make(e):
                ops = self.ops[e]

                def body(eng):
                    waited = {}
                    for op in ops:
                        need = {}
                        for d in op.deps:
                            if d.is_dma:
                                k = ("d", d.dsem.idx)
                                h = d.dsem.handle
                                v = d.dval
                            else:
                                k = ("e", d.eng)
                                h = esem[d.eng]
                                v = d.seq
                            if waited.get(k, 0) >= v:
                                continue
                            if k not in need or need[k][1] < v:
                                need[k] = (h, v)
                        for k, (h, v) in need.items():
                            eng.wait_ge(h, v)
                            waited[k] = v
                        name, a, k = op.call
                        inst = getattr(eng, name)(*a, **k)
                        if op.is_dma:
                            inst.then_inc(op.dsem.handle, 16)
                        elif op.need_inc:
                            inst.then_inc(esem[e], 1)
                    if e == "sp":
                        for d in self.dsems:
                            if d.val > 0:
                                eng.wait_ge(d.handle, d.val)
                return body

            for e in ENGS:
                getattr(block, hmap[e])(make(e))


class Ring:
    def __init__(self, S, aps, with_dsem=False):
        self.slots = [(ap, Res(), S.dsem() if with_dsem else None) for ap in aps]
        self.i = 0

    def next(self):
        s = self.slots[self.i % len(self.slots)]
        self.i += 1
        return s


def head_info(h):
    if h < HA:
        base, n, i, typ = 0, HA, h, "A"
    elif h < HA + HB:
        base, n, i, typ = 3 * HA * HD, HB, h - HA, "B"
    else:
        base, n, i, typ = 3 * HA * HD + 3 * HB * HD, HC, h - HA - HB, "C"
    return typ, i, base + i * HD, base + n * HD + i * HD, base + 2 * n * HD + i * HD


def build_nc(dbg=None, n_layers=DEPTH):
    dbg = dbg or {}
    nc = bass.Bass("TRN2", target_bir_lowering=False)

    def din(name, shape, dt=F32):
        return nc.dram_tensor(name, list(shape), dt, kind="ExternalInput").ap()

    x_d = din("x", [T, D])
    w_in_d = din("w_in", [DEPTH, D, N_IN])
    w_o_d = din("w_o", [DEPTH, D, D])
    w_gu_d = din("w_gu", [DEPTH, D, 2 * DFF])
    w_down_d = din("w_down", [DEPTH, DFF, D])
    gmix_d = din("gmixT", [DEPTH, 128, 16])
    gffn_d = din("gffnT", [DEPTH, 128, 16])
    ghead_d = din("gheadT", [DEPTH, 128, 16])
    bf_d = din("bfb", [DEPTH, 128, HC])
    cw_d = din("cwT", [DEPTH, 128, 88 * 3])
    cb_d = din("cbT", [DEPTH, 128, 88])
    gfin_d = din("gfinb", [128, D])
    y_d = nc.dram_tensor("y", [T, D], F32, kind="ExternalOutput").ap()
    xres_d = nc.dram_tensor("xres", [T, D], F32).ap()
    oT_d = nc.dram_tensor("oTd", [NH, 128, T], BF16).ap()
    dbg_out = {}
    for k, (shape, dt) in dbg.items():
        dbg_out[k] = nc.dram_tensor("dbg_" + k, list(shape), dt, kind="ExternalOutput").ap()

    S = Sched()
    st = contextlib.ExitStack()
    with st:
        def sb(name, free, dt=F32):
            return st.enter_context(nc.sbuf_tensor(name, [128] + list(free), dt))

        BIGA = sb("BIGA", [16384], F32)
        BIGB = sb("BIGB", [22528], F32)
        XT = sb("XT", [4096], F32)
        WGU = sb("WGU", [6144], F32)
        identb = sb("identb", [128], BF16)
        identf = sb("identf", [128], F32)
        maskLEb = sb("maskLEb", [128], BF16)
        triLEf = sb("triLEf", [128], F32)
        maskLTf = sb("maskLTf", [128], F32)
        onesf = sb("onesf", [128], F32)
        negLEf = sb("negLEf", [128], F32)
        gmix = sb("gmix", [DEPTH, 16], F32)
        gffn = sb("gffn", [DEPTH, 16], F32)
        ghead = sb("ghead", [DEPTH, 16], F32)
        bfb = sb("bfb_s", [DEPTH, HC], F32)
        cw = sb("cw", [DEPTH, 88 * 3], F32)
        cb = sb("cb", [DEPTH, 88], F32)
        carry = sb("carry", [88 * 2], F32)
        stat = sb("stat", [64], F32)
        cneg = sb("cneg", [16 * HC], F32)
        ccar = sb("ccar", [17 * HC], F32)
        biasC = sb("biasC", [4 * 16 * HC], F32)
        lf = sb("lf", [16 * HC], F32)
        epsb = sb("epsb", [1], F32)
        oneb = sb("oneb", [1], F32)
        mhalf = sb("mhalf", [1], F32)

        pS = st.enter_context(nc.psum_tensor("pS", [128, 2, 512], F32))
        pO = st.enter_context(nc.psum_tensor("pO", [128, 4, 512], F32))
        pX = st.enter_context(nc.psum_tensor("pX", [128, 512], F32))
        pT = st.enter_context(nc.psum_tensor("pT", [128, 512], F32))
        r_pS = [Res("pS0"), Res("pS1")]
        r_pO = [Res("pO%d" % i) for i in range(4)]
        r_pX = Res("pX")
        r_pT = Res("pT")

        def vw(region, off_b, nbytes, dt, pat=None, **kw):
            a = region[:, off_b // 4:(off_b + nbytes) // 4]
            if dt != F32:
                a = a.bitcast(dt)
            if pat:
                a = a.rearrange(pat, **kw)
            return a

        dcnt = [0]

        def dbg_dump(key, ap_sb, res, dram_slice=None):
            if key not in dbg_out:
                return
            d = S.dsem()
            tgt = dbg_out[key] if dram_slice is None else dram_slice(dbg_out[key])
            S.dma("sp", lambda g: g.dma_start(out=tgt, in_=ap_sb), d, reads=res)

        r_c = Res("consts")
        r_par = Res("params")
        dpar = S.dsem()
        for (dst, src) in ((gmix, gmix_d), (gffn, gffn_d), (ghead, ghead_d), (bfb, bf_d), (cw, cw_d), (cb, cb_d)):
            for l in range(DEPTH):
                S.dma("sp", lambda g, dst=dst, src=src, l=l: g.dma_start(out=dst[:, l, :], in_=src[l]), dpar,
                      writes=[r_par])
        tmpi = vw(XT, 0, 512, I32)
        tmpf = vw(XT, 512, 512, F32)
        S.op("pool", lambda g: g.iota(tmpi, pattern=[[1, 128]], base=0, channel_multiplier=-1), writes=[r_c])
        S.op("dve", lambda g: g.tensor_copy(out=tmpf, in_=tmpi), reads=[r_c], writes=[r_c])
        S.op("dve", lambda g: g.tensor_single_scalar(out=identf[:], in_=tmpf, scalar=0.0, op=ALU.is_equal), reads=[r_c], writes=[r_c])
        S.op("dve", lambda g: g.tensor_copy(out=identb[:], in_=identf[:]), reads=[r_c], writes=[r_c])
        S.op("dve", lambda g: g.tensor_single_scalar(out=triLEf[:], in_=tmpf, scalar=0.0, op=ALU.is_ge), reads=[r_c], writes=[r_c])
        S.op("dve", lambda g: g.tensor_copy(out=maskLEb[:], in_=triLEf[:]), reads=[r_c], writes=[r_c])
        S.op("dve", lambda g: g.tensor_single_scalar(out=maskLTf[:], in_=tmpf, scalar=0.0, op=ALU.is_lt), reads=[r_c], writes=[r_c])
        S.op("dve", lambda g: g.memset(onesf[:], 1.0), writes=[r_c])
        S.op("dve", lambda g: g.tensor_scalar(out=negLEf[:], in0=triLEf[:], scalar1=-1.0, scalar2=30000.0, op0=ALU.add, op1=ALU.mult),
             reads=[r_c], writes=[r_c])
        S.op("dve", lambda g: g.memset(epsb[:], EPS), writes=[r_c])
        S.op("dve", lambda g: g.memset(oneb[:], 1.0), writes=[r_c])
        S.op("dve", lambda g: g.memset(mhalf[:], -0.5), writes=[r_c])
        S.op("dve", lambda g: g.memset(carry[:], 0.0), writes=[r_c])

        o = 0
        QTs = []
        KTs = []
        for i in range(2):
            QTs.append(vw(BIGB, o, 4096, BF16)); o += 4096
            KTs.append(vw(BIGB, o, 4096, BF16)); o += 4096
        Vts = []
        for i in range(2):
            Vts.append(vw(BIGB, o, 16 * 132 * 2, BF16, "p (c n) -> p c n", n=132)); o += 16 * 132 * 2
        LM0 = vw(BIGB, o, 8192, F32); o += 8192
        LH = vw(BIGB, o, 8192, F32); o += 8192
        LHi = LH.bitcast(I32)
        sfs = [vw(BIGB, o + 2048 * i, 2048, F32) for i in range(2)]; o += 4096
        pts = [vw(BIGB, o + 1024 * i, 1024, BF16) for i in range(3)]; o += 3072
        ebuf = vw(BIGB, o, 8256, F32); o += 8256
        cumbuf = vw(BIGB, o, 8256, F32); o += 8256
        abuf = vw(BIGB, o, 4096, BF16); o += 4096
        aTb = vw(BIGB, o, 4096, BF16, "p (c n) -> p c n", n=128); o += 4096
        oThs = [vw(BIGB, o + 4096 * i, 4096, BF16) for i in range(2)]; o += 8192
        obuf = vw(BIGB, o, 1024, BF16, "p (c n) -> p c n", n=128); o += 1024
        junk = vw(BIGB, o, 4096, BF16); o += 4096
        assert o <= 22528 * 4, o
        hnT = vw(BIGA, 0, 65536, BF16, "p (c n) -> p c n", n=T)
        r_hnT = Res("hnT")
        wslots = [vw(WGU, 4096 * i, 4096, BF16, "p (c n) -> p c n", n=128) for i in range(6)]
        wring = Ring(S, wslots, with_dsem=True)

        r_lm = Res("LM0")

        def build_masks():
            t_i = LHi
            ti2 = ebuf[:, 0:2048].bitcast(I32)
            tf_d = cumbuf[:, 1:2049]
            tf_b = vw(BIGB, 64896, 8192, F32)
            tf_c = vw(BIGB, 73088, 8192, F32)
            R = [r_lm]

            def dv(fn):
                S.op("dve", fn, reads=R, writes=R)
            S.op("pool", lambda g: g.iota(t_i, pattern=[[1, 2048]], base=0, channel_multiplier=-1), reads=R, writes=R)
            dv(lambda g: g.tensor_copy(out=tf_d, in_=t_i))
            dv(lambda g: g.tensor_scalar(out=tf_b, in0=tf_d, scalar1=0.0, scalar2=None, op0=ALU.is_ge))
            dv(lambda g: g.tensor_scalar(out=tf_c, in0=tf_d, scalar1=128.0, scalar2=None, op0=ALU.is_le))
            dv(lambda g: g.tensor_tensor(out=LM0, in0=tf_b, in1=tf_c, op=ALU.mult))
            for (div, lim) in ((4.0, 512.0), (16.0, None)):
                dv(lambda g, div=div: g.tensor_scalar(out=tf_c, in0=tf_d, scalar1=1.0 / div, scalar2=None, op0=ALU.mult))
                dv(lambda g: g.tensor_copy(out=ti2, in_=tf_c))
                dv(lambda g: g.tensor_copy(out=ebuf[:, 0:2048], in_=ti2))
                dv(lambda g: g.tensor_tensor(out=tf_c, in0=tf_c, in1=ebuf[:, 0:2048], op=ALU.is_equal))
                dv(lambda g: g.tensor_tensor(out=tf_c, in0=tf_c, in1=tf_b, op=ALU.mult))
                if lim is not None:
                    dv(lambda g, lim=lim: g.scalar_tensor_tensor(out=tf_c, in0=tf_d, scalar=lim, in1=tf_c, op0=ALU.is_le, op1=ALU.mult))
                dv(lambda g: g.tensor_tensor(out=LM0, in0=LM0, in1=tf_c, op=ALU.add))
            dv(lambda g: g.tensor_scalar(out=tf_b, in0=LM0, scalar1=1.0, scalar2=-1.0, op0=ALU.min, op1=ALU.add))
            dv(lambda g: g.tensor_scalar(out=LM0, in0=LM0, scalar1=1e-18, scalar2=None, op0=ALU.max))
            S.op("act", lambda g: g.activation(out=LM0, in_=LM0, func=AF.Ln), reads=R, writes=R)
            dv(lambda g: g.scalar_tensor_tensor(out=LM0, in0=tf_b, scalar=1000.0, in1=LM0, op0=ALU.mult, op1=ALU.add))
            dv(lambda g: g.tensor_scalar(out=LM0, in0=LM0, scalar1=1.0 / SCALE, scalar2=None, op0=ALU.mult))
            for i in range(2):
                dv(lambda g, i=i: g.memset(Vts[i][:, :, 128:129], 1.0))
            dv(lambda g: g.memset(cumbuf[:, 0:1], 0.0))
            S.barrier()

        S.barrier()

        xt_slots = [(vw(XT, 8192 * i, 8192, F32), Res("xt%d" % i), S.dsem()) for i in range(2)]
        xs_b = junk
        r_xs = Res("xs")
        r_stat = Res("stat")
        ptr_views = [pO[:, 0:2, :].rearrange("p a b -> p (a b)").bitcast(BF16).rearrange("p (c n) -> p c n", n=128),
                     pO[:, 2:4, :].rearrange("p a b -> p (a b)").bitcast(BF16).rearrange("p (c n) -> p c n", n=128)]

        xs_bufs = [junk, vw(BIGB, 73088, 4096, BF16)]
        r_xss = [r_xs, Res("xs2")]
        r_stats = [r_stat, Res("stat2")]

        def norm_transpose(src_ap, r_src, i, gvec, dstT, r_dst, col0):
            k = i % 2
            xt, r_xt, d_xt = xt_slots[k]
            xsb, r_x_s, r_st, sc = xs_bufs[k], r_xss[k], r_stats[k], 40 * k
            S.dma("sp", lambda g: g.dma_start(out=xt, in_=src_ap), d_xt, reads=[r_src], writes=[r_xt])
            S.op("act", lambda g: g.activation(out=xsb, in_=xt, func=AF.Square, accum_out=stat[:, sc:sc + 1]),
                 reads=[r_xt], writes=[r_x_s, r_st])
            S.op("act", lambda g: g.activation(out=stat[:, sc + 1:sc + 2], in_=stat[:, sc:sc + 1], func=AF.Ln, scale=1.0 / D, bias=epsb[:]),
                 reads=[r_st], writes=[r_st])
            S.op("act", lambda g: g.activation(out=stat[:, sc + 2:sc + 3], in_=stat[:, sc + 1:sc + 2], func=AF.Exp, scale=-0.5),
                 reads=[r_st], writes=[r_st])
            S.op("dve", lambda g: g.tensor_scalar(out=xsb, in0=xt, scalar1=stat[:, sc + 2:sc + 3], scalar2=None, op0=ALU.mult),
                 reads=[r_xt, r_st], writes=[r_x_s])
            pv = ptr_views[k]
            rp = [r_pO[2 * k], r_pO[2 * k + 1]]
            for c in range(16):
                S.op("pe", lambda g, c=c: g.transpose(out=pv[:, c, :], in_=xsb[:, c * 128:(c + 1) * 128], identity=identb[:]),
                     reads=[r_x_s, r_c], writes=rp)
            S.op("dve", lambda g: g.tensor_tensor(out=dstT[:, :, col0:col0 + 128], in0=pv,
                                                  in1=gvec.unsqueeze(2).broadcast_to([128, 16, 128]), op=ALU.mult),
                 reads=rp + [r_par], writes=[r_dst])

        r_x = [Res("xrow%d" % i) for i in range(NT)]
        r_oTd = [Res("oTd%d" % h) for h in range(NH)]
        slopes = [2.0 ** (-8.0 * (i + 1) / HA) for i in range(HA)]

        for l in range(n_layers):
            src_d = x_d if l == 0 else xres_d
            for i in range(NT):
                norm_transpose(src_d[i * 128:(i + 1) * 128, :], r_x[i], i, gmix[:, l, :], hnT, r_hnT, i * 128)
            if l == 0:
                dbg_dump("hnT", hnT, [r_hnT])
            wf, r_wf, d_wf = wring.next()
            wfv = wf[:, :, 0:HC]
            S.dma("pool", lambda g: g.dma_start(out=wfv, in_=w_in_d[l].rearrange("(c p) n -> p c n", p=128)[:, :, 6144:6144 + HC]),
                  d_wf, writes=[r_wf])
            pfc = pX[:, 0:16 * HC].rearrange("p (b h) -> p b h", h=HC)
            for tb in range(NT):
                for c in range(16):
                    S.op("pe", lambda g, tb=tb, c=c: g.matmul(pfc[:, tb, :], lhsT=hnT[:, c, tb * 128:(tb + 1) * 128],
                                                               rhs=wfv[:, c, :], start=(c == 0), stop=(c == 15)),
                         reads=[r_hnT, r_wf], writes=[r_pX])
            r_lf = Res("lf")
            lf3 = lf[:].rearrange("p (b h) -> p b h", h=HC)
            S.op("dve", lambda g: g.tensor_tensor(out=lf3, in0=pfc, in1=bfb[:, l, :].unsqueeze(1).broadcast_to([128, 16, HC]), op=ALU.add),
                 reads=[r_pX, r_par], writes=[r_lf])
            S.op("act", lambda g: g.activation(out=lf[:], in_=lf[:], func=AF.Exp, scale=-1.0), reads=[r_lf], writes=[r_lf])
            S.op("act", lambda g: g.activation(out=lf[:], in_=lf[:], func=AF.Ln, bias=oneb[:]), reads=[r_lf], writes=[r_lf])
            pc1 = pT[:, 0:16 * HC]
            pc2 = pT[:, 128:128 + 16 * HC]
            S.op("pe", lambda g: g.matmul(pc1, lhsT=triLEf[:], rhs=lf[:], start=True, stop=True), reads=[r_lf, r_c], writes=[r_pT])
            S.op("pe", lambda g: g.matmul(pc2, lhsT=onesf[:], rhs=lf[:], start=True, stop=True), reads=[r_lf, r_c], writes=[r_pT])
            r_cc = Res("ccar")
            cc3 = ccar[:].rearrange("p (b h) -> p b h", h=HC)
            cn3 = cneg[:].rearrange("p (b h) -> p b h", h=HC)
            pc2v = pc2.rearrange("p (b h) -> p b h", h=HC)
            pc1v = pc1.rearrange("p (b h) -> p b h", h=HC)
            S.op("dve", lambda g: g.memset(ccar[:, 0:HC], 0.0), writes=[r_cc])
            for b in range(16):
                S.op("dve", lambda g, b=b: g.tensor_tensor(out=cc3[:, b + 1, :], in0=pc2v[:, b, :], in1=cc3[:, b, :], op=ALU.add),
                     reads=[r_pT, r_cc], writes=[r_cc])
            S.op("dve", lambda g: g.tensor_tensor(out=cn3, in0=pc1v, in1=cc3[:, 0:16, :], op=ALU.add), reads=[r_pT, r_cc], writes=[r_cc])
            if l == 0:
                dbg_dump("cneg", cneg[:], [r_cc])

            S.barrier()
            build_masks()
            if l == 0:
                dbg_dump("LM0", LM0, [r_lm])
            r_LH = Res("LH")
            sf_ring = Ring(S, sfs)
            pt_ring = Ring(S, pts)
            r_ebuf = Res("ebuf")
            r_cum = Res("cum")
            r_abuf = Res("abuf")
            r_aT = Res("aT")
            r_ob = Res("obuf")
            r_junk = r_xs
            d_oT = [S.dsem(), S.dsem()]
            r_oTh = [Res("oTh0"), Res("oTh1")]
            ps_i = [0]

            def ps_next():
                k = ps_i[0] % 2
                ps_i[0] += 1
                return pS[:, k, :], r_pS[k]

            ev_i = [0]

            evac_pref = [None]

            def evac(out, in_, reads, writes):
                ev_i[0] += 1
                if evac_pref[0] == "act" or (evac_pref[0] is None and ev_i[0] % 2):
                    S.op("act", lambda g: g.activation(out=out, in_=in_, func=AF.Copy), reads=reads, writes=writes)
                else:
                    S.op("dve", lambda g: g.tensor_copy(out=out, in_=in_), reads=reads, writes=writes)

            pTb = pT[:].bitcast(BF16)[:, 0:512].rearrange("p (c n) -> p c n", n=128)

            def epilogue(l, h, nblk, pviews, rviews, qcol0, with_den, oTh, r_oT):
                nb = nblk
                if with_den:
                    for j in range(nb):
                        S.op("dve", lambda g, j=j: g.reciprocal(out=stat[:, 8 + j:9 + j], in_=pviews[j][:, 128:129]),
                             reads=[rviews[j]], writes=[r_stat])
                for j in range(nb):
                    if with_den:
                        S.op("act", lambda g, j=j: g.activation(out=junk[:, 0:128], in_=pviews[j][:, 0:128], func=AF.Square,
                                                               scale=stat[:, 8 + j:9 + j], accum_out=stat[:, 12 + j:13 + j]),
                             reads=[rviews[j], r_stat], writes=[r_junk, r_stat])
                    else:
                        S.op("act", lambda g, j=j: g.activation(out=junk[:, 0:128], in_=pviews[j][:, 0:128], func=AF.Square,
                                                               accum_out=stat[:, 12 + j:13 + j]),
                             reads=[rviews[j]], writes=[r_junk, r_stat])
                S.op("act", lambda g: g.activation(out=stat[:, 16:16 + nb], in_=stat[:, 12:12 + nb], func=AF.Ln, scale=1.0 / HD, bias=epsb[:]),
                     reads=[r_stat], writes=[r_stat])
                S.op("act", lambda g: g.activation(out=stat[:, 20:20 + nb], in_=stat[:, 16:16 + nb], func=AF.Exp, scale=-0.5),
                     reads=[r_stat], writes=[r_stat])
                if with_den:
                    S.op("dve", lambda g: g.tensor_tensor(out=stat[:, 20:20 + nb], in0=stat[:, 20:20 + nb], in1=stat[:, 8:8 + nb], op=ALU.mult),
                         reads=[r_stat], writes=[r_stat])
                for j in range(nb):
                    S.op("dve", lambda g, j=j: g.tensor_scalar(out=obuf[:, j, :], in0=pviews[j][:, 0:128], scalar1=stat[:, 20 + j:21 + j],
                                                               scalar2=None, op0=ALU.mult),
                         reads=[rviews[j], r_stat], writes=[r_ob])
                for j in range(nb):
                    S.op("pe", lambda g, j=j: g.transpose(out=pTb[:, j, :], in_=obuf[:, j, :], identity=identb[:]),
                         reads=[r_ob, r_c], writes=[r_pT])
                S.op("act", lambda g: g.activation(out=oTh[:, qcol0:qcol0 + 128 * nb],
                                                   in_=pTb[:, 0:nb, :].rearrange("p c n -> p (c n)"), func=AF.Copy,
                                                   scale=ghead[:, l, h:h + 1]),
                     reads=[r_pT, r_par], writes=[r_oT])

            r_QTs = [Res("QT0"), Res("QT1")]
            r_KTs = [Res("KT0"), Res("KT1")]
            r_Vts = [Res("Vt0"), Res("Vt1")]
            ebufs = [ebuf, vw(BIGB, 24832, 8256, F32)]
            cumbufs = [cumbuf, vw(BIGB, 24832 + 8256, 8256, F32)]
            abufs = [abuf, vw(XT, 0, 4096, BF16)]
            r_ebufs = [r_ebuf, Res("ebuf2")]
            r_cums = [r_cum, Res("cum2")]
            r_abufs = [r_abuf, Res("abuf2")]
            r_statB = [Res("negtot0"), Res("negtot1")]
            r_statE = Res("statE")
            r_ab3 = Res("abuf3")
            r_aT2 = Res("aT2")

            def proj_gen(h):
                typ, hi, cq, ck, cv = head_info(h)
                QT, KT, Vt = QTs[h % 2], KTs[h % 2], Vts[h % 2]
                r_QT, r_KT, r_Vt = r_QTs[h % 2], r_KTs[h % 2], r_Vts[h % 2]
                wv_in = w_in_d[l].rearrange("(c p) n -> p c n", p=128)
                ws = []
                for col in (cq, ck, cv):
                    w_ap, r_w, d_w = wring.next()
                    S.dma("pool", lambda g, w_ap=w_ap, col=col: g.dma_start(out=w_ap, in_=wv_in[:, :, col:col + 128]), d_w, writes=[r_w])
                    ws.append((w_ap, r_w))
                yield
                for (w_ap, r_w), dst, r_dst in ((ws[0], QT, r_QT), (ws[1], KT, r_KT)):
                    for tg in range(4):
                        ps, r_ps = proj_slot()
                        for c in range(16):
                            S.op("pe", lambda g, ps=ps, w_ap=w_ap, c=c, tg=tg: g.matmul(ps, lhsT=w_ap[:, c, :], rhs=hnT[:, c, tg * 512:(tg + 1) * 512],
                                                                                        start=(c == 0), stop=(c == 15)),
                                 reads=[r_w, r_hnT], writes=[r_ps])
                            if c % 4 == 3 and c != 15 and r_ps is r_pX:
                                yield
                        evac(dst[:, tg * 512:(tg + 1) * 512], ps, [r_ps], [r_dst])
                        yield
                w_ap, r_w = ws[2]
                for tg in range(4):
                    ps, r_ps = proj_slot()
                    for tb in range(4):
                        t0 = (tg * 4 + tb) * 128
                        for c in range(16):
                            S.op("pe", lambda g, ps=ps, w_ap=w_ap, c=c, tb=tb, t0=t0: g.matmul(ps[:, tb * 128:(tb + 1) * 128], lhsT=hnT[:, c, t0:t0 + 128],
                                                                                              rhs=w_ap[:, c, :], start=(c == 0), stop=(c == 15)),
                                 reads=[r_w, r_hnT], writes=[r_ps])
                        if tb != 3 and r_ps is r_pX:
                            yield
                    evac(Vt[:, tg * 4:(tg + 1) * 4, 0:128], ps.rearrange("p (c n) -> p c n", n=128), [r_ps], [r_Vt])
                    yield
                if l == 0 and h in (0, 6, 11):
                    dbg_dump("QT%d" % h, QT, [r_QT])
                    dbg_dump("KT%d" % h, KT, [r_KT])
                    dbg_dump("Vt%d" % h, Vt, [r_Vt])

            LHs = [LH, ebuf[:, 0:2048]]
            r_LHs = [r_LH, Res("LHb")]
            lh_built = set()

            def lh_build_gen(hh):
                lh_built.add(hh)
                typ_, hi_, _, _, _ = head_info(hh)
                L_, r_L = LHs[hh % 2], r_LHs[hh % 2]
                extra = [r_ebufs[1], r_cums[1]] if hh % 2 == 0 else [r_ebufs[0]]
                if typ_ == "A":
                    Li = L_.bitcast(I32)
                    S.op("pool", lambda g: g.iota(Li, pattern=[[1, 2048]], base=0, channel_multiplier=-1), writes=[r_L] + extra)
                    yield
                    S.op("dve", lambda g: g.tensor_copy(out=cumbuf[:, 1:2049], in_=Li), reads=[r_L], writes=[r_cum])
                    yield
                    S.op("dve", lambda g, sl=slopes[hi_]: g.scalar_tensor_tensor(out=L_, in0=cumbuf[:, 1:2049], scalar=-sl / SCALE, in1=LM0,
                                                                                  op0=ALU.mult, op1=ALU.add),
                         reads=[r_cum, r_lm], writes=[r_L])
                    yield
                else:
                    dgs = [abuf[:, 256 * i:256 * (i + 1)].bitcast(F32) for i in range(8)]
                    for bg in range(4):
                        ps, r_ps = ps_next()
                        for b4 in range(4):
                            b = bg * 4 + b4
                            dg, rdg = dgs[b % 8], r_dgs[b % 8]
                            S.op("dve", lambda g, dg=dg, b=b: g.tensor_scalar(out=dg, in0=identf[:], scalar1=cn3[:, b, hi_:hi_ + 1], scalar2=None,
                                                                             op0=ALU.mult),
                                 reads=[r_c, r_cc], writes=[rdg, r_abuf])
                            S.op("pe", lambda g, dg=dg, ps=ps, b4=b4: g.matmul(ps[:, b4 * 128:(b4 + 1) * 128], lhsT=onesf[:], rhs=dg, start=True, stop=True),
                                 reads=[rdg, r_c], writes=[r_ps])
                        S.op("act", lambda g, ps=ps, bg=bg: g.activation(out=L_[:, bg * 512:(bg + 1) * 512], in_=ps, func=AF.Copy, scale=-1.0 / SCALE),
                             reads=[r_ps], writes=[r_L] + extra)
                        yield

            r_dgs = [Res("dg%d" % i) for i in range(8)]
            stage = vw(XT, 12288, 4 * 132 * 4, F32, "p (c n) -> p c n", n=132)
            r_stage = Res("stage")

            def epilogue_ac_1(l, h, qg):
                S.op("act", lambda g: g.activation(out=stage[:, :, 0:129], in_=pO[:, :, 0:129], func=AF.Copy), reads=r_pO, writes=[r_stage])
                S.op("dve", lambda g: g.reciprocal(out=stat[:, 8:12].unsqueeze(2), in_=stage[:, :, 128:129]), reads=[r_stage], writes=[r_stat])
                for j in range(4):
                    S.op("act", lambda g, j=j: g.activation(out=junk[:, 0:128], in_=stage[:, j, 0:128], func=AF.Square,
                                                           scale=stat[:, 8 + j:9 + j], accum_out=stat[:, 12 + j:13 + j]),
                         reads=[r_stage, r_stat], writes=[r_junk, r_stat])
                S.op("act", lambda g: g.activation(out=stat[:, 16:20], in_=stat[:, 12:16], func=AF.Ln, scale=1.0 / HD, bias=epsb[:]),
                     reads=[r_stat], writes=[r_stat])
                S.op("act", lambda g: g.activation(out=stat[:, 20:24], in_=stat[:, 16:20], func=AF.Exp, scale=-0.5), reads=[r_stat], writes=[r_stat])
                S.op("dve", lambda g: g.tensor_tensor(out=stat[:, 20:24], in0=stat[:, 20:24], in1=stat[:, 8:12], op=ALU.mult), reads=[r_stat], writes=[r_stat])
                S.op("dve", lambda g: g.tensor_tensor(out=obuf, in0=stage[:, :, 0:128], in1=stat[:, 20:24].unsqueeze(2).broadcast_to([128, 4, 128]), op=ALU.mult),
                     reads=[r_stage, r_stat], writes=[r_ob])

            def epilogue_ac_2(l, h, qg, oTh, r_oT):
                for j in range(4):
                    S.op("pe", lambda g, j=j: g.transpose(out=pTb[:, j, :], in_=obuf[:, j, :], identity=identb[:]), reads=[r_ob, r_c], writes=[r_pT])
                S.op("act", lambda g: g.activation(out=oTh[:, qg * 512:(qg + 1) * 512], in_=pTb.rearrange("p c n -> p (c n)"), func=AF.Copy,
                                                   scale=ghead[:, l, h:h + 1]),
                     reads=[r_pT, r_par], writes=[r_oT])

            pslot_i = [0]

            def proj_slot():
                k = pslot_i[0] % 2
                pslot_i[0] += 1
                return (pX[:, :], r_pX) if k == 0 else (pT[:, :], r_pT)

            cur_gen = [None]

            def pump():
                gnr = cur_gen[0]
                if gnr is None:
                    return
                try:
                    next(gnr)
                except StopIteration:
                    cur_gen[0] = None

            def drain_gen():
                while cur_gen[0] is not None:
                    pump()

            cur_gen[0] = proj_gen(0)
            drain_gen()
            for h in range(NH):
                typ, hi, cq, ck, cv = head_info(h)
                QT, KT, Vt = QTs[h % 2], KTs[h % 2], Vts[h % 2]
                r_QT, r_KT, r_Vt = r_QTs[h % 2], r_KTs[h % 2], r_Vts[h % 2]
                oTh = oThs[h % 2]
                r_oT = r_oTh[h % 2]
                evac_pref[0] = "act" if typ == "B" else None
                if h + 1 < NH:
                    cur_gen[0] = proj_gen(h + 1)
                    pump()

                if typ in ("A", "C"):
                    if h not in lh_built:
                        for _ in lh_build_gen(h):
                            pass
                    LHc, r_LHc = LHs[h % 2], r_LHs[h % 2]
                    nxt_gen = None
                    if h + 1 < NH and head_info(h + 1)[0] == typ:
                        nxt_gen = lh_build_gen(h + 1)
                    steps = [(qg, m) for qg in range(4) for m in range(4 * qg + 4)]
                    state = {}

                    def do_s(i):
                        qg, m = steps[i]
                        c0 = max(0, m - 4 * qg) * 128
                        ps, r_ps = ps_next()
                        S.op("pe", lambda g: g.matmul(ps[:, c0:512], lhsT=KT[:, m * 128:(m + 1) * 128], rhs=QT[:, qg * 512 + c0:(qg + 1) * 512],
                                                      start=True, stop=True),
                             reads=[r_KT, r_QT], writes=[r_ps])
                        pt, r_pt, _ = pt_ring.next()
                        sf, r_sf, _ = sf_ring.next()
                        s0 = qg * 512 - m * 128 if typ == "A" else qg * 512
                        S.op("dve", lambda g: g.tensor_tensor(out=sf[:, c0:512], in0=ps[:, c0:512], in1=LHc[:, s0 + c0:s0 + 512], op=ALU.add),
                             reads=[r_ps, r_LHc], writes=[r_sf])
                        if typ == "A":
                            S.op("act", lambda g: g.activation(out=pt[:, c0:512], in_=sf[:, c0:512], func=AF.Exp, scale=SCALE),
                                 reads=[r_sf], writes=[r_pt])
                        else:
                            if m >= 4 * qg:
                                S.op("dve", lambda g: g.tensor_tensor(out=sf[:, c0:c0 + 128], in0=sf[:, c0:c0 + 128], in1=negLEf[:], op=ALU.add),
                                     reads=[r_sf, r_c], writes=[r_sf])
                            S.op("act", lambda g: g.activation(out=pt[:, c0:512], in_=sf[:, c0:512], func=AF.Exp, scale=SCALE,
                                                               bias=cn3[:, m, hi:hi + 1]),
                                 reads=[r_sf, r_cc], writes=[r_pt])
                        state[i] = (pt, r_pt, c0)

                    def do_av(i):
                        qg, m = steps[i]
                        pt, r_pt, c0 = state.pop(i)
                        for j in range(c0 // 128, 4):
                            S.op("pe", lambda g, j=j: g.matmul(pO[:, j, 0:129], lhsT=pt[:, j * 128:(j + 1) * 128], rhs=Vt[:, m, 0:129],
                                                               start=(m == 0), stop=(m == 4 * qg + j)),
                                 reads=[r_pt, r_Vt], writes=[r_pO[j]])
                        if m == 4 * qg + 3:
                            epilogue_ac_1(l, h, qg)
                            deferred.append((i + 3, qg))
                            pump()

                    deferred = []
                    do_s(0)
                    for i in range(len(steps)):
                        if i + 1 < len(steps):
                            do_s(i + 1)
                        pump()
                        do_av(i)
                        while deferred and deferred[0][0] <= i:
                            epilogue_ac_2(l, h, deferred.pop(0)[1], oTh, r_oT)
                        if nxt_gen is not None and i >= 6 and i % 4 == 2:
                            try:
                                next(nxt_gen)
                            except StopIteration:
                                nxt_gen = None
                    while deferred:
                        epilogue_ac_2(l, h, deferred.pop(0)[1], oTh, r_oT)
                    if nxt_gen is not None:
                        for _ in nxt_gen:
                            pass
                else:
                    paT = pO[:, 0:2, :].rearrange("p a b -> p (a b)").bitcast(BF16).rearrange("p (c n) -> p c n", n=128)
                    if hi == 0:
                        S.op("dve", lambda g: g.memset(cumbufs[1][:, 0:1], 0.0), reads=[r_LH, r_lm], writes=[r_cums[1], r_LH, r_LHs[1], r_lm])

                    abufs3 = [abuf, vw(XT, 0, 4096, BF16), vw(XT, 4096, 4096, BF16)]
                    r_abufs3 = [r_abuf, r_abufs[1], r_ab3]
                    aTbs = [aTb, vw(XT, 8192, 4096, BF16, "p (c n) -> p c n", n=128)]
                    r_aTs = [r_aT, r_aT2]

                    r_ebc = [[Res("eb%d_%d" % (k, c)) for c in range(4)] for k in range(2)]

                    def front(n):
                        eb, cb_, r_cb = ebufs[n % 2], cumbufs[n % 2], r_cums[n % 2]
                        Nk = 128 * (n + 1)
                        nch = (Nk + 511) // 512
                        S.op("dve", lambda g: g.memset(cb_[:, Nk:Nk + 1], 1.0), writes=[r_cb])
                        for ch in range(nch - 1, -1, -1):
                            k0 = ch * 512
                            kw = min(512, Nk - k0)
                            r_e = r_ebc[n % 2][ch]
                            ps, r_ps = ps_next()
                            S.op("pe", lambda g, ps=ps, k0=k0, kw=kw: g.matmul(ps[:, 0:kw], lhsT=QT[:, n * 128:(n + 1) * 128], rhs=KT[:, k0:k0 + kw],
                                                                               start=True, stop=True),
                                 reads=[r_QT, r_KT], writes=[r_ps])
                            S.op("act", lambda g, ps=ps, k0=k0, kw=kw: g.activation(out=eb[:, k0:k0 + kw], in_=ps[:, 0:kw], func=AF.Sigmoid, scale=-SCALE),
                                 reads=[r_ps], writes=[r_e])
                            if ch == nch - 1:
                                S.op("dve", lambda g: g.tensor_tensor(out=eb[:, n * 128:(n + 1) * 128], in0=eb[:, n * 128:(n + 1) * 128],
                                                                      in1=triLEf[:], op=ALU.max),
                                     reads=[r_e, r_c], writes=[r_e])
                            init = 1.0 if ch == nch - 1 else cb_[:, k0 + kw:k0 + kw + 1]
                            S.op("dve", lambda g, k0=k0, kw=kw, init=init: g.tensor_tensor_scan(out=cb_[:, k0:k0 + kw][:, ::-1], data0=eb[:, k0:k0 + kw][:, ::-1],
                                                                                                 data1=eb[:, k0:k0 + kw][:, ::-1], initial=init,
                                                                                                 op0=ALU.mult, op1=ALU.min),
                                 reads=[r_e, r_cb], writes=[r_cb])

                    def midB(n):
                        cb_, r_cb, ab, r_ab = cumbufs[n % 2], r_cums[n % 2], abufs3[n % 3], r_abufs3[n % 3]
                        Nk = 128 * (n + 1)
                        S.op("dve", lambda g: g.tensor_tensor(out=ab[:, 0:Nk], in0=cb_[:, 1:Nk + 1], in1=cb_[:, 0:Nk], op=ALU.subtract),
                             reads=[r_cb], writes=[r_ab])

                    def tailA(n):
                        ab, r_ab = abufs3[n % 3], r_abufs3[n % 3]
                        for m in range(n + 1):
                            S.op("pe", lambda g, m=m: g.transpose(out=paT[:, m, :], in_=ab[:, m * 128:(m + 1) * 128], identity=identb[:]),
                                 reads=[r_ab, r_c], writes=[r_pO[0], r_pO[1]])
                        evac(aTbs[n % 2][:, 0:n + 1, :], paT[:, 0:n + 1, :], [r_pO[0], r_pO[1]], [r_aTs[n % 2]])

                    def tailB(n):
                        ko = 2 + n % 2
                        aT_ = aTbs[n % 2]
                        for m in range(n + 1):
                            S.op("pe", lambda g, m=m: g.matmul(pO[:, ko, 0:128], lhsT=aT_[:, m, :], rhs=Vt[:, m, 0:128], start=(m == 0), stop=(m == n)),
                                 reads=[r_aTs[n % 2], r_Vt], writes=[r_pO[ko]])

                    def epiA(n):
                        ko = 2 + n % 2
                        S.op("act", lambda g: g.activation(out=stage[:, n % 4, 0:128], in_=pO[:, ko, 0:128], func=AF.Copy),
                             reads=[r_pO[ko]], writes=[r_stage])
                        if n % 4 == 3:
                            for j in range(4):
                                S.op("act", lambda g, j=j: g.activation(out=junk[:, 0:128], in_=stage[:, j, 0:128], func=AF.Square,
                                                                       accum_out=stat[:, 12 + j:13 + j]),
                                     reads=[r_stage], writes=[r_junk, r_stat])
                            S.op("act", lambda g: g.activation(out=stat[:, 16:20], in_=stat[:, 12:16], func=AF.Ln, scale=1.0 / HD, bias=epsb[:]),
                                 reads=[r_stat], writes=[r_stat])
                            S.op("act", lambda g: g.activation(out=stat[:, 20:24], in_=stat[:, 16:20], func=AF.Exp, scale=-0.5), reads=[r_stat], writes=[r_stat])
                            S.op("dve", lambda g: g.tensor_tensor(out=obuf, in0=stage[:, :, 0:128],
                                                                  in1=stat[:, 20:24].unsqueeze(2).broadcast_to([128, 4, 128]), op=ALU.mult),
                                 reads=[r_stage, r_stat], writes=[r_ob])

                    def epiB(n):
                        if n % 4 == 3:
                            epilogue_ac_2(l, h, n // 4, oTh, r_oT)

                    for i in range(NT + 6):
                        pump()
                        pump()
                        if i < NT:
                            front(i)
                        if 0 <= i - 1 < NT:
                            midB(i - 1)
                        if 0 <= i - 3 < NT:
                            tailA(i - 3)
                        pump()
                        if 0 <= i - 4 < NT:
                            tailB(i - 4)
                        if 0 <= i - 5 < NT:
                            epiA(i - 5)
                        if 0 <= i - 6 < NT:
                            epiB(i - 6)
                drain_gen()
                S.dma("sp", lambda g, h=h, oTh=oTh: g.dma_start(out=oT_d[h], in_=oTh), d_oT[h % 2], reads=[r_oT], writes=[r_oTd[h]])
            S.barrier()

            wgu_slots = [vw(WGU, 8192 * i, 8192, BF16, "p (a c n) -> p a c n", a=2, n=128) for i in range(3)]
            wgu_ring = Ring(S, wgu_slots, with_dsem=True)
            wgu_v = w_gu_d[l].rearrange("(c p) n -> p c n", p=128)
            wgu_ld = {}

            def issue_wgu(tt, j):
                wgu, r_wgu, d_wgu = wgu_ring.next()
                S.dma("pool", lambda g: g.dma_start(out=wgu[:, 0, :, :], in_=wgu_v[:, :, j * 128:(j + 1) * 128]), d_wgu, writes=[r_wgu])
                S.dma("pool", lambda g: g.dma_start(out=wgu[:, 1, :, :], in_=wgu_v[:, :, DFF + j * 128:DFF + (j + 1) * 128]), d_wgu, writes=[r_wgu])
                wgu_ld[(tt, j)] = (wgu, r_wgu)

            oT = hnT
            r_oTs = Res("oT")
            d_oTl = S.dsem()
            for h in range(NH):
                S.dma("sp", lambda g, h=h: g.dma_start(out=oT[:, h, :], in_=oT_d[h]), d_oTl, reads=[r_oTd[h]], writes=[r_oTs])
            if l == 0:
                dbg_dump("oT", oT, [r_oTs])
            wo_slots = [vw(BIGB, 16384 * i, 16384, BF16, "p (c n) -> p c n", n=512) for i in range(2)]
            wo_ring = Ring(S, wo_slots, with_dsem=True)
            xp_ring = Ring(S, [vw(BIGB, 32768 + 2048 * i, 2048, F32) for i in range(4)], with_dsem=True)
            d_st = [S.dsem() for _ in range(4)]
            acc_i = 0
            wo_v = w_o_d[l].rearrange("(c p) n -> p c n", p=128)
            for cc in range(4):
                wo, r_wo, d_wo = wo_ring.next()
                S.dma("pool", lambda g, wo=wo, cc=cc: g.dma_start(out=wo, in_=wo_v[:, :, cc * 512:(cc + 1) * 512]), d_wo, writes=[r_wo])
                if cc == 1:
                    for jp in range(3):
                        issue_wgu(0, jp)
                for tb in range(NT):
                    k = acc_i % 6
                    acc_i += 1
                    if k < 2:
                        ps, r_ps = pS[:, k, :], r_pS[k]
                    else:
                        ps, r_ps = pO[:, k - 2, :], r_pO[k - 2]
                    for h in range(NH):
                        S.op("pe", lambda g, ps=ps, wo=wo, h=h, tb=tb: g.matmul(ps, lhsT=oT[:, h, tb * 128:(tb + 1) * 128], rhs=wo[:, h, :],
                                                                               start=(h == 0), stop=(h == NH - 1)),
                             reads=[r_oTs, r_wo], writes=[r_ps])
                    xp, r_xp, d_xp = xp_ring.next()
                    S.dma("sp", lambda g, xp=xp, tb=tb, cc=cc: g.dma_start(out=xp, in_=src_d[tb * 128:(tb + 1) * 128, cc * 512:(cc + 1) * 512]),
                          d_xp, reads=[r_x[tb]], writes=[r_xp])
                    S.op("dve", lambda g, xp=xp, ps=ps: g.tensor_tensor(out=xp, in0=ps, in1=xp, op=ALU.add), reads=[r_ps, r_xp], writes=[r_xp])
                    S.dma("sp", lambda g, xp=xp, tb=tb, cc=cc: g.dma_start(out=xres_d[tb * 128:(tb + 1) * 128, cc * 512:(cc + 1) * 512], in_=xp),
                          d_st[acc_i % 4], reads=[r_xp], writes=[r_x[tb]])
            S.barrier()
            if l == 0 and "x1" in dbg_out:
                dd = S.dsem()
                for tb in range(NT):
                    xt, r_xt, d_xt = xt_slots[tb % 2]
                    S.dma("sp", lambda g, xt=xt, tb=tb: g.dma_start(out=xt, in_=xres_d[tb * 128:(tb + 1) * 128, :]), d_xt, reads=[r_x[tb]], writes=[r_xt])
                    S.dma("sp", lambda g, xt=xt, tb=tb: g.dma_start(out=dbg_out["x1"][tb * 128:(tb + 1) * 128, :], in_=xt), dd, reads=[r_xt])
                S.barrier()

            hn2T = vw(BIGA, 0, 32768, BF16, "p (c n) -> p c n", n=1024)
            r_hn2 = Res("hn2T")
            wd_slots = [vw(BIGA, 32768 + 11264 * i, 11264, BF16, "p (c n) -> p c n", n=128) for i in range(2)]
            wd_ring = Ring(S, wd_slots, with_dsem=True)
            ots_ring = Ring(S, [vw(BIGA, 32768 + 22528 + 2048 * i, 2048, F32) for i in range(2)])
            actT = vw(BIGB, 0, 90112, BF16, "p (c n) -> p c n", n=1024)
            r_act = Res("actT")
            cv_ring = Ring(S, [vw(XT, 6144 * i, 6144, F32, "p (a n) -> p a n", a=3) for i in range(2)])
            xq_ring = Ring(S, [vw(XT, 2048 * i, 2048, F32, "p (a n) -> p a n", a=4) for i in range(6)], with_dsem=True)
            d_xst = [S.dsem() for _ in range(6)]
            r_carry = Res("carry")
            cw3 = cw[:, l, :].rearrange("p (j k) -> p j k", k=3)
            wd_v = w_down_d[l].rearrange("(j p) n -> p j n", p=128)
            wd_ld = {}

            def issue_wd(tt, dc):
                wd, r_wd, d_wd = wd_ring.next()
                S.dma("pool", lambda g: g.dma_start(out=wd, in_=wd_v[:, :, dc * 128:(dc + 1) * 128]), d_wd, writes=[r_wd])
                wd_ld[(tt, dc)] = (wd, r_wd)
            pgu_i = 0
            S.op("dve", lambda g: g.memset(carry[:], 0.0), writes=[r_carry])
            for tt in range(2):
                for i in range(8):
                    tb = tt * 8 + i
                    norm_transpose(xres_d[tb * 128:(tb + 1) * 128, :], r_x[tb], i, gffn[:, l, :], hn2T, r_hn2, i * 128)
                S.barrier()
                for j in range(NF):
                    if (tt, j) not in wgu_ld:
                        issue_wgu(tt, j)
                    wgu, r_wgu = wgu_ld.pop((tt, j))
                    if j == 6:
                        issue_wd(tt, 0)
                        issue_wd(tt, 1)
                    for half in range(2):
                        kk = (pgu_i % 2) * 2
                        pgu_i += 1
                        pv2 = [pO[:, kk, :], pO[:, kk + 1, :]]
                        rv2 = [r_pO[kk], r_pO[kk + 1]]
                        for a in range(2):
                            for c in range(16):
                                S.op("pe", lambda g, a=a, c=c, wgu=wgu, pv2=pv2, half=half: g.matmul(pv2[a], lhsT=wgu[:, a, c, :],
                                                                                                    rhs=hn2T[:, c, half * 512:(half + 1) * 512],
                                                                                                    start=(c == 0), stop=(c == 15)),
                                     reads=[r_wgu, r_hn2], writes=[rv2[a]])
                        cvb, r_cv, _ = cv_ring.next()
                        first = (tt == 0 and half == 0)
                        for a in range(2):
                            jj = a * NF + j
                            ph = pv2[a]
                            acc = cvb[:, a, :]
                            S.op("act", lambda g, acc=acc, ph=ph, jj=jj: g.activation(out=acc, in_=ph, func=AF.Identity, scale=cw3[:, jj, 2:3],
                                                                                    bias=cb[:, l, jj:jj + 1]),
                                 reads=[rv2[a], r_par], writes=[r_cv])
                            S.op("dve", lambda g, acc=acc, ph=ph, jj=jj: g.scalar_tensor_tensor(out=acc[:, 1:512], in0=ph[:, 0:511], scalar=cw3[:, jj, 1:2],
                                                                                               in1=acc[:, 1:512], op0=ALU.mult, op1=ALU.add),
                                 reads=[rv2[a], r_par, r_cv], writes=[r_cv])
                            S.op("dve", lambda g, acc=acc, ph=ph, jj=jj: g.scalar_tensor_tensor(out=acc[:, 2:512], in0=ph[:, 0:510], scalar=cw3[:, jj, 0:1],
                                                                                               in1=acc[:, 2:512], op0=ALU.mult, op1=ALU.add),
                                 reads=[rv2[a], r_par, r_cv], writes=[r_cv])
                            if not first:
                                cr = carry[:, 2 * jj:2 * jj + 2]
                                S.op("dve", lambda g, acc=acc, cr=cr, jj=jj: g.scalar_tensor_tensor(out=acc[:, 0:2], in0=cr, scalar=cw3[:, jj, 0:1],
                                                                                                   in1=acc[:, 0:2], op0=ALU.mult, op1=ALU.add),
                                     reads=[r_carry, r_par, r_cv], writes=[r_cv])
                                S.op("dve", lambda g, acc=acc, cr=cr, jj=jj: g.scalar_tensor_tensor(out=acc[:, 0:1], in0=cr[:, 1:2], scalar=cw3[:, jj, 1:2],
                                                                                                   in1=acc[:, 0:1], op0=ALU.mult, op1=ALU.add),
                                     reads=[r_carry, r_par, r_cv], writes=[r_cv])
                            S.op("dve", lambda g, ph=ph, jj=jj: g.tensor_copy(out=carry[:, 2 * jj:2 * jj + 2], in_=ph[:, 510:512]),
                                 reads=[rv2[a], r_cv], writes=[r_carry])
                        S.op("act", lambda g, cvb=cvb: g.activation(out=cvb[:, 2, :], in_=cvb[:, 0, :], func=AF.Silu), reads=[r_cv], writes=[r_cv])
                        S.op("dve", lambda g, cvb=cvb, j=j, half=half: g.tensor_tensor(out=actT[:, j, half * 512:(half + 1) * 512], in0=cvb[:, 2, :],
                                                                                      in1=cvb[:, 1, :], op=ALU.mult),
                             reads=[r_cv], writes=[r_act])
                if l == 0 and tt == 0:
                    dbg_dump("actT", actT, [r_act])
                S.barrier()
                po_i = 0
                ptk = 0
                for dc in range(16):
                    if (tt, dc) not in wd_ld:
                        issue_wd(tt, dc)
                    wd, r_wd = wd_ld.pop((tt, dc))
                    if tt == 0 and dc == 8:
                        for jp in range(3):
                            issue_wgu(1, jp)
                    for half in range(2):
                        k = po_i % 2
                        po_i += 1
                        po, r_po = pS[:, k, :], r_pS[k]
                        for j in range(NF):
                            S.op("pe", lambda g, po=po, wd=wd, j=j, half=half: g.matmul(po, lhsT=wd[:, j, :], rhs=actT[:, j, half * 512:(half + 1) * 512],
                                                                                       start=(j == 0), stop=(j == NF - 1)),
                                 reads=[r_wd, r_act], writes=[r_po])
                        ots, r_ots, _ = ots_ring.next()
                        S.op("act", lambda g, ots=ots, po=po: g.activation(out=ots, in_=po, func=AF.Copy), reads=[r_po], writes=[r_ots])
                        ptv, r_ptv = ((pX, r_pX), (pT, r_pT))[ptk % 2]
                        ptk += 1
                        for b in range(4):
                            S.op("pe", lambda g, ptv=ptv, ots=ots, b=b: g.transpose(out=ptv[:, b * 128:(b + 1) * 128], in_=ots[:, b * 128:(b + 1) * 128],
                                                                                     identity=identf[:]),
                                 reads=[r_ots, r_c], writes=[r_ptv])
                        xq, r_xq, d_xq = xq_ring.next()
                        t0 = tt * 1024 + half * 512
                        rows = [r_x[(t0 // 128) + b] for b in range(4)]
                        S.dma("sp", lambda g, xq=xq, t0=t0, dc=dc: g.dma_start(
                            out=xq, in_=xres_d[t0:t0 + 512, dc * 128:(dc + 1) * 128].rearrange("(b p) n -> p b n", p=128)),
                            d_xq, reads=rows, writes=[r_xq])
                        S.op("dve", lambda g, xq=xq, ptv=ptv: g.tensor_tensor(out=xq, in0=ptv.rearrange("p (b n) -> p b n", n=128), in1=xq, op=ALU.add),
                             reads=[r_ptv, r_xq], writes=[r_xq])
                        S.dma("sp", lambda g, xq=xq, t0=t0, dc=dc: g.dma_start(
                            out=xres_d[t0:t0 + 512, dc * 128:(dc + 1) * 128].rearrange("(b p) n -> p b n", p=128), in_=xq),
                            d_xst[(po_i - 1) % 6], reads=[r_xq], writes=rows)
                S.barrier()
            if l == 0 and "x2" in dbg_out:
                dd = S.dsem()
                for tb in range(NT):
                    xt, r_xt, d_xt = xt_slots[tb % 2]
                    S.dma("sp", lambda g, xt=xt, tb=tb: g.dma_start(out=xt, in_=xres_d[tb * 128:(tb + 1) * 128, :]), d_xt, reads=[r_x[tb]], writes=[r_xt])
                    S.dma("sp", lambda g, xt=xt, tb=tb: g.dma_start(out=dbg_out["x2"][tb * 128:(tb + 1) * 128, :], in_=xt), dd, reads=[r_xt])
                S.barrier()

        gfb = vw(BIGA, 0, 8192, F32)
        r_gfb = Res("gfb")
        d_g = S.dsem()
        S.dma("sp", lambda g: g.dma_start(out=gfb, in_=gfin_d), d_g, writes=[r_gfb])
        yo_ring = Ring(S, [vw(BIGB, 8192 * i, 8192, F32) for i in range(2)], with_dsem=True)
        src_fin = xres_d if n_layers > 0 else x_d
        for i in range(NT):
            xt, r_xt, d_xt = xt_slots[i % 2]
            S.dma("sp", lambda g, xt=xt, i=i: g.dma_start(out=xt, in_=src_fin[i * 128:(i + 1) * 128, :]), d_xt, reads=[r_x[i]], writes=[r_xt])
            S.op("act", lambda g, xt=xt: g.activation(out=xs_b, in_=xt, func=AF.Square, accum_out=stat[:, 0:1]), reads=[r_xt], writes=[r_xs, r_stat])
            S.op("act", lambda g: g.activation(out=stat[:, 1:2], in_=stat[:, 0:1], func=AF.Ln, scale=1.0 / D, bias=epsb[:]), reads=[r_stat], writes=[r_stat])
            S.op("act", lambda g: g.activation(out=stat[:, 2:3], in_=stat[:, 1:2], func=AF.Exp, scale=-0.5), reads=[r_stat], writes=[r_stat])
            yo, r_yo, d_yo = yo_ring.next()
            S.op("dve", lambda g, xt=xt, yo=yo: g.scalar_tensor_tensor(out=yo, in0=xt, scalar=stat[:, 2:3], in1=gfb, op0=ALU.mult, op1=ALU.mult),
                 reads=[r_xt, r_stat, r_gfb], writes=[r_yo])
            S.dma("sp", lambda g, yo=yo, i=i: g.dma_start(out=y_d[i * 128:(i + 1) * 128, :], in_=yo), d_yo, reads=[r_yo])
        S.emit(nc)
    return nc


_NC_CACHE = {}


def _host_params(g_mix, b_f, g_head, g_ffn, conv_w, conv_b, g_final):
    f = np.float32

    def colT(v):
        return np.ascontiguousarray(np.asarray(v, f).reshape(DEPTH, 16, 128).transpose(0, 2, 1))

    gheadT = np.ascontiguousarray(np.asarray(g_head, f).transpose(0, 2, 1))
    bfb = np.ascontiguousarray(np.broadcast_to(np.asarray(b_f, f)[:, None, :], (DEPTH, 128, HC)))
    cwT = np.ascontiguousarray(np.asarray(conv_w, f).reshape(DEPTH, 3, 88, 128).transpose(0, 3, 2, 1)).reshape(DEPTH, 128, 88 * 3)
    cbT = np.ascontiguousarray(np.asarray(conv_b, f).reshape(DEPTH, 88, 128).transpose(0, 2, 1))
    gfinb = np.ascontiguousarray(np.broadcast_to(np.asarray(g_final, f)[None, :], (128, D)))
    return dict(gmixT=colT(g_mix), gffnT=colT(g_ffn), gheadT=gheadT, bfb=bfb, cwT=cwT, cbT=cbT, gfinb=gfinb)


def kernel(x, g_mix, w_in, b_f, g_head, w_o, g_ffn, w_gu, conv_w, conv_b, w_down, g_final):
    if "nc" not in _NC_CACHE:
        _NC_CACHE["nc"] = build_nc()
    nc = _NC_CACHE["nc"]
    par = _host_params(g_mix, b_f, g_head, g_ffn, conv_w, conv_b, g_final)
    shared = dict(w_in=np.ascontiguousarray(w_in, np.float32), w_o=np.ascontiguousarray(w_o, np.float32),
                  w_gu=np.ascontiguousarray(w_gu, np.float32), w_down=np.ascontiguousarray(w_down, np.float32), **par)
    x = np.asarray(x, np.float32)
    in_maps = [dict(x=np.ascontiguousarray(x[b]), **shared) for b in range(8)]
    res = run_bass_kernel_spmd(nc, in_maps, core_ids=list(range(8)))
    return np.stack([res.results[b]["y"] for b in range(8)], axis=0)
```
